# Optimizing a Trainium2 kernel written in Bass

```python
import jax, jax.numpy as jnp
from jax import lax
import numpy as np

D_MODEL = 2048
BATCH = 4
SEQ = 8192
DEPTH = 2
DEC_BATCH = 1
DEC_SEQ = 16384
PAST_LEN = 128

N_MIXERS = 2
N_A_LAYERS = (DEPTH + N_MIXERS - 1) // N_MIXERS
N_B_LAYERS = DEPTH // N_MIXERS
RMS_EPS = 1e-6
L2_EPS = 1e-6

GDN_QK_HEADS = 16
GDN_V_HEADS = 32
GDN_DK = 128
GDN_DV = 128
GDN_CONV = 5
GDN_CHUNK = 64
GDN_Q_DIM = GDN_QK_HEADS * GDN_DK
GDN_V_DIM = GDN_V_HEADS * GDN_DV
GDN_CONV_DIM = 2 * GDN_Q_DIM + GDN_V_DIM
GDN_PROJ_DIM = GDN_CONV_DIM + GDN_V_DIM + 4 * GDN_V_HEADS

MLA_HEADS = 16
MLA_Q_RANK = 768
MLA_KV_RANK = 512
MLA_D_NOPE = 128
MLA_D_ROPE = 64
MLA_D_V = 128
MLA_D_QK = MLA_D_NOPE + MLA_D_ROPE
MLA_A_DIM = MLA_Q_RANK + MLA_KV_RANK + MLA_D_ROPE
ROPE_THETA = 10000.0
Q_BLOCK = 128

D_FF = 4 * D_MODEL

kernel_name = 'hybrid_gdn_mla_bidir_encoder'


def rms_norm(x, w):
    xf = x.astype(jnp.float32)
    y = xf * lax.rsqrt(jnp.mean(xf * xf, axis=-1, keepdims=True) + RMS_EPS)
    return (y * w.astype(jnp.float32)).astype(x.dtype)


def l2_normalize(x):
    xf = x.astype(jnp.float32)
    return xf * lax.rsqrt(jnp.sum(xf * xf, axis=-1, keepdims=True) + L2_EPS)


def centred_depthwise_conv(x, w):
    width = w.shape[0]
    r = width // 2
    L = x.shape[1]
    xp = jnp.pad(x, ((0, 0), (r, r), (0, 0)))
    out = xp[:, 0:L] * w[0]
    for j in range(1, width):
        out = out + xp[:, j:j + L] * w[j]
    return out


def chunk_gated_delta_rule(q, k, v, g, beta):
    B, H, L, DK = q.shape
    DV = v.shape[-1]
    C = GDN_CHUNK
    N = L // C
    q = q.reshape(B, H, N, C, DK)
    k = k.reshape(B, H, N, C, DK)
    v = v.reshape(B, H, N, C, DV)
    g = jnp.cumsum(g.reshape(B, H, N, C), axis=-1)
    beta = beta.reshape(B, H, N, C)
    tril = jnp.tril(jnp.ones((C, C), dtype=bool))
    strict = jnp.tril(jnp.ones((C, C), dtype=bool), -1)
    diff = g[..., :, None] - g[..., None, :]
    decay = jnp.where(tril, jnp.exp(jnp.where(tril, diff, 0.0)), 0.0)
    k_beta = k * beta[..., None]
    v_beta = v * beta[..., None]
    a = jnp.where(strict, jnp.einsum('bhncd,bhnsd->bhncs', k_beta, k) * decay, 0.0)
    rhs = jnp.concatenate([v_beta, k_beta * jnp.exp(g)[..., None]], axis=-1)
    sol = lax.linalg.triangular_solve(a, rhs, left_side=True, lower=True, unit_diagonal=True)
    u = sol[..., :DV]
    w = sol[..., DV:]
    qk = jnp.where(tril, jnp.einsum('bhncd,bhnsd->bhncs', q, k) * decay, 0.0)
    g_last = g[..., -1]
    q_dec = q * jnp.exp(g)[..., None]
    k_dec = k * jnp.exp(g_last[..., None] - g)[..., None]
    xs = (jnp.moveaxis(qk, 2, 0), jnp.moveaxis(q_dec, 2, 0), jnp.moveaxis(k_dec, 2, 0),
          jnp.moveaxis(u, 2, 0), jnp.moveaxis(w, 2, 0), jnp.moveaxis(g_last, 2, 0))

    def step(S, chunk):
        qk_c, q_c, k_c, u_c, w_c, gl = chunk
        v_new = u_c - jnp.einsum('bhcd,bhde->bhce', w_c, S)
        o = jnp.einsum('bhcd,bhde->bhce', q_c, S) + jnp.einsum('bhcs,bhse->bhce', qk_c, v_new)
        S = S * jnp.exp(gl)[..., None, None] + jnp.einsum('bhcd,bhce->bhde', k_c, v_new)
        return S, o

    S0 = jnp.zeros((B, H, DK, DV), jnp.float32)
    _, o = lax.scan(step, S0, xs)
    return jnp.moveaxis(o, 0, 2).reshape(B, H, L, DV)


def gated_deltanet_mixer(h, w_in, conv_w, a_log, dt_bias, norm_w, w_out):
    B, L, _ = h.shape
    f32 = jnp.float32
    proj = h @ w_in
    o_z = GDN_CONV_DIM
    o_b = o_z + GDN_V_DIM
    o_a = o_b + 2 * GDN_V_HEADS
    qkv = jax.nn.silu(centred_depthwise_conv(proj[..., :GDN_CONV_DIM], conv_w))
    z = proj[..., o_z:o_b].reshape(B, L, GDN_V_HEADS, GDN_DV)
    b_logit = proj[..., o_b:o_a].reshape(B, L, 2, GDN_V_HEADS)
    a_in = proj[..., o_a:].reshape(B, L, 2, GDN_V_HEADS)
    rep = GDN_V_HEADS // GDN_QK_HEADS
    q = qkv[..., :GDN_Q_DIM].reshape(B, L, GDN_QK_HEADS, GDN_DK)
    k = qkv[..., GDN_Q_DIM:2 * GDN_Q_DIM].reshape(B, L, GDN_QK_HEADS, GDN_DK)
    v = qkv[..., 2 * GDN_Q_DIM:].reshape(B, L, GDN_V_HEADS, GDN_DV).astype(f32)
    q = jnp.repeat(l2_normalize(q), rep, axis=2) * (GDN_DK ** -0.5)
    k = jnp.repeat(l2_normalize(k), rep, axis=2)
    beta = jax.nn.sigmoid(b_logit.astype(f32))
    g = -jnp.exp(a_log.astype(f32)) * jax.nn.softplus(a_in.astype(f32) + dt_bias.astype(f32))
    q = q.transpose(0, 2, 1, 3)
    k = k.transpose(0, 2, 1, 3)
    v = v.transpose(0, 2, 1, 3)
    g_f, g_b = g[:, :, 0].transpose(0, 2, 1), g[:, :, 1].transpose(0, 2, 1)
    beta_f, beta_b = beta[:, :, 0].transpose(0, 2, 1), beta[:, :, 1].transpose(0, 2, 1)
    flip = lambda t: jnp.flip(t, axis=2)
    o_fwd = chunk_gated_delta_rule(q, k, v, g_f, beta_f)
    o_bwd = flip(chunk_gated_delta_rule(flip(q), flip(k), flip(v), flip(g_b), flip(beta_b)))
    o = (o_fwd + o_bwd).transpose(0, 2, 1, 3)
    o = rms_norm(o, norm_w) * jax.nn.silu(z.astype(f32))
    return o.reshape(B, L, GDN_V_DIM).astype(h.dtype) @ w_out


def rope_tables(L):
    half = MLA_D_ROPE // 2
    inv_freq = ROPE_THETA ** (-jnp.arange(half, dtype=jnp.float32) / half)
    ang = jnp.arange(L, dtype=jnp.float32)[:, None] * inv_freq[None, :]
    return jnp.cos(ang), jnp.sin(ang)


def apply_rope(x, cos, sin):
    half = MLA_D_ROPE // 2
    xf = x.astype(jnp.float32)
    x1, x2 = xf[..., :half], xf[..., half:]
    c, s = cos[:, None, :], sin[:, None, :]
    return jnp.concatenate([x1 * c - x2 * s, x2 * c + x1 * s], axis=-1).astype(x.dtype)


def block_attention(q, k, v):
    B, L, H, D = q.shape
    nb = L // Q_BLOCK
    scale = D ** -0.5
    qb = jnp.moveaxis(q.reshape(B, nb, Q_BLOCK, H, D), 1, 0)

    def one_block(q_blk):
        s = jnp.einsum('bqhd,bkhd->bhqk', q_blk, k).astype(jnp.float32) * scale
        p = jax.nn.softmax(s, axis=-1)
        return jnp.einsum('bhqk,bkhe->bqhe', p.astype(v.dtype), v)

    o = lax.map(one_block, qb)
    return jnp.moveaxis(o, 0, 1).reshape(B, L, H, v.shape[-1])


def mla_mixer(h, w_a, q_a_norm, w_q_b, kv_a_norm, w_kv_b, w_o):
    B, L, _ = h.shape
    a = h @ w_a
    c_q = rms_norm(a[..., :MLA_Q_RANK], q_a_norm)
    c_kv = rms_norm(a[..., MLA_Q_RANK:MLA_Q_RANK + MLA_KV_RANK], kv_a_norm)
    k_rope = a[..., MLA_Q_RANK + MLA_KV_RANK:][:, :, None, :]
    q = (c_q @ w_q_b).reshape(B, L, MLA_HEADS, MLA_D_QK)
    kv = (c_kv @ w_kv_b).reshape(B, L, MLA_HEADS, MLA_D_NOPE + MLA_D_V)
    cos, sin = rope_tables(L)
    q = jnp.concatenate([q[..., :MLA_D_NOPE], apply_rope(q[..., MLA_D_NOPE:], cos, sin)], axis=-1)
    k_rope = jnp.broadcast_to(apply_rope(k_rope, cos, sin), (B, L, MLA_HEADS, MLA_D_ROPE))
    k = jnp.concatenate([kv[..., :MLA_D_NOPE], k_rope], axis=-1)
    v = kv[..., MLA_D_NOPE:]
    o = block_attention(q, k, v)
    return o.reshape(B, L, MLA_HEADS * MLA_D_V) @ w_o


def squared_relu_mlp(h, w_in, w_out):
    return jnp.square(jax.nn.relu(h @ w_in)) @ w_out


def trunk(x, norm_mix_pre, norm_mix_post, norm_ffn_pre, norm_ffn_post,
          gdn_w_in, gdn_conv_w, gdn_a_log, gdn_dt_bias, gdn_norm_w, gdn_w_out,
          mla_w_a, mla_q_a_norm, mla_w_q_b, mla_kv_a_norm, mla_w_kv_b, mla_w_o,
          ffn_w_in, ffn_w_out):
    for i in range(DEPTH):
        j = i // N_MIXERS
        h = rms_norm(x, norm_mix_pre[i])
        if i % N_MIXERS == 0:
            m = gated_deltanet_mixer(h, gdn_w_in[j], gdn_conv_w[j], gdn_a_log[j],
                                     gdn_dt_bias[j], gdn_norm_w[j], gdn_w_out[j])
        else:
            m = mla_mixer(h, mla_w_a[j], mla_q_a_norm[j], mla_w_q_b[j],
                          mla_kv_a_norm[j], mla_w_kv_b[j], mla_w_o[j])
        x = x + rms_norm(m, norm_mix_post[i])
        h = rms_norm(x, norm_ffn_pre[i])
        x = x + rms_norm(squared_relu_mlp(h, ffn_w_in[i], ffn_w_out[i]), norm_ffn_post[i])
    return x


def setup_inputs(seed: int = 0) -> dict:
    key = jax.random.key(seed)
    ks = jax.random.split(key, 24)
    f32 = jnp.float32

    def dense(k, shape, fan_in):
        return jax.random.normal(k, shape, f32) * (fan_in ** -0.5)

    def gain(k, shape):
        return 1.0 + 0.05 * jax.random.normal(k, shape, f32)

    dt = jnp.exp(jax.random.uniform(ks[8], (N_A_LAYERS, 2, GDN_V_HEADS), f32,
                                    np.log(1e-3), np.log(1e-1)))
    return {
        'x_prompt': jax.random.normal(ks[0], (BATCH, SEQ, D_MODEL), f32),
        'x_sample': jax.random.normal(ks[1], (DEC_BATCH, DEC_SEQ, D_MODEL), f32),
        'norm_mix_pre': gain(ks[2], (DEPTH, D_MODEL)),
        'norm_mix_post': gain(ks[3], (DEPTH, D_MODEL)),
        'norm_ffn_pre': gain(ks[4], (DEPTH, D_MODEL)),
        'norm_ffn_post': gain(ks[5], (DEPTH, D_MODEL)),
        'gdn_w_in': dense(ks[6], (N_A_LAYERS, D_MODEL, GDN_PROJ_DIM), D_MODEL),
        'gdn_conv_w': dense(ks[7], (N_A_LAYERS, GDN_CONV, GDN_CONV_DIM), GDN_CONV),
        'gdn_a_log': jnp.log(jax.random.uniform(ks[9], (N_A_LAYERS, 2, GDN_V_HEADS), f32, 1.0, 16.0)),
        'gdn_dt_bias': dt + jnp.log(-jnp.expm1(-dt)),
        'gdn_norm_w': gain(ks[10], (N_A_LAYERS, GDN_DV)),
        'gdn_w_out': dense(ks[11], (N_A_LAYERS, GDN_V_DIM, D_MODEL), GDN_V_DIM),
        'mla_w_a': dense(ks[12], (N_B_LAYERS, D_MODEL, MLA_A_DIM), D_MODEL),
        'mla_q_a_norm': gain(ks[13], (N_B_LAYERS, MLA_Q_RANK)),
        'mla_w_q_b': dense(ks[14], (N_B_LAYERS, MLA_Q_RANK, MLA_HEADS * MLA_D_QK), MLA_Q_RANK),
        'mla_kv_a_norm': gain(ks[15], (N_B_LAYERS, MLA_KV_RANK)),
        'mla_w_kv_b': dense(ks[16], (N_B_LAYERS, MLA_KV_RANK, MLA_HEADS * (MLA_D_NOPE + MLA_D_V)), MLA_KV_RANK),
        'mla_w_o': dense(ks[17], (N_B_LAYERS, MLA_HEADS * MLA_D_V, D_MODEL), MLA_HEADS * MLA_D_V),
        'ffn_w_in': dense(ks[18], (DEPTH, D_MODEL, D_FF), D_MODEL),
        'ffn_w_out': dense(ks[19], (DEPTH, D_FF, D_MODEL), D_FF),
    }


def reference(x_prompt, x_sample, norm_mix_pre, norm_mix_post, norm_ffn_pre, norm_ffn_post,
              gdn_w_in, gdn_conv_w, gdn_a_log, gdn_dt_bias, gdn_norm_w, gdn_w_out,
              mla_w_a, mla_q_a_norm, mla_w_q_b, mla_kv_a_norm, mla_w_kv_b, mla_w_o,
              ffn_w_in, ffn_w_out):
    y_prompt = trunk(x_prompt, norm_mix_pre, norm_mix_post, norm_ffn_pre, norm_ffn_post,
                     gdn_w_in, gdn_conv_w, gdn_a_log, gdn_dt_bias, gdn_norm_w, gdn_w_out,
                     mla_w_a, mla_q_a_norm, mla_w_q_b, mla_kv_a_norm, mla_w_kv_b, mla_w_o,
                     ffn_w_in, ffn_w_out)
    y_sample = trunk(x_sample, norm_mix_pre, norm_mix_post, norm_ffn_pre, norm_ffn_post,
                     gdn_w_in, gdn_conv_w, gdn_a_log, gdn_dt_bias, gdn_norm_w, gdn_w_out,
                     mla_w_a, mla_q_a_norm, mla_w_q_b, mla_kv_a_norm, mla_w_kv_b, mla_w_o,
                     ffn_w_in, ffn_w_out)
    return (y_prompt, y_sample)
```

```python
import numpy as np
import ml_dtypes
from contextlib import ExitStack
import concourse.bass as bass
import concourse.mybir as mybir
from concourse.bass_utils import run_bass_kernel_spmd

F32 = mybir.dt.float32
BF16 = mybir.dt.bfloat16
AF = mybir.ActivationFunctionType
ALU = mybir.AluOpType
NPBF = ml_dtypes.bfloat16

PE, ACT, DVE, POOL, SP = "pe", "act", "dve", "pool", "sp"
ENGS = (PE, ACT, DVE, POOL, SP)

D_MODEL = 2048
NCORE = 8
RMS_EPS = 1e-6
L2_EPS = 1e-6


class T:
    __slots__ = ("t", "w", "r", "name", "psum")

    def __init__(self, t, name="", psum=False):
        self.t = t
        self.w = {}
        self.r = {}
        self.name = name
        self.psum = psum

    def __getitem__(self, k):
        return self.t[k]


class Prog:
    NSLOT = 6

    def __init__(self, nc, es):
        self.nc = nc
        self.es = es
        self.ops = {e: [] for e in ENGS}
        self.cnt = {e: 0 for e in ENGS}
        self.sems = []
        self.esem = {}
        for e in ENGS:
            self.esem[e] = self._newsem("p_" + e)
        self.slots = {}
        self.slot_cnt = {}
        self.slot_next = {}
        for q in (SP, ACT, POOL):
            self.slots[q] = [self._newsem("d_%s%d" % (q, i)) for i in range(self.NSLOT)]
            self.slot_cnt[q] = [0] * self.NSLOT
            self.slot_next[q] = 0
        self.waited = {e: {} for e in ENGS}
        self.ninst = 0
        self.pes = None
        self.uid = 0

    def _newsem(self, name):
        h = self.es.enter_context(self.nc.semaphore(name))
        self.sems.append(h)
        return len(self.sems) - 1

    def sb(self, name, shape, dt):
        es = self.pes if self.pes is not None else self.es
        self.uid += 1
        name = "%s_%d" % (name, self.uid)
        return T(es.enter_context(self.nc.sbuf_tensor(name, list(shape), dt)), name)

    def ps(self, name, shape, dt=F32):
        es = self.pes if self.pes is not None else self.es
        self.uid += 1
        name = "%s_%d" % (name, self.uid)
        return T(es.enter_context(self.nc.psum_tensor(name, list(shape), dt)), name, psum=True)

    def dram(self, name, shape, dt, kind="Internal"):
        return T(self.nc.dram_tensor(name, list(shape), dt, kind=kind).ap(), name)

    def _deps(self, eng, reads, writes, is_dma):
        deps = {}
        own = None if is_dma else self.esem[eng]

        def add(d, skip_same):
            for s, v in d.items():
                if skip_same and s == own:
                    continue
                if deps.get(s, 0) < v:
                    deps[s] = v
        for t in reads:
            add(t.w, eng == PE)
            if t.psum:
                add(t.r, True)
        for t in writes:
            add(t.w, True)
            add(t.r, True)
        return deps

    def _waits(self, eng, deps):
        wl = []
        wd = self.waited[eng]
        for s, v in deps.items():
            if wd.get(s, 0) < v:
                wd[s] = v
                wl.append((s, v))
        return wl

    def _mark(self, ev, reads, writes):
        for t in reads:
            if t.r.get(ev[0], 0) < ev[1]:
                t.r[ev[0]] = ev[1]
        for t in writes:
            t.w = {ev[0]: ev[1]}
            t.r = {}

    def op(self, eng, fn, reads=(), writes=()):
        wl = self._waits(eng, self._deps(eng, reads, writes, False))
        self.cnt[eng] += 1
        ev = (self.esem[eng], self.cnt[eng])
        self.ops[eng].append((wl, fn, self.esem[eng], 1))
        self._mark(ev, reads, writes)
        return ev

    def dma(self, q, out, in_, reads=(), writes=(), **kw):
        k = self.slot_next[q]
        self.slot_next[q] = (k + 1) % self.NSLOT
        sem = self.slots[q][k]
        deps = self._deps(q, reads, writes, True)
        prev = 16 * self.slot_cnt[q][k]
        if prev > 0 and deps.get(sem, 0) < prev:
            deps[sem] = prev
        wl = self._waits(q, deps)
        self.slot_cnt[q][k] += 1
        ev = (sem, 16 * self.slot_cnt[q][k])

        def fn(e, out=out, in_=in_, kw=kw):
            return e.dma_start(out=out, in_=in_, **kw)
        self.ops[q].append((wl, fn, sem, 16))
        self._mark(ev, reads, writes)
        return ev

    def _all_events(self):
        ev = {}
        for e in ENGS:
            if self.cnt[e] > 0:
                ev[self.esem[e]] = self.cnt[e]
        for q in (SP, ACT, POOL):
            for k in range(self.NSLOT):
                if self.slot_cnt[q][k] > 0:
                    ev[self.slots[q][k]] = 16 * self.slot_cnt[q][k]
        return ev

    def flush(self, final=False):
        allev = self._all_events()
        sems = self.sems
        ops = self.ops
        waited = self.waited

        def run(e, name):
            for wl, fn, sem, inc in ops[name]:
                for s, v in wl:
                    e.wait_ge(sems[s], v)
                fn(e).then_inc(sems[sem], inc)
                self.ninst += 1 + len(wl)
            own = self.esem[name]
            for s, v in allev.items():
                if s == own:
                    continue
                if waited[name].get(s, 0) < v:
                    waited[name][s] = v
                    e.wait_ge(sems[s], v)
        with self.nc.Block() as block:
            @block.sync
            def _(e):
                run(e, SP)

            @block.tensor
            def _(e):
                run(e, PE)

            @block.scalar
            def _(e):
                run(e, ACT)

            @block.vector
            def _(e):
                run(e, DVE)

            @block.gpsimd
            def _(e):
                run(e, POOL)
        self.ops = {e: [] for e in ENGS}


def _mm_group(P, out_t, out_ap, pairs, reads, first=True, last=True):
    def fn(e, pairs=pairs, out_ap=out_ap):
        n = len(pairs)
        for i, (l, r) in enumerate(pairs):
            ins = e.matmul(out_ap, l, r, start=(first and i == 0), stop=(last and i == n - 1))
        return ins
    return P.op(PE, fn, reads=reads, writes=[out_t])


CH = 128
NLVL = 7


def _gdn_consts():
    i = np.arange(CH)
    s = i[:, None]
    c = i[None, :]
    f32 = np.zeros((CH, 36, CH), np.float32)
    f32[:, 0, :] = (s <= c)
    f32[:, 1, :] = (s >= c)
    f32[:, 2, :] = -1.0 * (c > s)
    f32[:, 3, :] = -1.0 * (c < s)
    f32[:, 4, :] = (c >= s)
    f32[:, 5, :] = (c <= s)
    for l in range(NLVL):
        b = 1 << l
        r = i[:, None]
        q = i[None, :]
        same = (r // (2 * b)) == (q // (2 * b))
        low = same & ((r // b) % 2 == 1) & ((q // b) % 2 == 0)
        f32[:, 6 + l, :] = low
        f32[:, 6 + NLVL + l, :] = low.T
    f32[:, 20, :] = np.eye(CH)
    f32[:, 21, :] = 1.0
    return f32


def build_A(seq_lens, TT=512, upto=9, debug=False):
    NTOK = sum(seq_lens)
    offs = np.concatenate([[0], np.cumsum(seq_lens)]).astype(int)
    nc = bass.Bass("TRN2", target_bir_lowering=False)
    with ExitStack() as es:
        P = Prog(nc, es)
        xT = P.dram("xT", [D_MODEL, NTOK], F32, kind="ExternalInput")
        gpre = P.dram("gpre", [128, 16], F32, kind="ExternalInput")
        Wqkv = P.dram("Wqkv", [D_MODEL, 1024], F32, kind="ExternalInput")
        Wz = P.dram("Wz", [D_MODEL, 512], F32, kind="ExternalInput")
        Wg = P.dram("Wg", [D_MODEL, 16], F32, kind="ExternalInput")
        convw = P.dram("convw", [128, 8, 5], F32, kind="ExternalInput")
        alog = P.dram("alog", [128, 8], F32, kind="ExternalInput")
        dtb = P.dram("dtb", [128, 8], F32, kind="ExternalInput")
        gnw = P.dram("gnw", [128, 128], F32, kind="ExternalInput")
        cst = P.dram("cst", [128, 36, 128], F32, kind="ExternalInput")
        og = P.dram("og", [NTOK, 512], BF16, kind="ExternalOutput")
        Wqkv_b = P.dram("Wqkv_b", [D_MODEL, 1024], BF16)
        Wz_b = P.dram("Wz_b", [D_MODEL, 512], BF16)
        Wg_b = P.dram("Wg_b", [D_MODEL, 16], BF16)
        dk = "ExternalOutput" if debug else "Internal"
        rawT = P.dram("rawT", [8, 128, NTOK], F32, kind=dk)
        zsil = P.dram("zsil", [NTOK, 512], F32, kind=dk)
        gates = P.dram("gates", [NTOK, 16], F32, kind=dk)
        qkT = P.dram("qkT", [4, 128, NTOK], BF16, kind=dk)
        ktok = P.dram("ktok", [NTOK, 2, 128], BF16, kind=dk)
        vtok = P.dram("vtok", [NTOK, 4, 128], BF16, kind=dk)
        osc = P.dram("osc", [2, NTOK, 512], F32, kind=dk)

        P.dma(POOL, Wqkv_b.t, Wqkv.t, reads=[Wqkv], writes=[Wqkv_b])
        P.dma(POOL, Wz_b.t, Wz.t, reads=[Wz], writes=[Wz_b])
        P.dma(POOL, Wg_b.t, Wg.t, reads=[Wg], writes=[Wg_b])

        with ExitStack() as pes:
            P.pes = pes
            wqkv = P.sb("wqkv", [128, 16, 1024], BF16)
            wz = P.sb("wz", [128, 16, 512], BF16)
            wg = P.sb("wg", [128, 16, 16], BF16)
            gt = P.sb("gt", [128, 16], F32)
            ones = P.sb("ones", [128, 128], BF16)
            alog_s = P.sb("alog_s", [128, 8], F32)
            nega = P.sb("nega", [128, 8], F32)
            dtb_s = P.sb("dtb_s", [128, 8], F32)
            xs = [P.sb("xs%d" % i, [128, 16, TT], F32) for i in range(2)]
            sq = P.sb("sq", [128, 16, TT], BF16)
            hb = P.sb("hb", [128, 16, TT], BF16)
            rstd = P.sb("rstd", [128, TT], F32)
            rcol = P.sb("rcol", [128, 4], F32)
            rawo = [P.sb("rawo%d" % i, [128, TT], F32) for i in range(2)]
            zo = [P.sb("zo%d" % i, [128, 512], F32) for i in range(2)]
            gsb = [P.sb("gsb%d" % i, [128, 16], F32) for i in range(2)]
            gtmp = P.sb("gtmp", [128, 8], F32)
            pss = P.ps("pss", [128, 512])
            psc = P.ps("psc", [128, 512])
            pm = [P.ps("pm%d" % i, [128, 512]) for i in range(4)]
            psg = P.ps("psg", [128, 512])

            P.dma(SP, wqkv[:], Wqkv_b.t.rearrange("(c p) m -> p c m", p=128), reads=[Wqkv_b], writes=[wqkv])
            P.dma(SP, wz[:], Wz_b.t.rearrange("(c p) m -> p c m", p=128), reads=[Wz_b], writes=[wz])
            P.dma(SP, wg[:], Wg_b.t.rearrange("(c p) m -> p c m", p=128), reads=[Wg_b], writes=[wg])
            P.dma(SP, gt[:], gpre.t, writes=[gt])
            P.dma(SP, alog_s[:], alog.t, writes=[alog_s])
            P.dma(SP, dtb_s[:], dtb.t, writes=[dtb_s])
            P.op(POOL, lambda e: e.memset(ones[:], 1.0), writes=[ones])
            P.op(ACT, lambda e: e.activation(out=nega[:], in_=alog_s[:], func=AF.Exp), reads=[alog_s], writes=[nega])
            P.op(DVE, lambda e: e.tensor_scalar(out=nega[:], in0=nega[:], scalar1=-1.0, scalar2=None, op0=ALU.mult),
                 reads=[nega], writes=[nega])

            ntile = NTOK // TT
            for ti in range(ntile):
                t0 = ti * TT
                x = xs[ti % 2]
                P.dma(SP, x[:], xT.t[:, t0:t0 + TT].rearrange("(c p) t -> p c t", p=128), writes=[x])
                P.op(ACT, lambda e, x=x: e.activation(out=sq[:], in_=x[:], func=AF.Square), reads=[x], writes=[sq])
                P.op(DVE, lambda e, x=x: e.tensor_tensor(out=hb[:], in0=x[:], in1=gt[:].unsqueeze(2).to_broadcast([128, 16, TT]),
                                                         op=ALU.mult), reads=[x, gt], writes=[hb])
                _mm_group(P, pss, pss[:, 0:TT], [(ones[:], sq[:, k, :]) for k in range(16)], [ones, sq])
                P.op(ACT, lambda e: e.activation(out=rstd[:], in_=pss[:, 0:TT], func=AF.Ln, bias=RMS_EPS, scale=1.0 / D_MODEL),
                     reads=[pss], writes=[rstd])
                P.op(ACT, lambda e: e.activation(out=rstd[:], in_=rstd[:], func=AF.Exp, scale=-0.5), reads=[rstd], writes=[rstd])
                nsub = TT // 128

                def colsum(e):
                    for sub in range(nsub):
                        for k in range(16):
                            ins = e.matmul(psc[:, sub:sub + 1], sq[:, k, sub * 128:(sub + 1) * 128], ones[:, 0:1],
                                           start=(k == 0), stop=(k == 15))
                    return ins
                P.op(PE, colsum, reads=[sq, ones], writes=[psc])
                P.op(ACT, lambda e: e.activation(out=rcol[:, 0:nsub], in_=psc[:, 0:nsub], func=AF.Ln, bias=RMS_EPS,
                                                 scale=1.0 / D_MODEL), reads=[psc], writes=[rcol])
                P.op(ACT, lambda e: e.activation(out=rcol[:, 0:nsub], in_=rcol[:, 0:nsub], func=AF.Exp, scale=-0.5),
                     reads=[rcol], writes=[rcol])
                for m in range(8):
                    p = pm[m % 4]
                    ro = rawo[m % 2]
                    _mm_group(P, p, p[:, 0:TT], [(wqkv[:, k, m * 128:(m + 1) * 128], hb[:, k, :]) for k in range(16)], [wqkv, hb])
                    P.op(DVE, lambda e, p=p, ro=ro: e.tensor_tensor(out=ro[:], in0=p[:, 0:TT], in1=rstd[:], op=ALU.mult),
                         reads=[p, rstd], writes=[ro])
                    P.dma(SP, rawT.t[m, :, t0:t0 + TT], ro[:], reads=[ro], writes=[rawT])
                for sub in range(nsub):
                    p = pm[sub % 4]
                    z = zo[sub % 2]
                    _mm_group(P, p, p[:, 0:512], [(hb[:, k, sub * 128:(sub + 1) * 128], wz[:, k, :]) for k in range(16)], [wz, hb])
                    P.op(ACT, lambda e, p=p, z=z, sub=sub: e.activation(out=z[:], in_=p[:, 0:512], func=AF.Silu,
                                                                        scale=rcol[:, sub:sub + 1]), reads=[p, rcol], writes=[z])
                    P.dma(SP, zsil.t[t0 + sub * 128:t0 + (sub + 1) * 128, :], z[:], reads=[z], writes=[zsil])
                    g = gsb[sub % 2]
                    _mm_group(P, psg, psg[:, 0:16], [(hb[:, k, sub * 128:(sub + 1) * 128], wg[:, k, :]) for k in range(16)], [wg, hb])
                    P.op(ACT, lambda e, g=g, sub=sub: e.activation(out=g[:, 0:8], in_=psg[:, 0:8], func=AF.Sigmoid,
                                                                   scale=rcol[:, sub:sub + 1]), reads=[psg, rcol], writes=[g])
                    P.op(DVE, lambda e, sub=sub: e.scalar_tensor_tensor(out=gtmp[:], in0=psg[:, 8:16], scalar=rcol[:, sub:sub + 1],
                                                                        in1=dtb_s[:], op0=ALU.mult, op1=ALU.add),
                         reads=[psg, rcol, dtb_s], writes=[gtmp])
                    P.op(ACT, lambda e: e.activation(out=gtmp[:], in_=gtmp[:], func=AF.Exp), reads=[gtmp], writes=[gtmp])
                    P.op(ACT, lambda e: e.activation(out=gtmp[:], in_=gtmp[:], func=AF.Ln, bias=1.0), reads=[gtmp], writes=[gtmp])
                    P.op(DVE, lambda e, g=g: e.tensor_tensor(out=g[:, 8:16], in0=gtmp[:], in1=nega[:], op=ALU.mult),
                         reads=[gtmp, nega], writes=[g])
                    P.dma(SP, gates.t[t0 + sub * 128:t0 + (sub + 1) * 128, :], g[:], reads=[g], writes=[gates])
            P.flush()
        P.pes = None

        if upto < 2:
            return nc
        with ExitStack() as pes:
            P.pes = pes
            cw = P.sb("cw", [128, 8, 5], F32)
            ones = P.sb("ones", [128, 128], BF16)
            identb = P.sb("identb", [128, 128], BF16)
            identf = P.sb("identf", [128, 128], F32)
            raw = [P.sb("raw%d" % i, [128, TT + 4], F32) for i in range(3)]
            acc = [P.sb("acc%d" % i, [128, TT], F32) for i in range(2)]
            sqb = P.sb("sqb", [128, TT], BF16)
            rn = P.sb("rn", [128, TT], F32)
            nb = [P.sb("nb%d" % i, [128, TT], BF16) for i in range(2)]
            tk = [P.sb("tk%d" % i, [128, TT // 128, 128], BF16) for i in range(2)]
            pss = P.ps("pss", [128, 512])
            ptr = [P.ps("ptr%d" % i, [128, 8, 128], BF16) for i in range(2)]
            P.dma(SP, cw[:], convw.t, writes=[cw])
            P.dma(SP, identf[:], cst.t[:, 20, :], writes=[identf])
            P.op(POOL, lambda e: e.memset(ones[:], 1.0), writes=[ones])
            P.op(DVE, lambda e: e.tensor_copy(out=identb[:], in_=identf[:]), reads=[identf], writes=[identb])
            nsub = TT // 128
            cnt = 0
            for si, L in enumerate(seq_lens):
                for ti in range(L // TT):
                    l0 = ti * TT
                    t0 = offs[si] + l0
                    for j in range(8):
                        r = raw[cnt % 3]
                        a = acc[cnt % 2]
                        o = nb[cnt % 2]
                        tkk = tk[cnt % 2]
                        pt = ptr[cnt % 2]
                        ceng = DVE
                        cnt += 1
                        lo = 2 if l0 == 0 else 0
                        hi = TT + 2 if l0 + TT == L else TT + 4
                        if lo or hi < TT + 4:
                            P.op(POOL, lambda e, r=r: e.memset(r[:], 0.0), writes=[r])
                        P.dma(SP, r[:, lo:hi], rawT.t[j, :, t0 - 2 + lo:t0 - 2 + hi], reads=[rawT], writes=[r])
                        P.op(ceng, lambda e, r=r, a=a, j=j: e.tensor_scalar(out=a[:], in0=r[:, 0:TT], scalar1=cw[:, j, 0:1],
                                                                            scalar2=None, op0=ALU.mult), reads=[r, cw], writes=[a])
                        for tap in range(1, 5):
                            P.op(ceng, lambda e, r=r, a=a, j=j, tap=tap: e.scalar_tensor_tensor(
                                out=a[:], in0=r[:, tap:tap + TT], scalar=cw[:, j, tap:tap + 1], in1=a[:],
                                op0=ALU.mult, op1=ALU.add), reads=[r, cw, a], writes=[a])
                        P.op(ACT, lambda e, a=a: e.activation(out=a[:], in_=a[:], func=AF.Silu), reads=[a], writes=[a])
                        if j < 4:
                            P.op(ACT, lambda e, a=a: e.activation(out=sqb[:], in_=a[:], func=AF.Square), reads=[a], writes=[sqb])
                            _mm_group(P, pss, pss[:, 0:TT], [(ones[:], sqb[:])], [ones, sqb])
                            P.op(ACT, lambda e: e.activation(out=rn[:], in_=pss[:, 0:TT], func=AF.Ln, bias=L2_EPS),
                                 reads=[pss], writes=[rn])
                            sc = -0.5
                            P.op(ACT, lambda e: e.activation(out=rn[:], in_=rn[:], func=AF.Exp, scale=-0.5), reads=[rn], writes=[rn])
                            if j < 2:
                                qs = float(128 ** -0.5)
                                P.op(DVE, lambda e, a=a, o=o, qs=qs: e.scalar_tensor_tensor(
                                    out=o[:], in0=a[:], scalar=qs, in1=rn[:], op0=ALU.mult, op1=ALU.mult),
                                    reads=[a, rn], writes=[o])
                            else:
                                P.op(DVE, lambda e, a=a, o=o: e.tensor_tensor(out=o[:], in0=a[:], in1=rn[:], op=ALU.mult),
                                     reads=[a, rn], writes=[o])
                            P.dma(SP, qkT.t[j, :, t0:t0 + TT], o[:], reads=[o], writes=[qkT])
                        else:
                            P.op(DVE, lambda e, a=a, o=o: e.tensor_copy(out=o[:], in_=a[:]), reads=[a], writes=[o])
                        if j >= 2:
                            def trs(e, o=o, pt=pt):
                                for sub in range(nsub):
                                    ins = e.transpose(pt[:, sub, :], o[:, sub * 128:(sub + 1) * 128], identb[:])
                                return ins
                            P.op(PE, trs, reads=[o, identb], writes=[pt])
                            P.op(ACT, lambda e, pt=pt, tkk=tkk: e.activation(out=tkk[:], in_=pt[:, 0:nsub, :], func=AF.Copy),
                                 reads=[pt], writes=[tkk])
                            if j < 4:
                                dst = ktok.t[t0:t0 + TT, j - 2, :]
                                dt_ = ktok
                            else:
                                dst = vtok.t[t0:t0 + TT, j - 4, :]
                                dt_ = vtok
                            P.dma(SP, dst.rearrange("(s p) d -> p s d", p=128), tkk[:], reads=[tkk], writes=[dt_])
            P.flush()
        P.pes = None

        if upto < 3:
            return nc
        with ExitStack() as pes:
            P.pes = pes
            cs = P.sb("cs", [128, 36, 128], F32)
            onesf = P.sb("onesf", [128, 128], F32)
            identb = P.sb("identb", [128, 128], BF16)
            P.dma(SP, cs[:], cst.t, writes=[cs])
            P.op(POOL, lambda e: e.memset(onesf[:], 1.0), writes=[onesf])
            P.op(DVE, lambda e: e.tensor_copy(out=identb[:], in_=cs[:, 20, :]), reads=[cs], writes=[identb])
            pkq = P.ps("pkq", [128, 4, 128])
            pg = P.ps("pg", [128, 4, 128])
            pc = P.ps("pc", [128, 512])
            pT_raw = P.ps("pT", [128, 2, 512], BF16)
            pTd = [pT_raw, pT_raw]
            S = {}
            for d in range(2):
                S[d] = dict(
                    pX=P.ps("pX%d" % d, [128, 4, 128]), pY=P.ps("pY%d" % d, [128, 4, 128]),
                    kT=P.sb("kT%d" % d, [128, 2, 128], BF16), qT=P.sb("qT%d" % d, [128, 2, 128], BF16),
                    kt=P.sb("kt%d" % d, [128, 2, 128], BF16), vt=P.sb("vt%d" % d, [128, 4, 128], BF16),
                    gt=P.sb("gt%d" % d, [128, 16], F32),
                    Ug=P.sb("Ug%d" % d, [128, 4, 128], F32), gcc=P.sb("gcc%d" % d, [128, 4], F32),
                    Dm=P.sb("Dm%d" % d, [128, 4, 128], F32), Dsn=P.sb("Dsn%d" % d, [128, 4, 128], F32),
                    Dt=P.sb("Dt%d" % d, [128, 4, 128], F32),
                    LpT=P.sb("LpT%d" % d, [128, 4, 128], BF16), QKm=P.sb("QKm%d" % d, [128, 4, 128], BF16),
                    Pm=P.sb("Pm%d" % d, [128, 4, 128], BF16), Qm=P.sb("Qm%d" % d, [128, 4, 128], BF16),
                    X=P.sb("X%d" % d, [128, 4, 128], BF16), tmp=P.sb("tmp%d" % d, [128, 4, 128], BF16),
                    egc=P.sb("egc%d" % d, [128, 4], F32), Rg=P.sb("Rg%d" % d, [128, 4, 128], BF16),
                    nwT=P.sb("nwT%d" % d, [128, 4, 128], BF16), gl=P.sb("gl%d" % d, [128, 4], F32),
                    kd=P.sb("kd%d" % d, [128, 4], F32), egl=P.sb("egl%d" % d, [128, 4], F32),
                    vnb=P.sb("vnb%d" % d, [128, 4, 128], BF16), vns=P.sb("vns%d" % d, [128, 4, 128], BF16),
                    Asb=P.sb("Asb%d" % d, [128, 4, 128], F32), osb=P.sb("osb%d" % d, [128, 4, 128], F32),
                    St=P.sb("St%d" % d, [128, 4, 128], F32), Sb=P.sb("Sb%d" % d, [128, 4, 128], BF16),
                    Stmp=P.sb("Stmp%d" % d, [128, 4, 128], F32),
                )

            def bc_h(ap2):
                return ap2.unsqueeze(1).to_broadcast([128, 4, 128])

            def bc_c(ap2):
                return ap2.unsqueeze(2).to_broadcast([128, 4, 128])

            def unit_stages(d, t0):
                B = S[d]
                last = (CH - 1) if d == 0 else 0
                st = []

                def s_load():
                    P.dma(SP, B["kT"][:], qkT.t[2:4, :, t0:t0 + CH].rearrange("h p c -> p h c"), reads=[qkT], writes=[B["kT"]])
                    P.dma(SP, B["qT"][:], qkT.t[0:2, :, t0:t0 + CH].rearrange("h p c -> p h c"), reads=[qkT], writes=[B["qT"]])
                    P.dma(SP, B["kt"][:], ktok.t[t0:t0 + CH, :, :], reads=[ktok], writes=[B["kt"]])
                    P.dma(SP, B["vt"][:], vtok.t[t0:t0 + CH, :, :], reads=[vtok], writes=[B["vt"]])
                    P.dma(SP, B["gt"][:], gates.t[t0:t0 + CH, :], reads=[gates], writes=[B["gt"]])
                st.append(s_load)

                def s_gates():
                    g = B["gt"]
                    for h in range(4):
                        P.op(POOL, lambda e, h=h: e.tensor_scalar(out=B["Ug"][:, h, :], in0=cs[:, d, :],
                                                                  scalar1=g[:, 8 + 4 * d + h:9 + 4 * d + h], scalar2=None,
                                                                  op0=ALU.mult), reads=[cs, g], writes=[B["Ug"]])
                    _mm_group(P, pg, pg[:].rearrange("p h c -> p (h c)"),
                              [(onesf[:], B["Ug"][:].rearrange("p h c -> p (h c)"))], [onesf, B["Ug"]])
                    _mm_group(P, pc, pc[:, 0:4], [(cs[:, d, :], g[:, 8 + 4 * d:12 + 4 * d])], [cs, g])
                    P.op(ACT, lambda e: e.activation(out=B["gcc"][:], in_=pc[:, 0:4], func=AF.Copy), reads=[pc], writes=[B["gcc"]])
                    P.op(DVE, lambda e: e.tensor_copy(out=B["gl"][:], in_=pg[:, :, last]), reads=[pg], writes=[B["gl"]])
                    for h in range(4):
                        P.op(DVE, lambda e, h=h: e.tensor_scalar(out=B["Dm"][:, h, :], in0=pg[:, h, :],
                                                                 scalar1=B["gcc"][:, h:h + 1], scalar2=0.0,
                                                                 op0=ALU.subtract, op1=ALU.min),
                             reads=[pg, B["gcc"]], writes=[B["Dm"]])
                    P.op(ACT, lambda e: e.activation(out=B["Dm"][:], in_=B["Dm"][:], func=AF.Exp), reads=[B["Dm"]], writes=[B["Dm"]])
                    P.op(POOL, lambda e: e.tensor_tensor(out=B["Dsn"][:], in0=B["Dm"][:], in1=bc_h(cs[:, 2 + d, :]), op=ALU.mult),
                         reads=[B["Dm"], cs], writes=[B["Dsn"]])
                    P.op(POOL, lambda e: e.tensor_tensor(out=B["Dt"][:], in0=B["Dm"][:], in1=bc_h(cs[:, 4 + d, :]), op=ALU.mult),
                         reads=[B["Dm"], cs], writes=[B["Dt"]])
                    P.op(ACT, lambda e: e.activation(out=B["egc"][:], in_=B["gcc"][:], func=AF.Exp), reads=[B["gcc"]], writes=[B["egc"]])
                    P.op(DVE, lambda e: e.tensor_tensor(out=B["kd"][:], in0=B["gl"][:], in1=B["gcc"][:], op=ALU.subtract),
                         reads=[B["gl"], B["gcc"]], writes=[B["kd"]])
                    P.op(ACT, lambda e: e.activation(out=B["kd"][:], in_=B["kd"][:], func=AF.Exp), reads=[B["kd"]], writes=[B["kd"]])
                    P.op(ACT, lambda e: e.activation(out=B["egl"][:], in_=B["gl"][:], func=AF.Exp), reads=[B["gl"]], writes=[B["egl"]])
                st.append(s_gates)

                def s_kk():
                    def fn(e):
                        for hq in range(2):
                            e.matmul(pkq[:, hq, :], B["kT"][:, hq, :], B["kT"][:, hq, :], start=True, stop=True)
                        for hq in range(2):
                            ins = e.matmul(pkq[:, 2 + hq, :], B["kT"][:, hq, :], B["qT"][:, hq, :], start=True, stop=True)
                        return ins
                    P.op(PE, fn, reads=[B["kT"], B["qT"]], writes=[pkq])
                    for hq in range(2):
                        P.op(DVE, lambda e, hq=hq: e.tensor_tensor(
                            out=B["LpT"][:, 2 * hq:2 * hq + 2, :],
                            in0=pkq[:, hq, :].unsqueeze(1).to_broadcast([128, 2, 128]),
                            in1=B["Dsn"][:, 2 * hq:2 * hq + 2, :], op=ALU.mult), reads=[pkq, B["Dsn"]], writes=[B["LpT"]])
                        P.op(DVE, lambda e, hq=hq: e.tensor_tensor(
                            out=B["QKm"][:, 2 * hq:2 * hq + 2, :],
                            in0=pkq[:, 2 + hq, :].unsqueeze(1).to_broadcast([128, 2, 128]),
                            in1=B["Dt"][:, 2 * hq:2 * hq + 2, :], op=ALU.mult), reads=[pkq, B["Dt"]], writes=[B["QKm"]])
                    bsl = B["gt"][:, 4 * d:4 * d + 4]
                    P.op(POOL, lambda e: e.tensor_tensor(out=B["Pm"][:], in0=bc_h(cs[:, 20, :]), in1=bc_c(bsl), op=ALU.mult),
                         reads=[cs, B["gt"]], writes=[B["Pm"]])
                    P.op(POOL, lambda e: e.tensor_tensor(out=B["Qm"][:], in0=bc_h(cs[:, 20, :]), in1=bc_c(bsl), op=ALU.mult),
                         reads=[cs, B["gt"]], writes=[B["Qm"]])
                st.append(s_kk)

                for l in range(NLVL):
                    def s_x(l=l):
                        def fn(e):
                            for h in range(4):
                                ins = e.matmul(B["pX"][:, h, :], B["LpT"][:, h, :], B["Pm"][:, h, :], start=True, stop=True)
                            return ins
                        P.op(PE, fn, reads=[B["LpT"], B["Pm"]], writes=[B["pX"]])
                        P.op(ACT, lambda e: e.activation(out=B["X"][:], in_=B["pX"][:], func=AF.Copy), reads=[B["pX"]], writes=[B["X"]])
                    st.append(s_x)

                    def s_y(l=l):
                        def fn(e):
                            for h in range(4):
                                ins = e.matmul(B["pY"][:, h, :], B["Qm"][:, h, :], B["X"][:, h, :], start=True, stop=True)
                            return ins
                        P.op(PE, fn, reads=[B["Qm"], B["X"]], writes=[B["pY"]])
                        mk = cs[:, 6 + d * NLVL + l, :]
                        P.op(DVE, lambda e: e.tensor_tensor(out=B["tmp"][:], in0=B["pY"][:], in1=bc_h(mk), op=ALU.mult),
                             reads=[B["pY"], cs], writes=[B["tmp"]])
                        P.op(POOL, lambda e: e.tensor_tensor(out=B["Pm"][:], in0=B["Pm"][:], in1=B["tmp"][:], op=ALU.add),
                             reads=[B["Pm"], B["tmp"]], writes=[B["Pm"]])
                    st.append(s_y)

                    def s_t(l=l):
                        def fn(e):
                            for h in range(4):
                                ins = e.transpose(pTd[d][:, d, h * 128:(h + 1) * 128], B["Pm"][:, h, :], identb[:])
                            return ins
                        P.op(PE, fn, reads=[B["Pm"], identb], writes=[pTd[d]])
                        P.op(ACT, lambda e: e.activation(out=B["Qm"][:].rearrange("p h c -> p (h c)"), in_=pTd[d][:, d, :], func=AF.Copy),
                             reads=[pTd[d]], writes=[B["Qm"]])
                    st.append(s_t)

                def s_scan1():
                    P.op(DVE, lambda e: e.tensor_tensor(out=B["Rg"][:], in0=B["Qm"][:], in1=bc_c(B["egc"][:]), op=ALU.mult),
                         reads=[B["Qm"], B["egc"]], writes=[B["Rg"]])

                    def fn(e):
                        for h in range(4):
                            ins = e.matmul(B["pX"][:, h, :], B["kt"][:, h // 2, :], B["Rg"][:, h, :], start=True, stop=True)
                        return ins
                    P.op(PE, fn, reads=[B["kt"], B["Rg"]], writes=[B["pX"]])
                    P.op(ACT, lambda e: e.activation(out=B["nwT"][:], in_=B["pX"][:], func=AF.Copy, scale=-1.0),
                         reads=[B["pX"]], writes=[B["nwT"]])
                st.append(s_scan1)

                def s_scan2():
                    def fn(e):
                        for h in range(4):
                            e.matmul(B["pY"][:, h, :], B["Qm"][:, h, :], B["vt"][:, h, :], start=True, stop=False)
                            ins = e.matmul(B["pY"][:, h, :], B["nwT"][:, h, :], B["Sb"][:, h, :], start=False, stop=True)
                        return ins
                    P.op(PE, fn, reads=[B["Qm"], B["vt"], B["nwT"], B["Sb"]], writes=[B["pY"]])
                    P.op(ACT, lambda e: e.activation(out=B["vnb"][:], in_=B["pY"][:], func=AF.Copy), reads=[B["pY"]], writes=[B["vnb"]])
                    P.op(DVE, lambda e: e.tensor_tensor(out=B["vns"][:], in0=B["pY"][:], in1=bc_c(B["kd"][:]), op=ALU.mult),
                         reads=[B["pY"], B["kd"]], writes=[B["vns"]])

                    def fa(e):
                        for h in range(4):
                            ins = e.matmul(B["pX"][:, h, :], B["qT"][:, h // 2, :], B["Sb"][:, h, :], start=True, stop=True)
                        return ins
                    P.op(PE, fa, reads=[B["qT"], B["Sb"]], writes=[B["pX"]])
                    P.op(DVE, lambda e: e.tensor_tensor(out=B["Asb"][:], in0=B["pX"][:], in1=bc_c(B["egc"][:]), op=ALU.mult),
                         reads=[B["pX"], B["egc"]], writes=[B["Asb"]])
                st.append(s_scan2)

                def s_scan3():
                    def fb(e):
                        for h in range(4):
                            ins = e.matmul(B["pY"][:, h, :], B["QKm"][:, h, :], B["vnb"][:, h, :], start=True, stop=True)
                        return ins
                    P.op(PE, fb, reads=[B["QKm"], B["vnb"]], writes=[B["pY"]])
                    P.op(DVE, lambda e: e.tensor_tensor(out=B["osb"][:], in0=B["pY"][:], in1=B["Asb"][:], op=ALU.add),
                         reads=[B["pY"], B["Asb"]], writes=[B["osb"]])
                    P.dma(SP, osc.t[d, t0:t0 + CH, :], B["osb"][:].rearrange("p h c -> p (h c)"), reads=[B["osb"]], writes=[osc])

                    def fs(e):
                        for h in range(4):
                            ins = e.matmul(B["pX"][:, h, :], B["kt"][:, h // 2, :], B["vns"][:, h, :], start=True, stop=True)
                        return ins
                    P.op(PE, fs, reads=[B["kt"], B["vns"]], writes=[B["pX"]])
                    P.op(POOL, lambda e: e.tensor_tensor(out=B["Stmp"][:], in0=B["St"][:], in1=bc_c(B["egl"][:]), op=ALU.mult),
                         reads=[B["St"], B["egl"]], writes=[B["Stmp"]])
                    P.op(DVE, lambda e: e.tensor_tensor(out=B["St"][:], in0=B["pX"][:], in1=B["Stmp"][:], op=ALU.add),
                         reads=[B["pX"], B["Stmp"]], writes=[B["St"]])
                    P.op(ACT, lambda e: e.activation(out=B["Sb"][:], in_=B["St"][:], func=AF.Copy), reads=[B["St"]], writes=[B["Sb"]])
                st.append(s_scan3)
                return st

            for si, L in enumerate(seq_lens):
                nch = L // CH
                for d in range(2):
                    P.op(POOL, lambda e, d=d: e.memset(S[d]["St"][:], 0.0), writes=[S[d]["St"]])
                    P.op(POOL, lambda e, d=d: e.memset(S[d]["Sb"][:], 0.0), writes=[S[d]["Sb"]])
                for i in range(nch):
                    sf = unit_stages(0, offs[si] + i * CH)
                    sbk = unit_stages(1, offs[si] + (nch - 1 - i) * CH)
                    for a, b in zip(sf, sbk):
                        a()
                        b()
            P.flush()
        P.pes = None

        if upto < 4:
            return nc
        with ExitStack() as pes:
            P.pes = pes
            gn = P.sb("gn", [128, 128], F32)
            P.dma(SP, gn[:], gnw.t, writes=[gn])
            of = [P.sb("of%d" % i, [128, 4, 128], F32) for i in range(2)]
            ob = [P.sb("ob%d" % i, [128, 4, 128], F32) for i in range(2)]
            zz = [P.sb("zz%d" % i, [128, 4, 128], F32) for i in range(2)]
            sqj = P.sb("sqj", [128, 128], F32)
            ssum = P.sb("ssum", [128, 4], F32)
            oo = [P.sb("oo%d" % i, [128, 4, 128], BF16) for i in range(2)]
            for bi in range(NTOK // 128):
                t0 = bi * 128
                a, b, z, o = of[bi % 2], ob[bi % 2], zz[bi % 2], oo[bi % 2]
                P.dma(SP, a[:].rearrange("p h c -> p (h c)"), osc.t[0, t0:t0 + 128, :], reads=[osc], writes=[a])
                P.dma(SP, b[:].rearrange("p h c -> p (h c)"), osc.t[1, t0:t0 + 128, :], reads=[osc], writes=[b])
                P.dma(SP, z[:].rearrange("p h c -> p (h c)"), zsil.t[t0:t0 + 128, :], reads=[zsil], writes=[z])
                P.op(DVE, lambda e, a=a, b=b: e.tensor_tensor(out=a[:], in0=a[:], in1=b[:], op=ALU.add), reads=[a, b], writes=[a])
                P.op(POOL, lambda e: e.memset(ssum[:], 0.0), writes=[ssum])
                for h in range(4):
                    P.op(ACT, lambda e, a=a, h=h: e.activation(out=sqj[:], in_=a[:, h, :], func=AF.Square,
                                                               accum_out=ssum[:, h:h + 1]), reads=[a], writes=[sqj, ssum])
                P.op(ACT, lambda e: e.activation(out=ssum[:], in_=ssum[:], func=AF.Ln, bias=RMS_EPS, scale=1.0 / 128),
                     reads=[ssum], writes=[ssum])
                P.op(ACT, lambda e: e.activation(out=ssum[:], in_=ssum[:], func=AF.Exp, scale=-0.5), reads=[ssum], writes=[ssum])
                P.op(POOL, lambda e, z=z: e.tensor_tensor(out=z[:], in0=z[:], in1=gn[:].unsqueeze(1).to_broadcast([128, 4, 128]),
                                                          op=ALU.mult), reads=[z, gn], writes=[z])
                P.op(DVE, lambda e, a=a: e.tensor_tensor(out=a[:], in0=a[:], in1=ssum[:].unsqueeze(2).to_broadcast([128, 4, 128]),
                                                         op=ALU.mult), reads=[a, ssum], writes=[a])
                P.op(DVE, lambda e, a=a, z=z, o=o: e.tensor_tensor(out=o[:], in0=a[:], in1=z[:], op=ALU.mult),
                     reads=[a, z], writes=[o])
                P.dma(SP, og.t[t0:t0 + 128, :], o[:].rearrange("p h c -> p (h c)"), reads=[o], writes=[og])
            P.flush()
        P.pes = None
    return nc


GDN_QK_HEADS = 16
GDN_V_HEADS = 32
GDN_Q_DIM = 2048
GDN_V_DIM = 4096
GDN_CONV_DIM = 8192


def prep_A(c, xT_all, norm_mix_pre, gdn_w_in, gdn_conv_w, gdn_a_log, gdn_dt_bias, gdn_norm_w, cst):
    w_in = gdn_w_in[0]
    qh = [2 * c, 2 * c + 1]
    vh = [4 * c + i for i in range(4)]
    qcols = np.concatenate([np.arange(h * 128, (h + 1) * 128) for h in qh])
    kcols = GDN_Q_DIM + qcols
    vcols = 2 * GDN_Q_DIM + np.concatenate([np.arange(h * 128, (h + 1) * 128) for h in vh])
    zcols = GDN_CONV_DIM + np.concatenate([np.arange(h * 128, (h + 1) * 128) for h in vh])
    o_b = GDN_CONV_DIM + GDN_V_DIM
    o_a = o_b + 2 * GDN_V_HEADS
    bcols = np.array([o_b + d * GDN_V_HEADS + h for d in range(2) for h in vh])
    acols = np.array([o_a + d * GDN_V_HEADS + h for d in range(2) for h in vh])
    conv_cols = np.concatenate([qcols, kcols, vcols])
    cw = gdn_conv_w[0][:, conv_cols]
    cw = np.ascontiguousarray(cw.reshape(5, 8, 128).transpose(2, 1, 0))
    al = np.array([gdn_a_log[0, d, h] for d in range(2) for h in vh], np.float32)
    db = np.array([gdn_dt_bias[0, d, h] for d in range(2) for h in vh], np.float32)
    return {
        "xT": xT_all,
        "gpre": np.ascontiguousarray(norm_mix_pre[0].reshape(16, 128).T),
        "Wqkv": np.ascontiguousarray(w_in[:, conv_cols]),
        "Wz": np.ascontiguousarray(w_in[:, zcols]),
        "Wg": np.ascontiguousarray(w_in[:, np.concatenate([bcols, acols])]),
        "convw": cw,
        "alog": np.ascontiguousarray(np.broadcast_to(al[None, :], (128, 8))),
        "dtb": np.ascontiguousarray(np.broadcast_to(db[None, :], (128, 8))),
        "gnw": np.ascontiguousarray(np.broadcast_to(gdn_norm_w[0][None, :], (128, 128))),
        "cst": cst,
    }


def run_A(seq_lens, xT_all, norm_mix_pre, gdn_w_in, gdn_conv_w, gdn_a_log, gdn_dt_bias, gdn_norm_w):
    cst = _gdn_consts()
    nc = build_A(seq_lens)
    in_maps = [prep_A(c, xT_all, norm_mix_pre, gdn_w_in, gdn_conv_w, gdn_a_log, gdn_dt_bias, gdn_norm_w, cst)
               for c in range(NCORE)]
    res = run_bass_kernel_spmd(nc, in_maps, core_ids=list(range(NCORE)))
    return np.concatenate([np.asarray(res.results[c]["og"]) for c in range(NCORE)], axis=1)


def lay_w(W):
    K, M = W.shape
    return np.ascontiguousarray(W.reshape(K // 128, 128, M // 128, 128).transpose(2, 1, 0, 3))


def lay_g(g):
    return np.ascontiguousarray(g.reshape(-1, 128).T)


class PostBufs:
    def __init__(self, P, TT, Kc):
        self.TT = TT
        self.xs = P.sb("xs", [128, 16, TT], F32)
        self.at = P.sb("at", [128, Kc, TT], BF16)
        self.mT = P.sb("mT", [128, 16, TT], F32)
        self.sq = P.sb("sq", [128, 16, TT], BF16)
        self.hb = P.sb("hb", [128, 16, TT], BF16)
        self.hid = P.sb("hid", [128, 64, TT], BF16)
        self.rstd = P.sb("rstd", [128, TT], F32)
        self.r2 = P.sb("r2", [128, TT], F32)
        self.tmpf = [P.sb("tmpf%d" % i, [128, TT], F32) for i in range(2)]
        self.wA = [P.sb("wA%d" % i, [128, Kc, 128], BF16) for i in range(2)]
        self.wI = [P.sb("wI%d" % i, [128, 16, 128], BF16) for i in range(3)]
        self.wO = [P.sb("wO%d" % i, [128, 32, 128], BF16) for i in range(2)]
        self.ones = P.sb("ones", [128, 128], BF16)
        self.pacc = [P.ps("pacc%d" % i, [128, 512]) for i in range(4)]
        self.pss = P.ps("pss", [128, 512])
        self.pi = 0
        P.op(POOL, lambda e: e.memset(self.ones[:], 1.0), writes=[self.ones])

    def nextp(self):
        p = self.pacc[self.pi % 4]
        self.pi += 1
        return p


class MlaBufs:
    def __init__(self, P, TT):
        self.TT = TT
        self.xs = P.sb("xs", [128, 16, TT], F32)
        self.sq = P.sb("sq", [128, 16, TT], BF16)
        self.hb = P.sb("hb", [128, 16, TT], BF16)
        self.rstd = P.sb("rstd", [128, TT], F32)
        self.ones = P.sb("ones", [128, 128], BF16)
        self.pacc = [P.ps("pacc%d" % i, [128, 512]) for i in range(4)]
        self.pss = P.ps("pss", [128, 512])
        self.pi = 0
        P.op(POOL, lambda e: e.memset(self.ones[:], 1.0), writes=[self.ones])

    def nextp(self):
        p = self.pacc[self.pi % 4]
        self.pi += 1
        return p


def _rms_rstd(P, Bf, src_t, nch, dim, out_t):
    TT = Bf.TT
    _mm_group(P, Bf.pss, Bf.pss[:, 0:TT], [(Bf.ones[:], Bf.sq[:, k, :]) for k in range(nch)], [Bf.ones, Bf.sq])
    P.op(ACT, lambda e: e.activation(out=out_t[:], in_=Bf.pss[:, 0:TT], func=AF.Ln, bias=RMS_EPS, scale=1.0 / dim),
         reads=[Bf.pss], writes=[out_t])
    P.op(ACT, lambda e: e.activation(out=out_t[:], in_=out_t[:], func=AF.Exp, scale=-0.5), reads=[out_t], writes=[out_t])


def _norm_residual(P, Bf, g_t):
    TT = Bf.TT
    P.op(POOL, lambda e: e.tensor_tensor(out=Bf.sq[:], in0=Bf.mT[:], in1=Bf.mT[:], op=ALU.mult), reads=[Bf.mT], writes=[Bf.sq])
    _rms_rstd(P, Bf, Bf.mT, 16, D_MODEL, Bf.rstd)
    for mc in range(16):
        tf = Bf.tmpf[mc % 2]
        P.op(POOL, lambda e, mc=mc, tf=tf: e.tensor_tensor(out=tf[:], in0=Bf.mT[:, mc, :], in1=Bf.rstd[:], op=ALU.mult),
             reads=[Bf.mT, Bf.rstd], writes=[tf])
        P.op(DVE, lambda e, mc=mc, tf=tf: e.scalar_tensor_tensor(out=Bf.xs[:, mc, :], in0=tf[:], scalar=g_t[:, mc:mc + 1],
                                                                 in1=Bf.xs[:, mc, :], op0=ALU.mult, op1=ALU.add),
             reads=[tf, g_t, Bf.xs], writes=[Bf.xs])


def post_block(P, Bf, Kc, a_src, Wmix_b, x_src, g_post, g_fpre, g_fpost, Wfi_b, Wfo_b, t0):
    TT = Bf.TT
    P.dma(SP, Bf.xs[:], x_src.t[:, t0:t0 + TT].rearrange("(c p) t -> p c t", p=128), reads=[x_src], writes=[Bf.xs])
    P.dma(SP, Bf.at[:], a_src.t[:, t0:t0 + TT].rearrange("(c p) t -> p c t", p=128), reads=[a_src], writes=[Bf.at])
    for mc in range(16):
        w = Bf.wA[mc % 2]
        P.dma(SP, w[:], Wmix_b.t[mc], reads=[Wmix_b], writes=[w])
        p = Bf.nextp()
        _mm_group(P, p, p[:, 0:TT], [(w[:, k, :], Bf.at[:, k, :]) for k in range(Kc)], [w, Bf.at])
        P.op(ACT, lambda e, p=p, mc=mc: e.activation(out=Bf.mT[:, mc, :], in_=p[:, 0:TT], func=AF.Copy), reads=[p], writes=[Bf.mT])
    _norm_residual(P, Bf, g_post)
    P.op(ACT, lambda e: e.activation(out=Bf.sq[:], in_=Bf.xs[:], func=AF.Square), reads=[Bf.xs], writes=[Bf.sq])
    P.op(DVE, lambda e: e.tensor_tensor(out=Bf.hb[:], in0=Bf.xs[:], in1=g_fpre[:].unsqueeze(2).to_broadcast([128, 16, TT]),
                                        op=ALU.mult), reads=[Bf.xs, g_fpre], writes=[Bf.hb])
    _rms_rstd(P, Bf, Bf.xs, 16, D_MODEL, Bf.rstd)
    P.op(DVE, lambda e: e.tensor_tensor(out=Bf.r2[:], in0=Bf.rstd[:], in1=Bf.rstd[:], op=ALU.mult), reads=[Bf.rstd], writes=[Bf.r2])
    for mc in range(64):
        w = Bf.wI[mc % 3]
        P.dma(SP, w[:], Wfi_b.t[mc], reads=[Wfi_b], writes=[w])
        p = Bf.nextp()
        _mm_group(P, p, p[:, 0:TT], [(w[:, k, :], Bf.hb[:, k, :]) for k in range(16)], [w, Bf.hb])
        tf = Bf.tmpf[mc % 2]
        P.op(ACT, lambda e, p=p, tf=tf: e.activation(out=tf[:], in_=p[:, 0:TT], func=AF.Relu), reads=[p], writes=[tf])
        P.op(POOL, lambda e, mc=mc, tf=tf: e.tensor_tensor(out=Bf.hid[:, mc, :], in0=tf[:], in1=tf[:], op=ALU.mult),
             reads=[tf], writes=[Bf.hid])
    for mc in range(16):
        p = Bf.nextp()
        for hf in range(2):
            w = Bf.wO[hf]
            P.dma(SP, w[:], Wfo_b.t[mc, :, hf * 32:(hf + 1) * 32, :], reads=[Wfo_b], writes=[w])
            _mm_group(P, p, p[:, 0:TT], [(w[:, k, :], Bf.hid[:, hf * 32 + k, :]) for k in range(32)], [w, Bf.hid],
                      first=(hf == 0), last=(hf == 1))
        P.op(DVE, lambda e, p=p, mc=mc: e.tensor_tensor(out=Bf.mT[:, mc, :], in0=p[:, 0:TT], in1=Bf.r2[:], op=ALU.mult),
             reads=[p, Bf.r2], writes=[Bf.mT])
    _norm_residual(P, Bf, g_fpost)


def cast_w(P, dst, src, nsplit):
    n = src.t.shape[0]
    step = max(1, n // nsplit)
    for i in range(0, n, step):
        P.dma(POOL, dst.t[i:i + step], src.t[i:i + step], reads=[src], writes=[dst])


def build_B(NT, TT=256, debug=False):
    nc = bass.Bass("TRN2", target_bir_lowering=False)
    with ExitStack() as es:
        P = Prog(nc, es)
        xT = P.dram("xT", [D_MODEL, NT], F32, kind="ExternalInput")
        ogT = P.dram("ogT", [4096, NT], BF16, kind="ExternalInput")
        Wout = P.dram("Wout", [16, 128, 32, 128], F32, kind="ExternalInput")
        Wfi = P.dram("Wfi", [64, 128, 16, 128], F32, kind="ExternalInput")
        Wfo = P.dram("Wfo", [16, 128, 64, 128], F32, kind="ExternalInput")
        Wa = P.dram("Wa", [12, 128, 16, 128], F32, kind="ExternalInput")
        Wq = P.dram("Wq", [48, 128, 6, 128], F32, kind="ExternalInput")
        Wk = P.dram("Wk", [16, 128, 4, 128], F32, kind="ExternalInput")
        Wv = P.dram("Wv", [128, 4, 2048], F32, kind="ExternalInput")
        gv = P.dram("gv", [128, 4, 16], F32, kind="ExternalInput")
        gq = P.dram("gq", [128, 6], F32, kind="ExternalInput")
        gkv = P.dram("gkv", [128, 4], F32, kind="ExternalInput")
        C2 = P.dram("C2", [64, NT], F32, kind="ExternalInput")
        S2 = P.dram("S2", [64, NT], F32, kind="ExternalInput")
        E64 = P.dram("E64", [128, 65], F32, kind="ExternalInput")
        x2T = P.dram("x2T", [D_MODEL, NT], F32, kind="ExternalOutput")
        QN = P.dram("QN", [16, 128, NT], BF16, kind="ExternalOutput")
        QR = P.dram("QR", [16, 65, NT], BF16, kind="ExternalOutput")
        KN = P.dram("KN", [16, 128, NT], BF16, kind="ExternalOutput")
        KR = P.dram("KR", [65, NT], BF16, kind="ExternalOutput")
        V = P.dram("V", [NT, 2048], BF16, kind="ExternalOutput")
        Wout_b = P.dram("Wout_b", [16, 128, 32, 128], BF16)
        Wfi_b = P.dram("Wfi_b", [64, 128, 16, 128], BF16)
        Wfo_b = P.dram("Wfo_b", [16, 128, 64, 128], BF16)
        Wa_b = P.dram("Wa_b", [12, 128, 16, 128], BF16)
        Wq_b = P.dram("Wq_b", [48, 128, 6, 128], BF16)
        Wk_b = P.dram("Wk_b", [16, 128, 4, 128], BF16)
        Wv_b = P.dram("Wv_b", [128, 4, 2048], BF16)
        cast_w(P, Wout_b, Wout, 4)
        cast_w(P, Wfi_b, Wfi, 8)
        cast_w(P, Wfo_b, Wfo, 8)
        cast_w(P, Wa_b, Wa, 2)
        cast_w(P, Wq_b, Wq, 2)
        cast_w(P, Wk_b, Wk, 1)
        cast_w(P, Wv_b, Wv, 1)
        with ExitStack() as pes:
            P.pes = pes
            Bf = PostBufs(P, TT, 32)
            g_ts = [P.sb("g_t%d" % i, [128, 16], F32) for i in range(3)]
            for i in range(3):
                P.dma(SP, g_ts[i][:], gv.t[:, i, :], writes=[g_ts[i]])
            for ti in range(NT // TT):
                t0 = ti * TT
                post_block(P, Bf, 32, ogT, Wout_b, xT, g_ts[0], g_ts[1], g_ts[2], Wfi_b, Wfo_b, t0)
                P.dma(SP, x2T.t[:, t0:t0 + TT].rearrange("(c p) t -> p c t", p=128), Bf.xs[:], reads=[Bf.xs], writes=[x2T])
            P.flush()
        P.pes = None
        with ExitStack() as pes:
            P.pes = pes
            Bf = MlaBufs(P, TT)
            g_t3 = P.sb("g_t3", [128, 16], F32)
            gq_t = P.sb("gq_t", [128, 6], F32)
            gkv_t = P.sb("gkv_t", [128, 4], F32)
            e64f = P.sb("e64f", [128, 65], F32)
            e64 = P.sb("e64", [128, 65], BF16)
            wv = P.sb("wv", [128, 4, 2048], BF16)
            P.dma(SP, g_t3[:], gv.t[:, 3, :], writes=[g_t3])
            P.dma(SP, gq_t[:], gq.t, writes=[gq_t])
            P.dma(SP, gkv_t[:], gkv.t, writes=[gkv_t])
            P.dma(SP, e64f[:], E64.t, writes=[e64f])
            P.op(DVE, lambda e: e.tensor_copy(out=e64[:], in_=e64f[:]), reads=[e64f], writes=[e64])
            P.dma(SP, wv[:], Wv_b.t, reads=[Wv_b], writes=[wv])
            cq = P.sb("cq", [128, 6, TT], F32)
            ckv = P.sb("ckv", [128, 4, TT], F32)
            cqb = P.sb("cqb", [128, 6, TT], BF16)
            ckvb = P.sb("ckvb", [128, 4, TT], BF16)
            rq = P.sb("rq", [128, TT], F32)
            rkv = P.sb("rkv", [128, TT], F32)
            rkc = P.sb("rkc", [128, 4], F32)
            c2 = P.sb("c2", [64, TT], F32)
            s2 = P.sb("s2", [64, TT], F32)
            c2r = P.sb("c2r", [64, TT], F32)
            s2r = P.sb("s2r", [64, TT], F32)
            Ak = P.sb("Ak", [64, TT], F32)
            Bk = P.sb("Bk", [64, TT], F32)
            krf = P.sb("krf", [64, TT], F32)
            krt = [P.sb("krt%d" % i, [65, TT], BF16) for i in range(2)]
            qrt = [P.sb("qrt%d" % i, [65, TT], BF16) for i in range(2)]
            qnt = [P.sb("qnt%d" % i, [128, TT], BF16) for i in range(2)]
            knall = P.sb("knall", [128, 16, TT], BF16)
            prn = P.sb("prn", [128, TT], BF16)
            prr = P.sb("prr", [64, TT], BF16)
            vt = [P.sb("vt%d" % i, [128, 2048], BF16) for i in range(2)]
            wa_t = [P.sb("wa_t%d" % i, [128, 16, 128], BF16) for i in range(2)]
            wq_t = [P.sb("wq_t%d" % i, [128, 6, 128], BF16) for i in range(3)]
            wk_t = [P.sb("wk_t%d" % i, [128, 4, 128], BF16) for i in range(2)]
            psc = P.ps("psc", [128, 512])
            p65 = P.ps("p65", [128, 512])
            for i in range(2):
                P.op(POOL, lambda e, i=i: e.memset(krt[i][:], 1.0), writes=[krt[i]])
            nsub = TT // 128
            for ti in range(NT // TT):
                t0 = ti * TT
                xs = Bf.xs
                P.dma(SP, xs[:], x2T.t[:, t0:t0 + TT].rearrange("(c p) t -> p c t", p=128), reads=[x2T], writes=[xs])
                P.dma(SP, c2[:], C2.t[:, t0:t0 + TT], writes=[c2])
                P.dma(SP, s2[:], S2.t[:, t0:t0 + TT], writes=[s2])
                P.op(ACT, lambda e: e.activation(out=Bf.sq[:], in_=xs[:], func=AF.Square), reads=[xs], writes=[Bf.sq])
                P.op(DVE, lambda e: e.tensor_tensor(out=Bf.hb[:], in0=xs[:], in1=g_t3[:].unsqueeze(2).to_broadcast([128, 16, TT]),
                                                    op=ALU.mult), reads=[xs, g_t3], writes=[Bf.hb])
                _rms_rstd(P, Bf, xs, 16, D_MODEL, Bf.rstd)
                for mc in range(12):
                    w = wa_t[mc % 2]
                    P.dma(SP, w[:], Wa_b.t[mc], reads=[Wa_b], writes=[w])
                    p = Bf.nextp()
                    _mm_group(P, p, p[:, 0:TT], [(w[:, k, :], Bf.hb[:, k, :]) for k in range(16)], [w, Bf.hb])
                    if mc < 6:
                        P.op(DVE, lambda e, p=p, mc=mc: e.tensor_tensor(out=cq[:, mc, :], in0=p[:, 0:TT], in1=Bf.rstd[:], op=ALU.mult),
                             reads=[p, Bf.rstd], writes=[cq])
                    elif mc < 10:
                        P.op(DVE, lambda e, p=p, mc=mc: e.tensor_tensor(out=ckv[:, mc - 6, :], in0=p[:, 0:TT], in1=Bf.rstd[:], op=ALU.mult),
                             reads=[p, Bf.rstd], writes=[ckv])
                    else:
                        dst = Ak if mc == 10 else Bk
                        P.op(DVE, lambda e, p=p, dst=dst: e.tensor_tensor(out=dst[:], in0=p[0:64, 0:TT], in1=Bf.rstd[0:64, :], op=ALU.mult),
                             reads=[p, Bf.rstd], writes=[dst])
                kr = krt[ti % 2]
                P.op(DVE, lambda e: e.tensor_tensor(out=Ak[:], in0=Ak[:], in1=c2[:], op=ALU.mult), reads=[Ak, c2], writes=[Ak])
                P.op(DVE, lambda e: e.tensor_tensor(out=Bk[:], in0=Bk[:], in1=s2[:], op=ALU.mult), reads=[Bk, s2], writes=[Bk])
                P.op(DVE, lambda e: e.tensor_tensor(out=krf[:], in0=Ak[:], in1=Bk[:], op=ALU.add), reads=[Ak, Bk], writes=[krf])
                P.op(ACT, lambda e, kr=kr: e.activation(out=kr[0:64, :], in_=krf[:], func=AF.Copy), reads=[krf], writes=[kr])
                P.dma(SP, KR.t[:, t0:t0 + TT], kr[:], reads=[kr], writes=[KR])
                P.op(POOL, lambda e: e.tensor_tensor(out=Bf.sq[:, 0:6, :], in0=cq[:], in1=cq[:], op=ALU.mult), reads=[cq], writes=[Bf.sq])
                _rms_rstd(P, Bf, cq, 6, 768, rq)
                P.op(DVE, lambda e: e.tensor_tensor(out=cqb[:], in0=cq[:], in1=gq_t[:].unsqueeze(2).to_broadcast([128, 6, TT]), op=ALU.mult),
                     reads=[cq, gq_t], writes=[cqb])
                P.op(POOL, lambda e: e.tensor_tensor(out=Bf.sq[:, 0:4, :], in0=ckv[:], in1=ckv[:], op=ALU.mult), reads=[ckv], writes=[Bf.sq])
                _rms_rstd(P, Bf, ckv, 4, 512, rkv)

                def colsum(e):
                    for sub in range(nsub):
                        for k in range(4):
                            ins = e.matmul(psc[:, sub:sub + 1], Bf.sq[:, k, sub * 128:(sub + 1) * 128], Bf.ones[:, 0:1],
                                           start=(k == 0), stop=(k == 3))
                    return ins
                P.op(PE, colsum, reads=[Bf.sq, Bf.ones], writes=[psc])
                P.op(ACT, lambda e: e.activation(out=rkc[:, 0:nsub], in_=psc[:, 0:nsub], func=AF.Ln, bias=RMS_EPS, scale=1.0 / 512),
                     reads=[psc], writes=[rkc])
                P.op(ACT, lambda e: e.activation(out=rkc[:, 0:nsub], in_=rkc[:, 0:nsub], func=AF.Exp, scale=-0.5), reads=[rkc], writes=[rkc])
                P.op(DVE, lambda e: e.tensor_tensor(out=ckvb[:], in0=ckv[:], in1=gkv_t[:].unsqueeze(2).to_broadcast([128, 4, TT]), op=ALU.mult),
                     reads=[ckv, gkv_t], writes=[ckvb])
                P.op(DVE, lambda e: e.tensor_tensor(out=c2r[:], in0=c2[:], in1=rq[0:64, :], op=ALU.mult), reads=[c2, rq], writes=[c2r])
                P.op(DVE, lambda e: e.tensor_tensor(out=s2r[:], in0=s2[:], in1=rq[0:64, :], op=ALU.mult), reads=[s2, rq], writes=[s2r])
                for h in range(16):
                    w = wk_t[h % 2]
                    P.dma(SP, w[:], Wk_b.t[h], reads=[Wk_b], writes=[w])
                    p = Bf.nextp()
                    _mm_group(P, p, p[:, 0:TT], [(w[:, k, :], ckvb[:, k, :]) for k in range(4)], [w, ckvb])
                    P.op(DVE, lambda e, p=p, h=h: e.tensor_tensor(out=knall[:, h, :], in0=p[:, 0:TT], in1=rkv[:], op=ALU.mult),
                         reads=[p, rkv], writes=[knall])
                P.dma(SP, KN.t[:, :, t0:t0 + TT].rearrange("h p t -> p h t"), knall[:], reads=[knall], writes=[KN])
                for sub in range(nsub):
                    v = vt[sub % 2]
                    for g4 in range(4):
                        p = Bf.nextp()
                        _mm_group(P, p, p[:, 0:512], [(ckvb[:, k, sub * 128:(sub + 1) * 128], wv[:, k, g4 * 512:(g4 + 1) * 512])
                                                     for k in range(4)], [ckvb, wv])
                        P.op(ACT, lambda e, p=p, v=v, g4=g4, sub=sub: e.activation(out=v[:, g4 * 512:(g4 + 1) * 512], in_=p[:, 0:512],
                                                                                    func=AF.Copy, scale=rkc[:, sub:sub + 1]),
                             reads=[p, rkc], writes=[v])
                    P.dma(SP, V.t[t0 + sub * 128:t0 + (sub + 1) * 128, :], v[:], reads=[v], writes=[V])
                for h in range(16):
                    qn = qnt[h % 2]
                    qr = qrt[h % 2]
                    w0, w1, w2 = wq_t[0], wq_t[1], wq_t[2]
                    P.dma(SP, w0[:], Wq_b.t[3 * h], reads=[Wq_b], writes=[w0])
                    P.dma(SP, w1[:], Wq_b.t[3 * h + 1], reads=[Wq_b], writes=[w1])
                    P.dma(SP, w2[:], Wq_b.t[3 * h + 2], reads=[Wq_b], writes=[w2])
                    p = Bf.nextp()
                    _mm_group(P, p, p[:, 0:TT], [(w0[:, k, :], cqb[:, k, :]) for k in range(6)], [w0, cqb])
                    P.op(DVE, lambda e, p=p, qn=qn: e.tensor_tensor(out=qn[:], in0=p[:, 0:TT], in1=rq[:], op=ALU.mult),
                         reads=[p, rq], writes=[qn])
                    P.dma(SP, QN.t[h, :, t0:t0 + TT], qn[:], reads=[qn], writes=[QN])
                    pa = Bf.nextp()
                    _mm_group(P, pa, pa[0:64, 0:TT], [(w1[:, k, 0:64], cqb[:, k, :]) for k in range(6)], [w1, cqb])
                    P.op(DVE, lambda e, pa=pa: e.tensor_tensor(out=Ak[:], in0=pa[0:64, 0:TT], in1=c2r[:], op=ALU.mult),
                         reads=[pa, c2r], writes=[Ak])
                    pb = Bf.nextp()
                    _mm_group(P, pb, pb[0:64, 0:TT], [(w2[:, k, 0:64], cqb[:, k, :]) for k in range(6)], [w2, cqb])
                    P.op(DVE, lambda e, pb=pb: e.tensor_tensor(out=Bk[:], in0=pb[0:64, 0:TT], in1=s2r[:], op=ALU.mult),
                         reads=[pb, s2r], writes=[Bk])
                    P.op(DVE, lambda e, qr=qr: e.tensor_tensor(out=qr[0:64, :], in0=Ak[:], in1=Bk[:], op=ALU.add),
                         reads=[Ak, Bk], writes=[qr])
                    P.op(POOL, lambda e, qn=qn, h=h: e.tensor_tensor(out=prn[:], in0=qn[:], in1=knall[:, h, :], op=ALU.mult),
                         reads=[qn, knall], writes=[prn])
                    P.op(POOL, lambda e, qr=qr, kr=kr: e.tensor_tensor(out=prr[:], in0=qr[0:64, :], in1=kr[0:64, :], op=ALU.mult),
                         reads=[qr, kr], writes=[prr])
                    _mm_group(P, p65, p65[0:65, 0:TT], [(e64[:, :], prn[:]), (e64[0:64, :], prr[:])], [e64, prn, prr])
                    P.op(ACT, lambda e, qr=qr: e.activation(out=qr[64:65, :], in_=p65[64:65, 0:TT], func=AF.Copy, scale=-1.0),
                         reads=[p65], writes=[qr])
                    P.dma(SP, QR.t[h, :, t0:t0 + TT], qr[:], reads=[qr], writes=[QR])
            P.flush()
        P.pes = None
    return nc


MLA_HEADS = 16
ROPE_THETA = 10000.0


def _core_tokens(seq_lens, c):
    offs = np.concatenate([[0], np.cumsum(seq_lens)]).astype(int)
    idx, pos = [], []
    for s, L in enumerate(seq_lens):
        n = L // NCORE
        p = np.arange(c * n, (c + 1) * n)
        idx.append(offs[s] + p)
        pos.append(p)
    return np.concatenate(idx), np.concatenate(pos)


def _rope_tabs(pos):
    half = 32
    inv_freq = (np.float32(ROPE_THETA) ** (-(np.arange(half, dtype=np.float32) / np.float32(half)))).astype(np.float32)
    ang = (pos.astype(np.float32)[:, None] * inv_freq[None, :]).astype(np.float32)
    cos = np.cos(ang.astype(np.float64)).astype(np.float32)
    sin = np.sin(ang.astype(np.float64)).astype(np.float32)
    C2 = np.ascontiguousarray(np.concatenate([cos, cos], 1).T)
    S2 = np.ascontiguousarray(np.concatenate([-sin, sin], 1).T)
    return C2, S2


def prep_B_weights(norm_mix_post, norm_ffn_pre, norm_ffn_post, norm_mix_pre, gdn_w_out, ffn_w_in, ffn_w_out,
                   mla_w_a, mla_q_a_norm, mla_w_q_b, mla_kv_a_norm, mla_w_kv_b):
    wa = mla_w_a[0]
    rope = wa[:, 1280:1344]
    sw = np.concatenate([rope[:, 32:], rope[:, :32]], 1)
    z64 = np.zeros((2048, 64), np.float32)
    wa_p = np.concatenate([wa[:, :1280], rope, z64, sw, z64], 1)
    wq = mla_w_q_b[0]
    z = np.zeros((768, 64), np.float32)
    cols = []
    for h in range(16):
        b = h * 192
        r = wq[:, b + 128:b + 192]
        cols += [wq[:, b:b + 128], r, z, np.concatenate([r[:, 32:], r[:, :32]], 1), z]
    wq_p = np.concatenate(cols, 1)
    wkv = mla_w_kv_b[0].reshape(512, 16, 256)
    wk = np.ascontiguousarray(wkv[:, :, :128].reshape(512, 2048))
    wv = np.ascontiguousarray(wkv[:, :, 128:].reshape(512, 2048))
    e64 = np.zeros((128, 65), np.float32)
    e64[:, 64] = 1.0
    return {
        "Wout": lay_w(gdn_w_out[0]), "Wfi": lay_w(ffn_w_in[0]), "Wfo": lay_w(ffn_w_out[0]),
        "Wa": lay_w(wa_p), "Wq": lay_w(wq_p), "Wk": lay_w(wk),
        "Wv": np.ascontiguousarray(wv.reshape(4, 128, 2048).transpose(1, 0, 2)),
        "gv": np.ascontiguousarray(np.stack([lay_g(norm_mix_post[0]), lay_g(norm_ffn_pre[0]), lay_g(norm_ffn_post[0]),
                                            lay_g(norm_mix_pre[1])], 1)),
        "gq": lay_g(mla_q_a_norm[0]), "gkv": lay_g(mla_kv_a_norm[0]), "E64": e64,
    }


def build_C(slabs, TQ=512, TT=256):
    NT = sum(slabs)
    NTOK = NCORE * NT
    scale = float(192 ** -0.5)
    nc = bass.Bass("TRN2", target_bir_lowering=False)
    with ExitStack() as es:
        P = Prog(nc, es)
        QN = P.dram("QN", [16, 128, NT], BF16, kind="ExternalInput")
        QR = P.dram("QR", [16, 65, NT], BF16, kind="ExternalInput")
        KNa = P.dram("KNa", [16, 128, NTOK], BF16, kind="ExternalInput")
        KRa = P.dram("KRa", [65, NTOK], BF16, kind="ExternalInput")
        Va = P.dram("Va", [NTOK, 2048], BF16, kind="ExternalInput")
        x2T = P.dram("x2T", [D_MODEL, NT], F32, kind="ExternalInput")
        Wo = P.dram("Wo", [16, 128, 16, 128], F32, kind="ExternalInput")
        Wfi = P.dram("Wfi", [64, 128, 16, 128], F32, kind="ExternalInput")
        Wfo = P.dram("Wfo", [16, 128, 64, 128], F32, kind="ExternalInput")
        gv = P.dram("gv", [128, 3, 16], F32, kind="ExternalInput")
        yT = P.dram("yT", [D_MODEL, NT], F32, kind="ExternalOutput")
        aoT = P.dram("aoT", [D_MODEL, NT], BF16)
        Wo_b = P.dram("Wo_b", [16, 128, 16, 128], BF16)
        Wfi_b = P.dram("Wfi_b", [64, 128, 16, 128], BF16)
        Wfo_b = P.dram("Wfo_b", [16, 128, 64, 128], BF16)
        cast_w(P, Wo_b, Wo, 2)
        cast_w(P, Wfi_b, Wfi, 8)
        cast_w(P, Wfo_b, Wfo, 8)
        SEG = max(slabs)
        with ExitStack() as pes:
            P.pes = pes
            ones = P.sb("ones", [128, 128], BF16)
            P.op(POOL, lambda e: e.memset(ones[:], 1.0), writes=[ones])
            qn = [P.sb("qn%d" % i, [128, TQ], BF16) for i in range(2)]
            qr = [P.sb("qr%d" % i, [65, TQ], BF16) for i in range(2)]
            kn = [P.sb("kn%d" % i, [128, SEG], BF16) for i in range(2)]
            kr = [P.sb("kr%d" % i, [65, SEG], BF16) for i in range(2)]
            vv = [P.sb("vv%d" % i, [128, SEG // 128, 128], BF16) for i in range(2)]
            pt = [P.sb("pt%d" % i, [128, TQ], BF16) for i in range(3)]
            rinv = P.sb("rinv", [128, TQ], F32)
            ao = [P.sb("ao%d" % i, [128, TQ], BF16) for i in range(2)]
            ps_s = [P.ps("ps_s%d" % i, [128, 512]) for i in range(3)]
            ps_o = [P.ps("ps_o%d" % i, [128, 512]) for i in range(2)]
            ps_r = [P.ps("ps_r%d" % i, [128, 512]) for i in range(2)]
            soff = 0
            cnt = 0
            hc = 0
            sc = 0
            for slab in slabs:
                nkb = slab // 128
                for qt in range(slab // TQ):
                    q0 = soff + qt * TQ
                    for h in range(16):
                        qn_, qr_ = qn[hc % 2], qr[hc % 2]
                        po, pr = ps_o[hc % 2], ps_r[hc % 2]
                        ao_ = ao[hc % 2]
                        hc += 1
                        P.dma(SP, qn_[:], QN.t[h, :, q0:q0 + TQ], reads=[QN], writes=[qn_])
                        P.dma(SP, qr_[:], QR.t[h, :, q0:q0 + TQ], reads=[QR], writes=[qr_])
                        nblk = NCORE * nkb
                        bi = 0
                        for r in range(NCORE):
                            k0 = r * NT + soff
                            kn_, kr_, v_ = kn[sc % 2], kr[sc % 2], vv[sc % 2]
                            sc += 1
                            P.dma(SP, kn_[:, 0:slab], KNa.t[h, :, k0:k0 + slab], reads=[KNa], writes=[kn_])
                            P.dma(SP, kr_[:, 0:slab], KRa.t[:, k0:k0 + slab], reads=[KRa], writes=[kr_])
                            P.dma(SP, v_[:, 0:nkb, :], Va.t[k0:k0 + slab, h * 128:(h + 1) * 128].rearrange("(b p) e -> p b e", p=128),
                                  reads=[Va], writes=[v_])
                            for kb in range(nkb):
                                ps = ps_s[cnt % 3]
                                pt_ = pt[cnt % 3]
                                cnt += 1
                                ksl = slice(kb * 128, (kb + 1) * 128)
                                _mm_group(P, ps, ps[:, 0:TQ], [(kn_[:, ksl], qn_[:]), (kr_[:, ksl], qr_[:])], [kn_, kr_, qn_, qr_])
                                P.op(ACT, lambda e, ps=ps, pt_=pt_: e.activation(out=pt_[:], in_=ps[:, 0:TQ], func=AF.Exp, scale=scale),
                                     reads=[ps], writes=[pt_])
                                f, l = (bi == 0), (bi == nblk - 1)

                                def acc(e, v_=v_, kb=kb, pt_=pt_, po=po, pr=pr, f=f, l=l):
                                    e.matmul(po[:, 0:TQ], v_[:, kb, :], pt_[:], start=f, stop=l)
                                    return e.matmul(pr[:, 0:TQ], ones[:], pt_[:], start=f, stop=l)
                                P.op(PE, acc, reads=[v_, pt_, ones], writes=[po, pr])
                                bi += 1
                        P.op(DVE, lambda e, pr=pr: e.reciprocal(out=rinv[:], in_=pr[:, 0:TQ]), reads=[pr], writes=[rinv])
                        P.op(DVE, lambda e, po=po, ao_=ao_: e.tensor_tensor(out=ao_[:], in0=po[:, 0:TQ], in1=rinv[:], op=ALU.mult),
                             reads=[po, rinv], writes=[ao_])
                        P.dma(SP, aoT.t[h * 128:(h + 1) * 128, q0:q0 + TQ], ao_[:], reads=[ao_], writes=[aoT])
                soff += slab
            P.flush()
        P.pes = None
        with ExitStack() as pes:
            P.pes = pes
            Bf = PostBufs(P, TT, 16)
            g_ts = [P.sb("g_t%d" % i, [128, 16], F32) for i in range(3)]
            for i in range(3):
                P.dma(SP, g_ts[i][:], gv.t[:, i, :], writes=[g_ts[i]])
            for ti in range(NT // TT):
                t0 = ti * TT
                post_block(P, Bf, 16, aoT, Wo_b, x2T, g_ts[0], g_ts[1], g_ts[2], Wfi_b, Wfo_b, t0)
                P.dma(SP, yT.t[:, t0:t0 + TT].rearrange("(c p) t -> p c t", p=128), Bf.xs[:], reads=[Bf.xs], writes=[yT])
            P.flush()
        P.pes = None
    return nc


def run_model(seq_lens, xs, norm_mix_pre, norm_mix_post, norm_ffn_pre, norm_ffn_post,
              gdn_w_in, gdn_conv_w, gdn_a_log, gdn_dt_bias, gdn_norm_w, gdn_w_out,
              mla_w_a, mla_q_a_norm, mla_w_q_b, mla_kv_a_norm, mla_w_kv_b, mla_w_o,
              ffn_w_in, ffn_w_out):
    cores = list(range(NCORE))
    xT_all = np.ascontiguousarray(xs.T)
    og = run_A(seq_lens, xT_all, norm_mix_pre, gdn_w_in, gdn_conv_w, gdn_a_log, gdn_dt_bias, gdn_norm_w)
    slabs = [L // NCORE for L in seq_lens]
    NT = sum(slabs)
    WB = prep_B_weights(norm_mix_post, norm_ffn_pre, norm_ffn_post, norm_mix_pre, gdn_w_out, ffn_w_in, ffn_w_out,
                        mla_w_a, mla_q_a_norm, mla_w_q_b, mla_kv_a_norm, mla_w_kv_b)
    toks = [_core_tokens(seq_lens, c) for c in cores]
    in_maps = []
    for c in cores:
        idx, pos = toks[c]
        C2, S2 = _rope_tabs(pos)
        m = dict(WB)
        m.update({"xT": np.ascontiguousarray(xs[idx].T), "ogT": np.ascontiguousarray(og[idx].T), "C2": C2, "S2": S2})
        in_maps.append(m)
    ncB = build_B(NT)
    resB = run_bass_kernel_spmd(ncB, in_maps, core_ids=cores).results
    del in_maps
    KNa = np.ascontiguousarray(np.concatenate([np.asarray(resB[c]["KN"]) for c in cores], axis=2))
    KRa = np.ascontiguousarray(np.concatenate([np.asarray(resB[c]["KR"]) for c in cores], axis=1))
    Va = np.ascontiguousarray(np.concatenate([np.asarray(resB[c]["V"]) for c in cores], axis=0))
    WC = {
        "Wo": lay_w(mla_w_o[0]), "Wfi": lay_w(ffn_w_in[1]), "Wfo": lay_w(ffn_w_out[1]),
        "gv": np.ascontiguousarray(np.stack([lay_g(norm_mix_post[1]), lay_g(norm_ffn_pre[1]), lay_g(norm_ffn_post[1])], 1)),
        "KNa": KNa, "KRa": KRa, "Va": Va,
    }
    in_maps = []
    for c in cores:
        m = dict(WC)
        m.update({"QN": np.asarray(resB[c]["QN"]), "QR": np.asarray(resB[c]["QR"]), "x2T": np.asarray(resB[c]["x2T"])})
        in_maps.append(m)
    ncC = build_C(slabs)
    resC = run_bass_kernel_spmd(ncC, in_maps, core_ids=cores).results
    y = np.empty((sum(seq_lens), D_MODEL), np.float32)
    for c in cores:
        y[toks[c][0]] = np.asarray(resC[c]["yT"]).T
    return y


def kernel(x_prompt, x_sample, norm_mix_pre, norm_mix_post, norm_ffn_pre, norm_ffn_post,
           gdn_w_in, gdn_conv_w, gdn_a_log, gdn_dt_bias, gdn_norm_w, gdn_w_out,
           mla_w_a, mla_q_a_norm, mla_w_q_b, mla_kv_a_norm, mla_w_kv_b, mla_w_o,
           ffn_w_in, ffn_w_out):
    a = [np.asarray(v, dtype=np.float32) for v in (
        x_prompt, x_sample, norm_mix_pre, norm_mix_post, norm_ffn_pre, norm_ffn_post,
        gdn_w_in, gdn_conv_w, gdn_a_log, gdn_dt_bias, gdn_norm_w, gdn_w_out,
        mla_w_a, mla_q_a_norm, mla_w_q_b, mla_kv_a_norm, mla_w_kv_b, mla_w_o, ffn_w_in, ffn_w_out)]
    xp, xsm = a[0], a[1]
    Bp, Lp, _ = xp.shape
    Bs, Ls, _ = xsm.shape
    seq_lens = [Lp] * Bp + [Ls] * Bs
    xs = np.concatenate([xp.reshape(Bp * Lp, D_MODEL), xsm.reshape(Bs * Ls, D_MODEL)], axis=0)
    y = run_model(seq_lens, xs, *a[2:])
    yp = y[:Bp * Lp].reshape(Bp, Lp, D_MODEL)
    ysm = y[Bp * Lp:].reshape(Bs, Ls, D_MODEL)
    return (np.ascontiguousarray(yp), np.ascontiguousarray(ysm))
```

```python
import numpy as np
import ml_dtypes
from contextlib import ExitStack
import concourse.bass as bass
import concourse.mybir as mybir
from concourse.bass_utils import run_bass_kernel_spmd

F32 = mybir.dt.float32
BF16 = mybir.dt.bfloat16
AF = mybir.ActivationFunctionType
ALU = mybir.AluOpType
NPBF = ml_dtypes.bfloat16

PE, ACT, DVE, POOL, SP = "pe", "act", "dve", "pool", "sp"
ENGS = (PE, ACT, DVE, POOL, SP)

D_MODEL = 2048
NCORE = 8
RMS_EPS = 1e-6
L2_EPS = 1e-6


class T:
    __slots__ = ("t", "w", "r", "name", "psum")

    def __init__(self, t, name="", psum=False):
        self.t = t
        self.w = {}
        self.r = {}
        self.name = name
        self.psum = psum

    def __getitem__(self, k):
        return self.t[k]


class Prog:
    NSLOT = 6

    def __init__(self, nc, es):
        self.nc = nc
        self.es = es
        self.ops = {e: [] for e in ENGS}
        self.cnt = {e: 0 for e in ENGS}
        self.sems = []
        self.esem = {}
        for e in ENGS:
            self.esem[e] = self._newsem("p_" + e)
        self.slots = {}
        self.slot_cnt = {}
        self.slot_next = {}
        for q in (SP, ACT, POOL):
            self.slots[q] = [self._newsem("d_%s%d" % (q, i)) for i in range(self.NSLOT)]
            self.slot_cnt[q] = [0] * self.NSLOT
            self.slot_next[q] = 0
        self.waited = {e: {} for e in ENGS}
        self.ninst = 0
        self.pes = None
        self.uid = 0

    def _newsem(self, name):
        h = self.es.enter_context(self.nc.semaphore(name))
        self.sems.append(h)
        return len(self.sems) - 1

    def sb(self, name, shape, dt):
        es = self.pes if self.pes is not None else self.es
        self.uid += 1
        name = "%s_%d" % (name, self.uid)
        return T(es.enter_context(self.nc.sbuf_tensor(name, list(shape), dt)), name)

    def ps(self, name, shape, dt=F32):
        es = self.pes if self.pes is not None else self.es
        self.uid += 1
        name = "%s_%d" % (name, self.uid)
        return T(es.enter_context(self.nc.psum_tensor(name, list(shape), dt)), name, psum=True)

    def dram(self, name, shape, dt, kind="Internal"):
        return T(self.nc.dram_tensor(name, list(shape), dt, kind=kind).ap(), name)

    def _deps(self, eng, reads, writes, is_dma):
        deps = {}
        own = None if is_dma else self.esem[eng]

        def add(d, skip_same):
            for s, v in d.items():
                if skip_same and s == own:
                    continue
                if deps.get(s, 0) < v:
                    deps[s] = v
        for t in reads:
            add(t.w, eng == PE)
            if t.psum:
                add(t.r, True)
        for t in writes:
            add(t.w, True)
            add(t.r, True)
        return deps

    def _waits(self, eng, deps):
        wl = []
        wd = self.waited[eng]
        for s, v in deps.items():
            if wd.get(s, 0) < v:
                wd[s] = v
                wl.append((s, v))
        return wl

    def _mark(self, ev, reads, writes):
        for t in reads:
            if t.r.get(ev[0], 0) < ev[1]:
                t.r[ev[0]] = ev[1]
        for t in writes:
            t.w = {ev[0]: ev[1]}
            t.r = {}

    def op(self, eng, fn, reads=(), writes=()):
        wl = self._waits(eng, self._deps(eng, reads, writes, False))
        self.cnt[eng] += 1
        ev = (self.esem[eng], self.cnt[eng])
        self.ops[eng].append((wl, fn, self.esem[eng], 1))
        self._mark(ev, reads, writes)
        return ev

    def dma(self, q, out, in_, reads=(), writes=(), **kw):
        k = self.slot_next[q]
        self.slot_next[q] = (k + 1) % self.NSLOT
        sem = self.slots[q][k]
        deps = self._deps(q, reads, writes, True)
        prev = 16 * self.slot_cnt[q][k]
        if prev > 0 and deps.get(sem, 0) < prev:
            deps[sem] = prev
        wl = self._waits(q, deps)
        self.slot_cnt[q][k] += 1
        ev = (sem, 16 * self.slot_cnt[q][k])

        def fn(e, out=out, in_=in_, kw=kw):
            return e.dma_start(out=out, in_=in_, **kw)
        self.ops[q].append((wl, fn, sem, 16))
        self._mark(ev, reads, writes)
        return ev

    def _all_events(self):
        ev = {}
        for e in ENGS:
            if self.cnt[e] > 0:
                ev[self.esem[e]] = self.cnt[e]
        for q in (SP, ACT, POOL):
            for k in range(self.NSLOT):
                if self.slot_cnt[q][k] > 0:
                    ev[self.slots[q][k]] = 16 * self.slot_cnt[q][k]
        return ev

    def flush(self, final=False):
        allev = self._all_events()
        sems = self.sems
        ops = self.ops
        waited = self.waited

        def run(e, name):
            for wl, fn, sem, inc in ops[name]:
                for s, v in wl:
                    e.wait_ge(sems[s], v)
                fn(e).then_inc(sems[sem], inc)
                self.ninst += 1 + len(wl)
            own = self.esem[name]
            for s, v in allev.items():
                if s == own:
                    continue
                if waited[name].get(s, 0) < v:
                    waited[name][s] = v
                    e.wait_ge(sems[s], v)
        with self.nc.Block() as block:
            @block.sync
            def _(e):
                run(e, SP)

            @block.tensor
            def _(e):
                run(e, PE)

            @block.scalar
            def _(e):
                run(e, ACT)

            @block.vector
            def _(e):
                run(e, DVE)

            @block.gpsimd
            def _(e):
                run(e, POOL)
        self.ops = {e: [] for e in ENGS}


def _mm_group(P, out_t, out_ap, pairs, reads, first=True, last=True):
    def fn(e, pairs=pairs, out_ap=out_ap):
        n = len(pairs)
        for i, (l, r) in enumerate(pairs):
            ins = e.matmul(out_ap, l, r, start=(first and i == 0), stop=(last and i == n - 1))
        return ins
    return P.op(PE, fn, reads=reads, writes=[out_t])


CH = 128
NLVL = 7


def _gdn_consts():
    i = np.arange(CH)
    s = i[:, None]
    c = i[None, :]
    f32 = np.zeros((CH, 36, CH), np.float32)
    f32[:, 0, :] = (s <= c)
    f32[:, 1, :] = (s >= c)
    f32[:, 2, :] = -1.0 * (c > s)
    f32[:, 3, :] = -1.0 * (c < s)
    f32[:, 4, :] = (c >= s)
    f32[:, 5, :] = (c <= s)
    for l in range(NLVL):
        b = 1 << l
        r = i[:, None]
        q = i[None, :]
        same = (r // (2 * b)) == (q // (2 * b))
        low = same & ((r // b) % 2 == 1) & ((q // b) % 2 == 0)
        f32[:, 6 + l, :] = low
        f32[:, 6 + NLVL + l, :] = low.T
    f32[:, 20, :] = np.eye(CH)
    f32[:, 21, :] = 1.0
    return f32


def build_A(seq_lens, TT=512, upto=9, debug=False):
    NTOK = sum(seq_lens)
    offs = np.concatenate([[0], np.cumsum(seq_lens)]).astype(int)
    nc = bass.Bass("TRN2", target_bir_lowering=False)
    with ExitStack() as es:
        P = Prog(nc, es)
        xT = P.dram("xT", [D_MODEL, NTOK], F32, kind="ExternalInput")
        gpre = P.dram("gpre", [128, 16], F32, kind="ExternalInput")
        Wqkv = P.dram("Wqkv", [D_MODEL, 1024], F32, kind="ExternalInput")
        Wz = P.dram("Wz", [D_MODEL, 512], F32, kind="ExternalInput")
        Wg = P.dram("Wg", [D_MODEL, 16], F32, kind="ExternalInput")
        convw = P.dram("convw", [128, 8, 5], F32, kind="ExternalInput")
        alog = P.dram("alog", [128, 8], F32, kind="ExternalInput")
        dtb = P.dram("dtb", [128, 8], F32, kind="ExternalInput")
        gnw = P.dram("gnw", [128, 128], F32, kind="ExternalInput")
        cst = P.dram("cst", [128, 36, 128], F32, kind="ExternalInput")
        og = P.dram("og", [NTOK, 512], BF16, kind="ExternalOutput")
        Wqkv_b = P.dram("Wqkv_b", [D_MODEL, 1024], BF16)
        Wz_b = P.dram("Wz_b", [D_MODEL, 512], BF16)
        Wg_b = P.dram("Wg_b", [D_MODEL, 16], BF16)
        dk = "ExternalOutput" if debug else "Internal"
        rawT = P.dram("rawT", [8, 128, NTOK], F32, kind=dk)
        zsil = P.dram("zsil", [NTOK, 512], F32, kind=dk)
        gates = P.dram("gates", [NTOK, 16], F32, kind=dk)
        qkT = P.dram("qkT", [4, 128, NTOK], BF16, kind=dk)
        ktok = P.dram("ktok", [NTOK, 2, 128], BF16, kind=dk)
        vtok = P.dram("vtok", [NTOK, 4, 128], BF16, kind=dk)
        osc = P.dram("osc", [2, NTOK, 512], F32, kind=dk)

        P.dma(POOL, Wqkv_b.t, Wqkv.t, reads=[Wqkv], writes=[Wqkv_b])
        P.dma(POOL, Wz_b.t, Wz.t, reads=[Wz], writes=[Wz_b])
        P.dma(POOL, Wg_b.t, Wg.t, reads=[Wg], writes=[Wg_b])

        with ExitStack() as pes:
            P.pes = pes
            wqkv = P.sb("wqkv", [128, 16, 1024], BF16)
            wz = P.sb("wz", [128, 16, 512], BF16)
            wg = P.sb("wg", [128, 16, 16], BF16)
            gt = P.sb("gt", [128, 16], F32)
            ones = P.sb("ones", [128, 128], BF16)
            alog_s = P.sb("alog_s", [128, 8], F32)
            nega = P.sb("nega", [128, 8], F32)
            dtb_s = P.sb("dtb_s", [128, 8], F32)
            xs = [P.sb("xs%d" % i, [128, 16, TT], F32) for i in range(2)]
            sq = P.sb("sq", [128, 16, TT], BF16)
            hb = P.sb("hb", [128, 16, TT], BF16)
            rstd = P.sb("rstd", [128, TT], F32)
            rcol = P.sb("rcol", [128, 4], F32)
            rawo = [P.sb("rawo%d" % i, [128, TT], F32) for i in range(2)]
            zo = [P.sb("zo%d" % i, [128, 512], F32) for i in range(2)]
            gsb = [P.sb("gsb%d" % i, [128, 16], F32) for i in range(2)]
            gtmp = P.sb("gtmp", [128, 8], F32)
            pss = P.ps("pss", [128, 512])
            psc = P.ps("psc", [128, 512])
            pm = [P.ps("pm%d" % i, [128, 512]) for i in range(4)]
            psg = P.ps("psg", [128, 512])

            P.dma(SP, wqkv[:], Wqkv_b.t.rearrange("(c p) m -> p c m", p=128), reads=[Wqkv_b], writes=[wqkv])
            P.dma(SP, wz[:], Wz_b.t.rearrange("(c p) m -> p c m", p=128), reads=[Wz_b], writes=[wz])
            P.dma(SP, wg[:], Wg_b.t.rearrange("(c p) m -> p c m", p=128), reads=[Wg_b], writes=[wg])
            P.dma(SP, gt[:], gpre.t, writes=[gt])
            P.dma(SP, alog_s[:], alog.t, writes=[alog_s])
            P.dma(SP, dtb_s[:], dtb.t, writes=[dtb_s])
            P.op(POOL, lambda e: e.memset(ones[:], 1.0), writes=[ones])
            P.op(ACT, lambda e: e.activation(out=nega[:], in_=alog_s[:], func=AF.Exp), reads=[alog_s], writes=[nega])
            P.op(DVE, lambda e: e.tensor_scalar(out=nega[:], in0=nega[:], scalar1=-1.0, scalar2=None, op0=ALU.mult),
                 reads=[nega], writes=[nega])

            ntile = NTOK // TT
            for ti in range(ntile):
                t0 = ti * TT
                x = xs[ti % 2]
                P.dma(SP, x[:], xT.t[:, t0:t0 + TT].rearrange("(c p) t -> p c t", p=128), writes=[x])
                P.op(ACT, lambda e, x=x: e.activation(out=sq[:], in_=x[:], func=AF.Square), reads=[x], writes=[sq])
                P.op(DVE, lambda e, x=x: e.tensor_tensor(out=hb[:], in0=x[:], in1=gt[:].unsqueeze(2).to_broadcast([128, 16, TT]),
                                                         op=ALU.mult), reads=[x, gt], writes=[hb])
                _mm_group(P, pss, pss[:, 0:TT], [(ones[:], sq[:, k, :]) for k in range(16)], [ones, sq])
                P.op(ACT, lambda e: e.activation(out=rstd[:], in_=pss[:, 0:TT], func=AF.Ln, bias=RMS_EPS, scale=1.0 / D_MODEL),
                     reads=[pss], writes=[rstd])
                P.op(ACT, lambda e: e.activation(out=rstd[:], in_=rstd[:], func=AF.Exp, scale=-0.5), reads=[rstd], writes=[rstd])
                nsub = TT // 128

                def colsum(e):
                    for sub in range(nsub):
                        for k in range(16):
                            ins = e.matmul(psc[:, sub:sub + 1], sq[:, k, sub * 128:(sub + 1) * 128], ones[:, 0:1],
                                           start=(k == 0), stop=(k == 15))
                    return ins
                P.op(PE, colsum, reads=[sq, ones], writes=[psc])
                P.op(ACT, lambda e: e.activation(out=rcol[:, 0:nsub], in_=psc[:, 0:nsub], func=AF.Ln, bias=RMS_EPS,
                                                 scale=1.0 / D_MODEL), reads=[psc], writes=[rcol])
                P.op(ACT, lambda e: e.activation(out=rcol[:, 0:nsub], in_=rcol[:, 0:nsub], func=AF.Exp, scale=-0.5),
                     reads=[rcol], writes=[rcol])
                for m in range(8):
                    p = pm[m % 4]
                    ro = rawo[m % 2]
                    _mm_group(P, p, p[:, 0:TT], [(wqkv[:, k, m * 128:(m + 1) * 128], hb[:, k, :]) for k in range(16)], [wqkv, hb])
                    P.op(DVE, lambda e, p=p, ro=ro: e.tensor_tensor(out=ro[:], in0=p[:, 0:TT], in1=rstd[:], op=ALU.mult),
                         reads=[p, rstd], writes=[ro])
                    P.dma(SP, rawT.t[m, :, t0:t0 + TT], ro[:], reads=[ro], writes=[rawT])
                for sub in range(nsub):
                    p = pm[sub % 4]
                    z = zo[sub % 2]
                    _mm_group(P, p, p[:, 0:512], [(hb[:, k, sub * 128:(sub + 1) * 128], wz[:, k, :]) for k in range(16)], [wz, hb])
                    P.op(ACT, lambda e, p=p, z=z, sub=sub: e.activation(out=z[:], in_=p[:, 0:512], func=AF.Silu,
                                                                        scale=rcol[:, sub:sub + 1]), reads=[p, rcol], writes=[z])
                    P.dma(SP, zsil.t[t0 + sub * 128:t0 + (sub + 1) * 128, :], z[:], reads=[z], writes=[zsil])
                    g = gsb[sub % 2]
                    _mm_group(P, psg, psg[:, 0:16], [(hb[:, k, sub * 128:(sub + 1) * 128], wg[:, k, :]) for k in range(16)], [wg, hb])
                    P.op(ACT, lambda e, g=g, sub=sub: e.activation(out=g[:, 0:8], in_=psg[:, 0:8], func=AF.Sigmoid,
                                                                   scale=rcol[:, sub:sub + 1]), reads=[psg, rcol], writes=[g])
                    P.op(DVE, lambda e, sub=sub: e.scalar_tensor_tensor(out=gtmp[:], in0=psg[:, 8:16], scalar=rcol[:, sub:sub + 1],
                                                                        in1=dtb_s[:], op0=ALU.mult, op1=ALU.add),
                         reads=[psg, rcol, dtb_s], writes=[gtmp])
                    P.op(ACT, lambda e: e.activation(out=gtmp[:], in_=gtmp[:], func=AF.Exp), reads=[gtmp], writes=[gtmp])
                    P.op(ACT, lambda e: e.activation(out=gtmp[:], in_=gtmp[:], func=AF.Ln, bias=1.0), reads=[gtmp], writes=[gtmp])
                    P.op(DVE, lambda e, g=g: e.tensor_tensor(out=g[:, 8:16], in0=gtmp[:], in1=nega[:], op=ALU.mult),
                         reads=[gtmp, nega], writes=[g])
                    P.dma(SP, gates.t[t0 + sub * 128:t0 + (sub + 1) * 128, :], g[:], reads=[g], writes=[gates])
            P.flush()
        P.pes = None

        if upto < 2:
            return nc
        with ExitStack() as pes:
            P.pes = pes
            cw = P.sb("cw", [128, 8, 5], F32)
            ones = P.sb("ones", [128, 128], BF16)
            identb = P.sb("identb", [128, 128], BF16)
            identf = P.sb("identf", [128, 128], F32)
            raw = [P.sb("raw%d" % i, [128, TT + 4], F32) for i in range(3)]
            acc = [P.sb("acc%d" % i, [128, TT], F32) for i in range(2)]
            sqb = P.sb("sqb", [128, TT], BF16)
            rn = P.sb("rn", [128, TT], F32)
            nb = [P.sb("nb%d" % i, [128, TT], BF16) for i in range(2)]
            tk = [P.sb("tk%d" % i, [128, TT // 128, 128], BF16) for i in range(2)]
            pss = P.ps("pss", [128, 512])
            ptr = [P.ps("ptr%d" % i, [128, 8, 128], BF16) for i in range(2)]
            P.dma(SP, cw[:], convw.t, writes=[cw])
            P.dma(SP, identf[:], cst.t[:, 20, :], writes=[identf])
            P.op(POOL, lambda e: e.memset(ones[:], 1.0), writes=[ones])
            P.op(DVE, lambda e: e.tensor_copy(out=identb[:], in_=identf[:]), reads=[identf], writes=[identb])
            nsub = TT // 128
            cnt = 0
            for si, L in enumerate(seq_lens):
                for ti in range(L // TT):
                    l0 = ti * TT
                    t0 = offs[si] + l0
                    for j in range(8):
                        r = raw[cnt % 3]
                        a = acc[cnt % 2]
                        o = nb[cnt % 2]
                        tkk = tk[cnt % 2]
                        pt = ptr[cnt % 2]
                        ceng = DVE
                        cnt += 1
                        lo = 2 if l0 == 0 else 0
                        hi = TT + 2 if l0 + TT == L else TT + 4
                        if lo or hi < TT + 4:
                            P.op(POOL, lambda e, r=r: e.memset(r[:], 0.0), writes=[r])
                        P.dma(SP, r[:, lo:hi], rawT.t[j, :, t0 - 2 + lo:t0 - 2 + hi], reads=[rawT], writes=[r])
                        P.op(ceng, lambda e, r=r, a=a, j=j: e.tensor_scalar(out=a[:], in0=r[:, 0:TT], scalar1=cw[:, j, 0:1],
                                                                            scalar2=None, op0=ALU.mult), reads=[r, cw], writes=[a])
                        for tap in range(1, 5):
                            P.op(ceng, lambda e, r=r, a=a, j=j, tap=tap: e.scalar_tensor_tensor(
                                out=a[:], in0=r[:, tap:tap + TT], scalar=cw[:, j, tap:tap + 1], in1=a[:],
                                op0=ALU.mult, op1=ALU.add), reads=[r, cw, a], writes=[a])
                        P.op(ACT, lambda e, a=a: e.activation(out=a[:], in_=a[:], func=AF.Silu), reads=[a], writes=[a])
                        if j < 4:
                            P.op(ACT, lambda e, a=a: e.activation(out=sqb[:], in_=a[:], func=AF.Square), reads=[a], writes=[sqb])
                            _mm_group(P, pss, pss[:, 0:TT], [(ones[:], sqb[:])], [ones, sqb])
                            P.op(ACT, lambda e: e.activation(out=rn[:], in_=pss[:, 0:TT], func=AF.Ln, bias=L2_EPS),
                                 reads=[pss], writes=[rn])
                            sc = -0.5
                            P.op(ACT, lambda e: e.activation(out=rn[:], in_=rn[:], func=AF.Exp, scale=-0.5), reads=[rn], writes=[rn])
                            if j < 2:
                                qs = float(128 ** -0.5)
                                P.op(DVE, lambda e, a=a, o=o, qs=qs: e.scalar_tensor_tensor(
                                    out=o[:], in0=a[:], scalar=qs, in1=rn[:], op0=ALU.mult, op1=ALU.mult),
                                    reads=[a, rn], writes=[o])
                            else:
                                P.op(DVE, lambda e, a=a, o=o: e.tensor_tensor(out=o[:], in0=a[:], in1=rn[:], op=ALU.mult),
                                     reads=[a, rn], writes=[o])
                            P.dma(SP, qkT.t[j, :, t0:t0 + TT], o[:], reads=[o], writes=[qkT])
                        else:
                            P.op(DVE, lambda e, a=a, o=o: e.tensor_copy(out=o[:], in_=a[:]), reads=[a], writes=[o])
                        if j >= 2:
                            def trs(e, o=o, pt=pt):
                                for sub in range(nsub):
                                    ins = e.transpose(pt[:, sub, :], o[:, sub * 128:(sub + 1) * 128], identb[:])
                                return ins
                            P.op(PE, trs, reads=[o, identb], writes=[pt])
                            P.op(ACT, lambda e, pt=pt, tkk=tkk: e.activation(out=tkk[:], in_=pt[:, 0:nsub, :], func=AF.Copy),
                                 reads=[pt], writes=[tkk])
                            if j < 4:
                                dst = ktok.t[t0:t0 + TT, j - 2, :]
                                dt_ = ktok
                            else:
                                dst = vtok.t[t0:t0 + TT, j - 4, :]
                                dt_ = vtok
                            P.dma(SP, dst.rearrange("(s p) d -> p s d", p=128), tkk[:], reads=[tkk], writes=[dt_])
            P.flush()
        P.pes = None

        if upto < 3:
            return nc
        with ExitStack() as pes:
            P.pes = pes
            cs = P.sb("cs", [128, 36, 128], F32)
            onesf = P.sb("onesf", [128, 128], F32)
            identb = P.sb("identb", [128, 128], BF16)
            P.dma(SP, cs[:], cst.t, writes=[cs])
            P.op(POOL, lambda e: e.memset(onesf[:], 1.0), writes=[onesf])
            P.op(DVE, lambda e: e.tensor_copy(out=identb[:], in_=cs[:, 20, :]), reads=[cs], writes=[identb])
            pkq = P.ps("pkq", [128, 4, 128])
            pg = P.ps("pg", [128, 4, 128])
            pc = P.ps("pc", [128, 512])
            pT_raw = P.ps("pT", [128, 2, 512], BF16)
            pTd = [pT_raw, pT_raw]
            NST = 3 + 3 * NLVL + 3
            HLAG = (NST + 1) // 2
            S = {}
            SD = {}
            for d in range(2):
                SD[d] = dict(St=P.sb("St%d" % d, [128, 4, 128], F32), Sb=P.sb("Sb%d" % d, [128, 4, 128], BF16),
                             Stmp=P.sb("Stmp%d" % d, [128, 4, 128], F32))
            for d in range(4):
                pbank = P.ps("pb%d" % d, [128, 4, 128])
                S[d] = dict(
                    pX=pbank, pY=pbank,
                    kT=P.sb("kT%d" % d, [128, 2, 128], BF16), qT=P.sb("qT%d" % d, [128, 2, 128], BF16),
                    kt=P.sb("kt%d" % d, [128, 2, 128], BF16), vt=P.sb("vt%d" % d, [128, 4, 128], BF16),
                    gt=P.sb("gt%d" % d, [128, 16], F32),
                    Ug=P.sb("Ug%d" % d, [128, 4, 128], F32), gcc=P.sb("gcc%d" % d, [128, 4], F32),
                    Dm=P.sb("Dm%d" % d, [128, 4, 128], F32), Dsn=P.sb("Dsn%d" % d, [128, 4, 128], F32),
                    Dt=P.sb("Dt%d" % d, [128, 4, 128], F32),
                    LpT=P.sb("LpT%d" % d, [128, 4, 128], BF16), QKm=P.sb("QKm%d" % d, [128, 4, 128], BF16),
                    Pm=P.sb("Pm%d" % d, [128, 4, 128], BF16), Qm=P.sb("Qm%d" % d, [128, 4, 128], BF16),
                    X=P.sb("X%d" % d, [128, 4, 128], BF16), tmp=P.sb("tmp%d" % d, [128, 4, 128], BF16),
                    egc=P.sb("egc%d" % d, [128, 4], F32), Rg=P.sb("Rg%d" % d, [128, 4, 128], BF16),
                    nwT=P.sb("nwT%d" % d, [128, 4, 128], BF16), gl=P.sb("gl%d" % d, [128, 4], F32),
                    kd=P.sb("kd%d" % d, [128, 4], F32), egl=P.sb("egl%d" % d, [128, 4], F32),
                    vnb=P.sb("vnb%d" % d, [128, 4, 128], BF16), vns=P.sb("vns%d" % d, [128, 4, 128], BF16),
                    Asb=P.sb("Asb%d" % d, [128, 4, 128], F32), osb=P.sb("osb%d" % d, [128, 4, 128], F32),
                )

            def bc_h(ap2):
                return ap2.unsqueeze(1).to_broadcast([128, 4, 128])

            def bc_c(ap2):
                return ap2.unsqueeze(2).to_broadcast([128, 4, 128])

            def unit_stages(d, slot, t0):
                B = dict(S[slot])
                B.update(SD[d])
                last = (CH - 1) if d == 0 else 0
                st = []

                def s_load():
                    P.dma(SP, B["kT"][:], qkT.t[2:4, :, t0:t0 + CH].rearrange("h p c -> p h c"), reads=[qkT], writes=[B["kT"]])
                    P.dma(SP, B["qT"][:], qkT.t[0:2, :, t0:t0 + CH].rearrange("h p c -> p h c"), reads=[qkT], writes=[B["qT"]])
                    P.dma(SP, B["kt"][:], ktok.t[t0:t0 + CH, :, :], reads=[ktok], writes=[B["kt"]])
                    P.dma(SP, B["vt"][:], vtok.t[t0:t0 + CH, :, :], reads=[vtok], writes=[B["vt"]])
                    P.dma(SP, B["gt"][:], gates.t[t0:t0 + CH, :], reads=[gates], writes=[B["gt"]])
                st.append(s_load)

                def s_gates():
                    g = B["gt"]
                    for h in range(4):
                        P.op(POOL, lambda e, h=h: e.tensor_scalar(out=B["Ug"][:, h, :], in0=cs[:, d, :],
                                                                  scalar1=g[:, 8 + 4 * d + h:9 + 4 * d + h], scalar2=None,
                                                                  op0=ALU.mult), reads=[cs, g], writes=[B["Ug"]])
                    _mm_group(P, pg, pg[:].rearrange("p h c -> p (h c)"),
                              [(onesf[:], B["Ug"][:].rearrange("p h c -> p (h c)"))], [onesf, B["Ug"]])
                    _mm_group(P, pc, pc[:, 0:4], [(cs[:, d, :], g[:, 8 + 4 * d:12 + 4 * d])], [cs, g])
                    P.op(ACT, lambda e: e.activation(out=B["gcc"][:], in_=pc[:, 0:4], func=AF.Copy), reads=[pc], writes=[B["gcc"]])
                    P.op(DVE, lambda e: e.tensor_copy(out=B["gl"][:], in_=pg[:, :, last]), reads=[pg], writes=[B["gl"]])
                    for h in range(4):
                        P.op(DVE, lambda e, h=h: e.tensor_scalar(out=B["Dm"][:, h, :], in0=pg[:, h, :],
                                                                 scalar1=B["gcc"][:, h:h + 1], scalar2=0.0,
                                                                 op0=ALU.subtract, op1=ALU.min),
                             reads=[pg, B["gcc"]], writes=[B["Dm"]])
                    P.op(ACT, lambda e: e.activation(out=B["Dm"][:], in_=B["Dm"][:], func=AF.Exp), reads=[B["Dm"]], writes=[B["Dm"]])
                    P.op(POOL, lambda e: e.tensor_tensor(out=B["Dsn"][:], in0=B["Dm"][:], in1=bc_h(cs[:, 2 + d, :]), op=ALU.mult),
                         reads=[B["Dm"], cs], writes=[B["Dsn"]])
                    P.op(POOL, lambda e: e.tensor_tensor(out=B["Dt"][:], in0=B["Dm"][:], in1=bc_h(cs[:, 4 + d, :]), op=ALU.mult),
                         reads=[B["Dm"], cs], writes=[B["Dt"]])
                    P.op(ACT, lambda e: e.activation(out=B["egc"][:], in_=B["gcc"][:], func=AF.Exp), reads=[B["gcc"]], writes=[B["egc"]])
                    P.op(DVE, lambda e: e.tensor_tensor(out=B["kd"][:], in0=B["gl"][:], in1=B["gcc"][:], op=ALU.subtract),
                         reads=[B["gl"], B["gcc"]], writes=[B["kd"]])
                    P.op(ACT, lambda e: e.activation(out=B["kd"][:], in_=B["kd"][:], func=AF.Exp), reads=[B["kd"]], writes=[B["kd"]])
                    P.op(ACT, lambda e: e.activation(out=B["egl"][:], in_=B["gl"][:], func=AF.Exp), reads=[B["gl"]], writes=[B["egl"]])
                st.append(s_gates)

                def s_kk():
                    def fn(e):
                        for hq in range(2):
                            e.matmul(pkq[:, hq, :], B["kT"][:, hq, :], B["kT"][:, hq, :], start=True, stop=True)
                        for hq in range(2):
                            ins = e.matmul(pkq[:, 2 + hq, :], B["kT"][:, hq, :], B["qT"][:, hq, :], start=True, stop=True)
                        return ins
                    P.op(PE, fn, reads=[B["kT"], B["qT"]], writes=[pkq])
                    for hq in range(2):
                        P.op(DVE, lambda e, hq=hq: e.tensor_tensor(
                            out=B["LpT"][:, 2 * hq:2 * hq + 2, :],
                            in0=pkq[:, hq, :].unsqueeze(1).to_broadcast([128, 2, 128]),
                            in1=B["Dsn"][:, 2 * hq:2 * hq + 2, :], op=ALU.mult), reads=[pkq, B["Dsn"]], writes=[B["LpT"]])
                        P.op(DVE, lambda e, hq=hq: e.tensor_tensor(
                            out=B["QKm"][:, 2 * hq:2 * hq + 2, :],
                            in0=pkq[:, 2 + hq, :].unsqueeze(1).to_broadcast([128, 2, 128]),
                            in1=B["Dt"][:, 2 * hq:2 * hq + 2, :], op=ALU.mult), reads=[pkq, B["Dt"]], writes=[B["QKm"]])
                    bsl = B["gt"][:, 4 * d:4 * d + 4]
                    P.op(POOL, lambda e: e.tensor_tensor(out=B["Pm"][:], in0=bc_h(cs[:, 20, :]), in1=bc_c(bsl), op=ALU.mult),
                         reads=[cs, B["gt"]], writes=[B["Pm"]])
                    P.op(POOL, lambda e: e.tensor_tensor(out=B["Qm"][:], in0=bc_h(cs[:, 20, :]), in1=bc_c(bsl), op=ALU.mult),
                         reads=[cs, B["gt"]], writes=[B["Qm"]])
                st.append(s_kk)

                for l in range(NLVL):
                    def s_x(l=l):
                        def fn(e):
                            for h in range(4):
                                ins = e.matmul(B["pX"][:, h, :], B["LpT"][:, h, :], B["Pm"][:, h, :], start=True, stop=True)
                            return ins
                        P.op(PE, fn, reads=[B["LpT"], B["Pm"]], writes=[B["pX"]])
                        P.op(ACT, lambda e: e.activation(out=B["X"][:], in_=B["pX"][:], func=AF.Copy), reads=[B["pX"]], writes=[B["X"]])
                    st.append(s_x)

                    def s_y(l=l):
                        def fn(e):
                            for h in range(4):
                                ins = e.matmul(B["pY"][:, h, :], B["Qm"][:, h, :], B["X"][:, h, :], start=True, stop=True)
                            return ins
                        P.op(PE, fn, reads=[B["Qm"], B["X"]], writes=[B["pY"]])
                        mk = cs[:, 6 + d * NLVL + l, :]
                        P.op(DVE, lambda e: e.tensor_tensor(out=B["tmp"][:], in0=B["pY"][:], in1=bc_h(mk), op=ALU.mult),
                             reads=[B["pY"], cs], writes=[B["tmp"]])
                        P.op(POOL, lambda e: e.tensor_tensor(out=B["Pm"][:], in0=B["Pm"][:], in1=B["tmp"][:], op=ALU.add),
                             reads=[B["Pm"], B["tmp"]], writes=[B["Pm"]])
                    st.append(s_y)

                    def s_t(l=l):
                        def fn(e):
                            for h in range(4):
                                ins = e.transpose(pTd[d][:, d, h * 128:(h + 1) * 128], B["Pm"][:, h, :], identb[:])
                            return ins
                        P.op(PE, fn, reads=[B["Pm"], identb], writes=[pTd[d]])
                        P.op(ACT, lambda e: e.activation(out=B["Qm"][:].rearrange("p h c -> p (h c)"), in_=pTd[d][:, d, :], func=AF.Copy),
                             reads=[pTd[d]], writes=[B["Qm"]])
                    st.append(s_t)

                def s_scan1():
                    P.op(DVE, lambda e: e.tensor_tensor(out=B["Rg"][:], in0=B["Qm"][:], in1=bc_c(B["egc"][:]), op=ALU.mult),
                         reads=[B["Qm"], B["egc"]], writes=[B["Rg"]])

                    def fn(e):
                        for h in range(4):
                            ins = e.matmul(B["pX"][:, h, :], B["kt"][:, h // 2, :], B["Rg"][:, h, :], start=True, stop=True)
                        return ins
                    P.op(PE, fn, reads=[B["kt"], B["Rg"]], writes=[B["pX"]])
                    P.op(ACT, lambda e: e.activation(out=B["nwT"][:], in_=B["pX"][:], func=AF.Copy, scale=-1.0),
                         reads=[B["pX"]], writes=[B["nwT"]])
                st.append(s_scan1)

                def s_scan2():
                    def fn(e):
                        for h in range(4):
                            e.matmul(B["pY"][:, h, :], B["Qm"][:, h, :], B["vt"][:, h, :], start=True, stop=False)
                            ins = e.matmul(B["pY"][:, h, :], B["nwT"][:, h, :], B["Sb"][:, h, :], start=False, stop=True)
                        return ins
                    P.op(PE, fn, reads=[B["Qm"], B["vt"], B["nwT"], B["Sb"]], writes=[B["pY"]])
                    P.op(ACT, lambda e: e.activation(out=B["vnb"][:], in_=B["pY"][:], func=AF.Copy), reads=[B["pY"]], writes=[B["vnb"]])
                    P.op(DVE, lambda e: e.tensor_tensor(out=B["vns"][:], in0=B["pY"][:], in1=bc_c(B["kd"][:]), op=ALU.mult),
                         reads=[B["pY"], B["kd"]], writes=[B["vns"]])

                    def fa(e):
                        for h in range(4):
                            ins = e.matmul(B["pX"][:, h, :], B["qT"][:, h // 2, :], B["Sb"][:, h, :], start=True, stop=True)
                        return ins
                    P.op(PE, fa, reads=[B["qT"], B["Sb"]], writes=[B["pX"]])
                    P.op(DVE, lambda e: e.tensor_tensor(out=B["Asb"][:], in0=B["pX"][:], in1=bc_c(B["egc"][:]), op=ALU.mult),
                         reads=[B["pX"], B["egc"]], writes=[B["Asb"]])
                st.append(s_scan2)

                def s_scan3():
                    def fb(e):
                        for h in range(4):
                            ins = e.matmul(B["pY"][:, h, :], B["QKm"][:, h, :], B["vnb"][:, h, :], start=True, stop=True)
                        return ins
                    P.op(PE, fb, reads=[B["QKm"], B["vnb"]], writes=[B["pY"]])
                    P.op(DVE, lambda e: e.tensor_tensor(out=B["osb"][:], in0=B["pY"][:], in1=B["Asb"][:], op=ALU.add),
                         reads=[B["pY"], B["Asb"]], writes=[B["osb"]])
                    P.dma(SP, osc.t[d, t0:t0 + CH, :], B["osb"][:].rearrange("p h c -> p (h c)"), reads=[B["osb"]], writes=[osc])

                    def fs(e):
                        for h in range(4):
                            ins = e.matmul(B["pX"][:, h, :], B["kt"][:, h // 2, :], B["vns"][:, h, :], start=True, stop=True)
                        return ins
                    P.op(PE, fs, reads=[B["kt"], B["vns"]], writes=[B["pX"]])
                    P.op(POOL, lambda e: e.tensor_tensor(out=B["Stmp"][:], in0=B["St"][:], in1=bc_c(B["egl"][:]), op=ALU.mult),
                         reads=[B["St"], B["egl"]], writes=[B["Stmp"]])
                    P.op(DVE, lambda e: e.tensor_tensor(out=B["St"][:], in0=B["pX"][:], in1=B["Stmp"][:], op=ALU.add),
                         reads=[B["pX"], B["Stmp"]], writes=[B["St"]])
                    P.op(ACT, lambda e: e.activation(out=B["Sb"][:], in_=B["St"][:], func=AF.Copy), reads=[B["St"]], writes=[B["Sb"]])
                st.append(s_scan3)
                return st

            for si, L in enumerate(seq_lens):
                nch = L // CH
                for d in range(2):
                    P.op(POOL, lambda e, d=d: e.memset(SD[d]["St"][:], 0.0), writes=[SD[d]["St"]])
                    P.op(POOL, lambda e, d=d: e.memset(SD[d]["Sb"][:], 0.0), writes=[SD[d]["Sb"]])
                live = {}
                for step in range((nch - 1) * HLAG + NST):
                    for d in range(2):
                        n_new = step // HLAG
                        if step % HLAG == 0 and n_new < nch:
                            ci = n_new if d == 0 else nch - 1 - n_new
                            live[(d, n_new)] = unit_stages(d, d * 2 + n_new % 2, offs[si] + ci * CH)
                        for n in (n_new - 1, n_new):
                            k = step - n * HLAG
                            if n >= 0 and n < nch and 0 <= k < NST:
                                live[(d, n)][k]()
                                if k == NST - 1:
                                    del live[(d, n)]
            P.flush()
        P.pes = None

        if upto < 4:
            return nc
        with ExitStack() as pes:
            P.pes = pes
            gn = P.sb("gn", [128, 128], F32)
            P.dma(SP, gn[:], gnw.t, writes=[gn])
            of = [P.sb("of%d" % i, [128, 4, 128], F32) for i in range(2)]
            ob = [P.sb("ob%d" % i, [128, 4, 128], F32) for i in range(2)]
            zz = [P.sb("zz%d" % i, [128, 4, 128], F32) for i in range(2)]
            sqj = P.sb("sqj", [128, 128], F32)
            ssum = P.sb("ssum", [128, 4], F32)
            oo = [P.sb("oo%d" % i, [128, 4, 128], BF16) for i in range(2)]
            for bi in range(NTOK // 128):
                t0 = bi * 128
                a, b, z, o = of[bi % 2], ob[bi % 2], zz[bi % 2], oo[bi % 2]
                P.dma(SP, a[:].rearrange("p h c -> p (h c)"), osc.t[0, t0:t0 + 128, :], reads=[osc], writes=[a])
                P.dma(SP, b[:].rearrange("p h c -> p (h c)"), osc.t[1, t0:t0 + 128, :], reads=[osc], writes=[b])
                P.dma(SP, z[:].rearrange("p h c -> p (h c)"), zsil.t[t0:t0 + 128, :], reads=[zsil], writes=[z])
                P.op(DVE, lambda e, a=a, b=b: e.tensor_tensor(out=a[:], in0=a[:], in1=b[:], op=ALU.add), reads=[a, b], writes=[a])
                P.op(POOL, lambda e: e.memset(ssum[:], 0.0), writes=[ssum])
                for h in range(4):
                    P.op(ACT, lambda e, a=a, h=h: e.activation(out=sqj[:], in_=a[:, h, :], func=AF.Square,
                                                               accum_out=ssum[:, h:h + 1]), reads=[a], writes=[sqj, ssum])
                P.op(ACT, lambda e: e.activation(out=ssum[:], in_=ssum[:], func=AF.Ln, bias=RMS_EPS, scale=1.0 / 128),
                     reads=[ssum], writes=[ssum])
                P.op(ACT, lambda e: e.activation(out=ssum[:], in_=ssum[:], func=AF.Exp, scale=-0.5), reads=[ssum], writes=[ssum])
                P.op(POOL, lambda e, z=z: e.tensor_tensor(out=z[:], in0=z[:], in1=gn[:].unsqueeze(1).to_broadcast([128, 4, 128]),
                                                          op=ALU.mult), reads=[z, gn], writes=[z])
                P.op(DVE, lambda e, a=a: e.tensor_tensor(out=a[:], in0=a[:], in1=ssum[:].unsqueeze(2).to_broadcast([128, 4, 128]),
                                                         op=ALU.mult), reads=[a, ssum], writes=[a])
                P.op(DVE, lambda e, a=a, z=z, o=o: e.tensor_tensor(out=o[:], in0=a[:], in1=z[:], op=ALU.mult),
                     reads=[a, z], writes=[o])
                P.dma(SP, og.t[t0:t0 + 128, :], o[:].rearrange("p h c -> p (h c)"), reads=[o], writes=[og])
            P.flush()
        P.pes = None
    return nc


GDN_QK_HEADS = 16
GDN_V_HEADS = 32
GDN_Q_DIM = 2048
GDN_V_DIM = 4096
GDN_CONV_DIM = 8192


def prep_A(c, xT_all, norm_mix_pre, gdn_w_in, gdn_conv_w, gdn_a_log, gdn_dt_bias, gdn_norm_w, cst):
    w_in = gdn_w_in[0]
    qh = [2 * c, 2 * c + 1]
    vh = [4 * c + i for i in range(4)]
    qcols = np.concatenate([np.arange(h * 128, (h + 1) * 128) for h in qh])
    kcols = GDN_Q_DIM + qcols
    vcols = 2 * GDN_Q_DIM + np.concatenate([np.arange(h * 128, (h + 1) * 128) for h in vh])
    zcols = GDN_CONV_DIM + np.concatenate([np.arange(h * 128, (h + 1) * 128) for h in vh])
    o_b = GDN_CONV_DIM + GDN_V_DIM
    o_a = o_b + 2 * GDN_V_HEADS
    bcols = np.array([o_b + d * GDN_V_HEADS + h for d in range(2) for h in vh])
    acols = np.array([o_a + d * GDN_V_HEADS + h for d in range(2) for h in vh])
    conv_cols = np.concatenate([qcols, kcols, vcols])
    cw = gdn_conv_w[0][:, conv_cols]
    cw = np.ascontiguousarray(cw.reshape(5, 8, 128).transpose(2, 1, 0))
    al = np.array([gdn_a_log[0, d, h] for d in range(2) for h in vh], np.float32)
    db = np.array([gdn_dt_bias[0, d, h] for d in range(2) for h in vh], np.float32)
    return {
        "xT": xT_all,
        "gpre": np.ascontiguousarray(norm_mix_pre[0].reshape(16, 128).T),
        "Wqkv": np.ascontiguousarray(w_in[:, conv_cols]),
        "Wz": np.ascontiguousarray(w_in[:, zcols]),
        "Wg": np.ascontiguousarray(w_in[:, np.concatenate([bcols, acols])]),
        "convw": cw,
        "alog": np.ascontiguousarray(np.broadcast_to(al[None, :], (128, 8))),
        "dtb": np.ascontiguousarray(np.broadcast_to(db[None, :], (128, 8))),
        "gnw": np.ascontiguousarray(np.broadcast_to(gdn_norm_w[0][None, :], (128, 128))),
        "cst": cst,
    }


_PROF = []


def _launch(nc, in_maps, tag):
    res = run_bass_kernel_spmd(nc, in_maps, core_ids=list(range(NCORE)))
    t = getattr(res, "exec_time_ns", None)
    if t is not None:
        _PROF.append((tag, t))
    return res


def run_A(seq_lens, xT_all, norm_mix_pre, gdn_w_in, gdn_conv_w, gdn_a_log, gdn_dt_bias, gdn_norm_w):
    cst = _gdn_consts()
    nc = build_A(seq_lens)
    in_maps = [prep_A(c, xT_all, norm_mix_pre, gdn_w_in, gdn_conv_w, gdn_a_log, gdn_dt_bias, gdn_norm_w, cst)
               for c in range(NCORE)]
    res = _launch(nc, in_maps, "A")
    return np.concatenate([np.asarray(res.results[c]["og"]) for c in range(NCORE)], axis=1)


def lay_w(W):
    K, M = W.shape
    return np.ascontiguousarray(W.reshape(K // 128, 128, M // 128, 128).transpose(2, 1, 0, 3))


def lay_g(g):
    return np.ascontiguousarray(g.reshape(-1, 128).T)


class PostBufs:
    def __init__(self, P, TT, Kc):
        self.TT = TT
        self.xs = P.sb("xs", [128, 16, TT], F32)
        self.at = P.sb("at", [128, Kc, TT], BF16)
        self.mT = P.sb("mT", [128, 16, TT], F32)
        self.sq = P.sb("sq", [128, 16, TT], BF16)
        self.hb = P.sb("hb", [128, 16, TT], BF16)
        self.hid = P.sb("hid", [128, 64, TT], BF16)
        self.rstd = P.sb("rstd", [128, TT], F32)
        self.r2 = P.sb("r2", [128, TT], F32)
        self.tmpf = [P.sb("tmpf%d" % i, [128, TT], F32) for i in range(2)]
        self.wA = [P.sb("wA%d" % i, [128, Kc, 128], BF16) for i in range(2)]
        self.wI = [P.sb("wI%d" % i, [128, 16, 128], BF16) for i in range(3)]
        self.wO = [P.sb("wO%d" % i, [128, 32, 128], BF16) for i in range(2)]
        self.ones = P.sb("ones", [128, 128], BF16)
        self.pacc = [P.ps("pacc%d" % i, [128, 512]) for i in range(4)]
        self.pss = P.ps("pss", [128, 512])
        self.pi = 0
        P.op(POOL, lambda e: e.memset(self.ones[:], 1.0), writes=[self.ones])

    def nextp(self):
        p = self.pacc[self.pi % 4]
        self.pi += 1
        return p


class MlaBufs:
    def __init__(self, P, TT):
        self.TT = TT
        self.xs = P.sb("xs", [128, 16, TT], F32)
        self.sq = P.sb("sq", [128, 16, TT], BF16)
        self.hb = P.sb("hb", [128, 16, TT], BF16)
        self.rstd = P.sb("rstd", [128, TT], F32)
        self.ones = P.sb("ones", [128, 128], BF16)
        self.pacc = [P.ps("pacc%d" % i, [128, 512]) for i in range(4)]
        self.pss = P.ps("pss", [128, 512])
        self.pi = 0
        P.op(POOL, lambda e: e.memset(self.ones[:], 1.0), writes=[self.ones])

    def nextp(self):
        p = self.pacc[self.pi % 4]
        self.pi += 1
        return p


def _rms_rstd(P, Bf, src_t, nch, dim, out_t):
    TT = Bf.TT
    _mm_group(P, Bf.pss, Bf.pss[:, 0:TT], [(Bf.ones[:], Bf.sq[:, k, :]) for k in range(nch)], [Bf.ones, Bf.sq])
    P.op(ACT, lambda e: e.activation(out=out_t[:], in_=Bf.pss[:, 0:TT], func=AF.Ln, bias=RMS_EPS, scale=1.0 / dim),
         reads=[Bf.pss], writes=[out_t])
    P.op(ACT, lambda e: e.activation(out=out_t[:], in_=out_t[:], func=AF.Exp, scale=-0.5), reads=[out_t], writes=[out_t])


def _norm_residual(P, Bf, g_t):
    TT = Bf.TT
    P.op(POOL, lambda e: e.tensor_tensor(out=Bf.sq[:], in0=Bf.mT[:], in1=Bf.mT[:], op=ALU.mult), reads=[Bf.mT], writes=[Bf.sq])
    _rms_rstd(P, Bf, Bf.mT, 16, D_MODEL, Bf.rstd)
    for mc in range(16):
        tf = Bf.tmpf[mc % 2]
        P.op(POOL, lambda e, mc=mc, tf=tf: e.tensor_tensor(out=tf[:], in0=Bf.mT[:, mc, :], in1=Bf.rstd[:], op=ALU.mult),
             reads=[Bf.mT, Bf.rstd], writes=[tf])
        P.op(DVE, lambda e, mc=mc, tf=tf: e.scalar_tensor_tensor(out=Bf.xs[:, mc, :], in0=tf[:], scalar=g_t[:, mc:mc + 1],
                                                                 in1=Bf.xs[:, mc, :], op0=ALU.mult, op1=ALU.add),
             reads=[tf, g_t, Bf.xs], writes=[Bf.xs])


def post_block(P, Bf, Kc, a_src, Wmix_b, x_src, g_post, g_fpre, g_fpost, Wfi_b, Wfo_b, t0):
    TT = Bf.TT
    P.dma(SP, Bf.xs[:], x_src.t[:, t0:t0 + TT].rearrange("(c p) t -> p c t", p=128), reads=[x_src], writes=[Bf.xs])
    P.dma(SP, Bf.at[:], a_src.t[:, t0:t0 + TT].rearrange("(c p) t -> p c t", p=128), reads=[a_src], writes=[Bf.at])
    for mc in range(16):
        w = Bf.wA[mc % 2]
        P.dma(SP, w[:], Wmix_b.t[mc], reads=[Wmix_b], writes=[w])
        p = Bf.nextp()
        _mm_group(P, p, p[:, 0:TT], [(w[:, k, :], Bf.at[:, k, :]) for k in range(Kc)], [w, Bf.at])
        P.op(ACT, lambda e, p=p, mc=mc: e.activation(out=Bf.mT[:, mc, :], in_=p[:, 0:TT], func=AF.Copy), reads=[p], writes=[Bf.mT])
    _norm_residual(P, Bf, g_post)
    P.op(ACT, lambda e: e.activation(out=Bf.sq[:], in_=Bf.xs[:], func=AF.Square), reads=[Bf.xs], writes=[Bf.sq])
    P.op(DVE, lambda e: e.tensor_tensor(out=Bf.hb[:], in0=Bf.xs[:], in1=g_fpre[:].unsqueeze(2).to_broadcast([128, 16, TT]),
                                        op=ALU.mult), reads=[Bf.xs, g_fpre], writes=[Bf.hb])
    _rms_rstd(P, Bf, Bf.xs, 16, D_MODEL, Bf.rstd)
    P.op(DVE, lambda e: e.tensor_tensor(out=Bf.r2[:], in0=Bf.rstd[:], in1=Bf.rstd[:], op=ALU.mult), reads=[Bf.rstd], writes=[Bf.r2])
    for mc in range(64):
        w = Bf.wI[mc % 3]
        P.dma(SP, w[:], Wfi_b.t[mc], reads=[Wfi_b], writes=[w])
        p = Bf.nextp()
        _mm_group(P, p, p[:, 0:TT], [(w[:, k, :], Bf.hb[:, k, :]) for k in range(16)], [w, Bf.hb])
        tf = Bf.tmpf[mc % 2]
        P.op(ACT, lambda e, p=p, tf=tf: e.activation(out=tf[:], in_=p[:, 0:TT], func=AF.Relu), reads=[p], writes=[tf])
        P.op(POOL, lambda e, mc=mc, tf=tf: e.tensor_tensor(out=Bf.hid[:, mc, :], in0=tf[:], in1=tf[:], op=ALU.mult),
             reads=[tf], writes=[Bf.hid])
    for mc in range(16):
        p = Bf.nextp()
        for hf in range(2):
            w = Bf.wO[hf]
            P.dma(SP, w[:], Wfo_b.t[mc, :, hf * 32:(hf + 1) * 32, :], reads=[Wfo_b], writes=[w])
            _mm_group(P, p, p[:, 0:TT], [(w[:, k, :], Bf.hid[:, hf * 32 + k, :]) for k in range(32)], [w, Bf.hid],
                      first=(hf == 0), last=(hf == 1))
        P.op(DVE, lambda e, p=p, mc=mc: e.tensor_tensor(out=Bf.mT[:, mc, :], in0=p[:, 0:TT], in1=Bf.r2[:], op=ALU.mult),
             reads=[p, Bf.r2], writes=[Bf.mT])
    _norm_residual(P, Bf, g_fpost)


def cast_w(P, dst, src, nsplit):
    n = src.t.shape[0]
    step = max(1, n // nsplit)
    for i in range(0, n, step):
        P.dma(POOL, dst.t[i:i + step], src.t[i:i + step], reads=[src], writes=[dst])


def build_B(NT, TT=256, debug=False):
    nc = bass.Bass("TRN2", target_bir_lowering=False)
    with ExitStack() as es:
        P = Prog(nc, es)
        xT = P.dram("xT", [D_MODEL, NT], F32, kind="ExternalInput")
        ogT = P.dram("ogT", [4096, NT], BF16, kind="ExternalInput")
        Wout = P.dram("Wout", [16, 128, 32, 128], F32, kind="ExternalInput")
        Wfi = P.dram("Wfi", [64, 128, 16, 128], F32, kind="ExternalInput")
        Wfo = P.dram("Wfo", [16, 128, 64, 128], F32, kind="ExternalInput")
        Wa = P.dram("Wa", [12, 128, 16, 128], F32, kind="ExternalInput")
        Wq = P.dram("Wq", [48, 128, 6, 128], F32, kind="ExternalInput")
        Wk = P.dram("Wk", [16, 128, 4, 128], F32, kind="ExternalInput")
        Wv = P.dram("Wv", [128, 4, 2048], F32, kind="ExternalInput")
        gv = P.dram("gv", [128, 4, 16], F32, kind="ExternalInput")
        gq = P.dram("gq", [128, 6], F32, kind="ExternalInput")
        gkv = P.dram("gkv", [128, 4], F32, kind="ExternalInput")
        C2 = P.dram("C2", [64, NT], F32, kind="ExternalInput")
        S2 = P.dram("S2", [64, NT], F32, kind="ExternalInput")
        E64 = P.dram("E64", [128, 65], F32, kind="ExternalInput")
        x2T = P.dram("x2T", [D_MODEL, NT], F32, kind="ExternalOutput")
        QN = P.dram("QN", [16, 128, NT], BF16, kind="ExternalOutput")
        QR = P.dram("QR", [16, 65, NT], BF16, kind="ExternalOutput")
        KN = P.dram("KN", [16, 128, NT], BF16, kind="ExternalOutput")
        KR = P.dram("KR", [65, NT], BF16, kind="ExternalOutput")
        V = P.dram("V", [NT, 2048], BF16, kind="ExternalOutput")
        Wout_b = P.dram("Wout_b", [16, 128, 32, 128], BF16)
        Wfi_b = P.dram("Wfi_b", [64, 128, 16, 128], BF16)
        Wfo_b = P.dram("Wfo_b", [16, 128, 64, 128], BF16)
        Wa_b = P.dram("Wa_b", [12, 128, 16, 128], BF16)
        Wq_b = P.dram("Wq_b", [48, 128, 6, 128], BF16)
        Wk_b = P.dram("Wk_b", [16, 128, 4, 128], BF16)
        Wv_b = P.dram("Wv_b", [128, 4, 2048], BF16)
        cast_w(P, Wout_b, Wout, 4)
        cast_w(P, Wfi_b, Wfi, 8)
        cast_w(P, Wfo_b, Wfo, 8)
        cast_w(P, Wa_b, Wa, 2)
        cast_w(P, Wq_b, Wq, 2)
        cast_w(P, Wk_b, Wk, 1)
        cast_w(P, Wv_b, Wv, 1)
        with ExitStack() as pes:
            P.pes = pes
            Bf = PostBufs(P, TT, 32)
            g_ts = [P.sb("g_t%d" % i, [128, 16], F32) for i in range(3)]
            for i in range(3):
                P.dma(SP, g_ts[i][:], gv.t[:, i, :], writes=[g_ts[i]])
            for ti in range(NT // TT):
                t0 = ti * TT
                post_block(P, Bf, 32, ogT, Wout_b, xT, g_ts[0], g_ts[1], g_ts[2], Wfi_b, Wfo_b, t0)
                P.dma(SP, x2T.t[:, t0:t0 + TT].rearrange("(c p) t -> p c t", p=128), Bf.xs[:], reads=[Bf.xs], writes=[x2T])
            P.flush()
        P.pes = None
        with ExitStack() as pes:
            P.pes = pes
            Bf = MlaBufs(P, TT)
            g_t3 = P.sb("g_t3", [128, 16], F32)
            gq_t = P.sb("gq_t", [128, 6], F32)
            gkv_t = P.sb("gkv_t", [128, 4], F32)
            e64f = P.sb("e64f", [128, 65], F32)
            e64 = P.sb("e64", [128, 65], BF16)
            wv = P.sb("wv", [128, 4, 2048], BF16)
            P.dma(SP, g_t3[:], gv.t[:, 3, :], writes=[g_t3])
            P.dma(SP, gq_t[:], gq.t, writes=[gq_t])
            P.dma(SP, gkv_t[:], gkv.t, writes=[gkv_t])
            P.dma(SP, e64f[:], E64.t, writes=[e64f])
            P.op(DVE, lambda e: e.tensor_copy(out=e64[:], in_=e64f[:]), reads=[e64f], writes=[e64])
            P.dma(SP, wv[:], Wv_b.t, reads=[Wv_b], writes=[wv])
            cq = P.sb("cq", [128, 6, TT], F32)
            ckv = P.sb("ckv", [128, 4, TT], F32)
            cqb = P.sb("cqb", [128, 6, TT], BF16)
            ckvb = P.sb("ckvb", [128, 4, TT], BF16)
            rq = P.sb("rq", [128, TT], F32)
            rkv = P.sb("rkv", [128, TT], F32)
            rkc = P.sb("rkc", [128, 4], F32)
            c2 = P.sb("c2", [64, TT], F32)
            s2 = P.sb("s2", [64, TT], F32)
            c2r = P.sb("c2r", [64, TT], F32)
            s2r = P.sb("s2r", [64, TT], F32)
            Ak = P.sb("Ak", [64, TT], F32)
            Bk = P.sb("Bk", [64, TT], F32)
            krf = P.sb("krf", [64, TT], F32)
            krt = [P.sb("krt%d" % i, [65, TT], BF16) for i in range(2)]
            qrt = [P.sb("qrt%d" % i, [65, TT], BF16) for i in range(2)]
            qnt = [P.sb("qnt%d" % i, [128, TT], BF16) for i in range(2)]
            knall = P.sb("knall", [128, 16, TT], BF16)
            prn = P.sb("prn", [128, TT], BF16)
            prr = P.sb("prr", [64, TT], BF16)
            vt = [P.sb("vt%d" % i, [128, 2048], BF16) for i in range(2)]
            wa_t = [P.sb("wa_t%d" % i, [128, 16, 128], BF16) for i in range(2)]
            wq_t = [P.sb("wq_t%d" % i, [128, 6, 128], BF16) for i in range(3)]
            wk_t = [P.sb("wk_t%d" % i, [128, 4, 128], BF16) for i in range(2)]
            psc = P.ps("psc", [128, 512])
            p65 = P.ps("p65", [128, 512])
            for i in range(2):
                P.op(POOL, lambda e, i=i: e.memset(krt[i][:], 1.0), writes=[krt[i]])
            nsub = TT // 128
            for ti in range(NT // TT):
                t0 = ti * TT
                xs = Bf.xs
                P.dma(SP, xs[:], x2T.t[:, t0:t0 + TT].rearrange("(c p) t -> p c t", p=128), reads=[x2T], writes=[xs])
                P.dma(SP, c2[:], C2.t[:, t0:t0 + TT], writes=[c2])
                P.dma(SP, s2[:], S2.t[:, t0:t0 + TT], writes=[s2])
                P.op(ACT, lambda e: e.activation(out=Bf.sq[:], in_=xs[:], func=AF.Square), reads=[xs], writes=[Bf.sq])
                P.op(DVE, lambda e: e.tensor_tensor(out=Bf.hb[:], in0=xs[:], in1=g_t3[:].unsqueeze(2).to_broadcast([128, 16, TT]),
                                                    op=ALU.mult), reads=[xs, g_t3], writes=[Bf.hb])
                _rms_rstd(P, Bf, xs, 16, D_MODEL, Bf.rstd)
                for mc in range(12):
                    w = wa_t[mc % 2]
                    P.dma(SP, w[:], Wa_b.t[mc], reads=[Wa_b], writes=[w])
                    p = Bf.nextp()
                    _mm_group(P, p, p[:, 0:TT], [(w[:, k, :], Bf.hb[:, k, :]) for k in range(16)], [w, Bf.hb])
                    if mc < 6:
                        P.op(DVE, lambda e, p=p, mc=mc: e.tensor_tensor(out=cq[:, mc, :], in0=p[:, 0:TT], in1=Bf.rstd[:], op=ALU.mult),
                             reads=[p, Bf.rstd], writes=[cq])
                    elif mc < 10:
                        P.op(DVE, lambda e, p=p, mc=mc: e.tensor_tensor(out=ckv[:, mc - 6, :], in0=p[:, 0:TT], in1=Bf.rstd[:], op=ALU.mult),
                             reads=[p, Bf.rstd], writes=[ckv])
                    else:
                        dst = Ak if mc == 10 else Bk
                        P.op(DVE, lambda e, p=p, dst=dst: e.tensor_tensor(out=dst[:], in0=p[0:64, 0:TT], in1=Bf.rstd[0:64, :], op=ALU.mult),
                             reads=[p, Bf.rstd], writes=[dst])
                kr = krt[ti % 2]
                P.op(DVE, lambda e: e.tensor_tensor(out=Ak[:], in0=Ak[:], in1=c2[:], op=ALU.mult), reads=[Ak, c2], writes=[Ak])
                P.op(DVE, lambda e: e.tensor_tensor(out=Bk[:], in0=Bk[:], in1=s2[:], op=ALU.mult), reads=[Bk, s2], writes=[Bk])
                P.op(DVE, lambda e: e.tensor_tensor(out=krf[:], in0=Ak[:], in1=Bk[:], op=ALU.add), reads=[Ak, Bk], writes=[krf])
                P.op(ACT, lambda e, kr=kr: e.activation(out=kr[0:64, :], in_=krf[:], func=AF.Copy), reads=[krf], writes=[kr])
                P.dma(SP, KR.t[:, t0:t0 + TT], kr[:], reads=[kr], writes=[KR])
                P.op(POOL, lambda e: e.tensor_tensor(out=Bf.sq[:, 0:6, :], in0=cq[:], in1=cq[:], op=ALU.mult), reads=[cq], writes=[Bf.sq])
                _rms_rstd(P, Bf, cq, 6, 768, rq)
                P.op(DVE, lambda e: e.tensor_tensor(out=cqb[:], in0=cq[:], in1=gq_t[:].unsqueeze(2).to_broadcast([128, 6, TT]), op=ALU.mult),
                     reads=[cq, gq_t], writes=[cqb])
                P.op(POOL, lambda e: e.tensor_tensor(out=Bf.sq[:, 0:4, :], in0=ckv[:], in1=ckv[:], op=ALU.mult), reads=[ckv], writes=[Bf.sq])
                _rms_rstd(P, Bf, ckv, 4, 512, rkv)

                def colsum(e):
                    for sub in range(nsub):
                        for k in range(4):
                            ins = e.matmul(psc[:, sub:sub + 1], Bf.sq[:, k, sub * 128:(sub + 1) * 128], Bf.ones[:, 0:1],
                                           start=(k == 0), stop=(k == 3))
                    return ins
                P.op(PE, colsum, reads=[Bf.sq, Bf.ones], writes=[psc])
                P.op(ACT, lambda e: e.activation(out=rkc[:, 0:nsub], in_=psc[:, 0:nsub], func=AF.Ln, bias=RMS_EPS, scale=1.0 / 512),
                     reads=[psc], writes=[rkc])
                P.op(ACT, lambda e: e.activation(out=rkc[:, 0:nsub], in_=rkc[:, 0:nsub], func=AF.Exp, scale=-0.5), reads=[rkc], writes=[rkc])
                P.op(DVE, lambda e: e.tensor_tensor(out=ckvb[:], in0=ckv[:], in1=gkv_t[:].unsqueeze(2).to_broadcast([128, 4, TT]), op=ALU.mult),
                     reads=[ckv, gkv_t], writes=[ckvb])
                P.op(DVE, lambda e: e.tensor_tensor(out=c2r[:], in0=c2[:], in1=rq[0:64, :], op=ALU.mult), reads=[c2, rq], writes=[c2r])
                P.op(DVE, lambda e: e.tensor_tensor(out=s2r[:], in0=s2[:], in1=rq[0:64, :], op=ALU.mult), reads=[s2, rq], writes=[s2r])
                for h in range(16):
                    w = wk_t[h % 2]
                    P.dma(SP, w[:], Wk_b.t[h], reads=[Wk_b], writes=[w])
                    p = Bf.nextp()
                    _mm_group(P, p, p[:, 0:TT], [(w[:, k, :], ckvb[:, k, :]) for k in range(4)], [w, ckvb])
                    P.op(DVE, lambda e, p=p, h=h: e.tensor_tensor(out=knall[:, h, :], in0=p[:, 0:TT], in1=rkv[:], op=ALU.mult),
                         reads=[p, rkv], writes=[knall])
                P.dma(SP, KN.t[:, :, t0:t0 + TT].rearrange("h p t -> p h t"), knall[:], reads=[knall], writes=[KN])
                for sub in range(nsub):
                    v = vt[sub % 2]
                    for g4 in range(4):
                        p = Bf.nextp()
                        _mm_group(P, p, p[:, 0:512], [(ckvb[:, k, sub * 128:(sub + 1) * 128], wv[:, k, g4 * 512:(g4 + 1) * 512])
                                                     for k in range(4)], [ckvb, wv])
                        P.op(ACT, lambda e, p=p, v=v, g4=g4, sub=sub: e.activation(out=v[:, g4 * 512:(g4 + 1) * 512], in_=p[:, 0:512],
                                                                                    func=AF.Copy, scale=rkc[:, sub:sub + 1]),
                             reads=[p, rkc], writes=[v])
                    P.dma(SP, V.t[t0 + sub * 128:t0 + (sub + 1) * 128, :], v[:], reads=[v], writes=[V])
                for h in range(16):
                    qn = qnt[h % 2]
                    qr = qrt[h % 2]
                    w0, w1, w2 = wq_t[0], wq_t[1], wq_t[2]
                    P.dma(SP, w0[:], Wq_b.t[3 * h], reads=[Wq_b], writes=[w0])
                    P.dma(SP, w1[:], Wq_b.t[3 * h + 1], reads=[Wq_b], writes=[w1])
                    P.dma(SP, w2[:], Wq_b.t[3 * h + 2], reads=[Wq_b], writes=[w2])
                    p = Bf.nextp()
                    _mm_group(P, p, p[:, 0:TT], [(w0[:, k, :], cqb[:, k, :]) for k in range(6)], [w0, cqb])
                    P.op(DVE, lambda e, p=p, qn=qn: e.tensor_tensor(out=qn[:], in0=p[:, 0:TT], in1=rq[:], op=ALU.mult),
                         reads=[p, rq], writes=[qn])
                    P.dma(SP, QN.t[h, :, t0:t0 + TT], qn[:], reads=[qn], writes=[QN])
                    pa = Bf.nextp()
                    _mm_group(P, pa, pa[0:64, 0:TT], [(w1[:, k, 0:64], cqb[:, k, :]) for k in range(6)], [w1, cqb])
                    P.op(DVE, lambda e, pa=pa: e.tensor_tensor(out=Ak[:], in0=pa[0:64, 0:TT], in1=c2r[:], op=ALU.mult),
                         reads=[pa, c2r], writes=[Ak])
                    pb = Bf.nextp()
                    _mm_group(P, pb, pb[0:64, 0:TT], [(w2[:, k, 0:64], cqb[:, k, :]) for k in range(6)], [w2, cqb])
                    P.op(DVE, lambda e, pb=pb: e.tensor_tensor(out=Bk[:], in0=pb[0:64, 0:TT], in1=s2r[:], op=ALU.mult),
                         reads=[pb, s2r], writes=[Bk])
                    P.op(DVE, lambda e, qr=qr: e.tensor_tensor(out=qr[0:64, :], in0=Ak[:], in1=Bk[:], op=ALU.add),
                         reads=[Ak, Bk], writes=[qr])
                    P.op(POOL, lambda e, qn=qn, h=h: e.tensor_tensor(out=prn[:], in0=qn[:], in1=knall[:, h, :], op=ALU.mult),
                         reads=[qn, knall], writes=[prn])
                    P.op(POOL, lambda e, qr=qr, kr=kr: e.tensor_tensor(out=prr[:], in0=qr[0:64, :], in1=kr[0:64, :], op=ALU.mult),
                         reads=[qr, kr], writes=[prr])
                    _mm_group(P, p65, p65[0:65, 0:TT], [(e64[:, :], prn[:]), (e64[0:64, :], prr[:])], [e64, prn, prr])
                    P.op(ACT, lambda e, qr=qr: e.activation(out=qr[64:65, :], in_=p65[64:65, 0:TT], func=AF.Copy, scale=-1.0),
                         reads=[p65], writes=[qr])
                    P.dma(SP, QR.t[h, :, t0:t0 + TT], qr[:], reads=[qr], writes=[QR])
            P.flush()
        P.pes = None
    return nc


MLA_HEADS = 16
ROPE_THETA = 10000.0


def _core_tokens(seq_lens, c):
    offs = np.concatenate([[0], np.cumsum(seq_lens)]).astype(int)
    idx, pos = [], []
    for s, L in enumerate(seq_lens):
        n = L // NCORE
        p = np.arange(c * n, (c + 1) * n)
        idx.append(offs[s] + p)
        pos.append(p)
    return np.concatenate(idx), np.concatenate(pos)


def _rope_tabs(pos):
    half = 32
    inv_freq = (np.float32(ROPE_THETA) ** (-(np.arange(half, dtype=np.float32) / np.float32(half)))).astype(np.float32)
    ang = (pos.astype(np.float32)[:, None] * inv_freq[None, :]).astype(np.float32)
    cos = np.cos(ang.astype(np.float64)).astype(np.float32)
    sin = np.sin(ang.astype(np.float64)).astype(np.float32)
    C2 = np.ascontiguousarray(np.concatenate([cos, cos], 1).T)
    S2 = np.ascontiguousarray(np.concatenate([-sin, sin], 1).T)
    return C2, S2


def prep_B_weights(norm_mix_post, norm_ffn_pre, norm_ffn_post, norm_mix_pre, gdn_w_out, ffn_w_in, ffn_w_out,
                   mla_w_a, mla_q_a_norm, mla_w_q_b, mla_kv_a_norm, mla_w_kv_b):
    wa = mla_w_a[0]
    rope = wa[:, 1280:1344]
    sw = np.concatenate([rope[:, 32:], rope[:, :32]], 1)
    z64 = np.zeros((2048, 64), np.float32)
    wa_p = np.concatenate([wa[:, :1280], rope, z64, sw, z64], 1)
    wq = mla_w_q_b[0]
    z = np.zeros((768, 64), np.float32)
    cols = []
    for h in range(16):
        b = h * 192
        r = wq[:, b + 128:b + 192]
        cols += [wq[:, b:b + 128], r, z, np.concatenate([r[:, 32:], r[:, :32]], 1), z]
    wq_p = np.concatenate(cols, 1)
    wkv = mla_w_kv_b[0].reshape(512, 16, 256)
    wk = np.ascontiguousarray(wkv[:, :, :128].reshape(512, 2048))
    wv = np.ascontiguousarray(wkv[:, :, 128:].reshape(512, 2048))
    e64 = np.zeros((128, 65), np.float32)
    e64[:, 64] = 1.0
    return {
        "Wout": lay_w(gdn_w_out[0]), "Wfi": lay_w(ffn_w_in[0]), "Wfo": lay_w(ffn_w_out[0]),
        "Wa": lay_w(wa_p), "Wq": lay_w(wq_p), "Wk": lay_w(wk),
        "Wv": np.ascontiguousarray(wv.reshape(4, 128, 2048).transpose(1, 0, 2)),
        "gv": np.ascontiguousarray(np.stack([lay_g(norm_mix_post[0]), lay_g(norm_ffn_pre[0]), lay_g(norm_ffn_post[0]),
                                            lay_g(norm_mix_pre[1])], 1)),
        "gq": lay_g(mla_q_a_norm[0]), "gkv": lay_g(mla_kv_a_norm[0]), "E64": e64,
    }


def build_C(slabs, TQ=512, TT=256):
    NT = sum(slabs)
    NTOK = NCORE * NT
    scale = float(192 ** -0.5)
    nc = bass.Bass("TRN2", target_bir_lowering=False)
    with ExitStack() as es:
        P = Prog(nc, es)
        QN = P.dram("QN", [16, 128, NT], BF16, kind="ExternalInput")
        QR = P.dram("QR", [16, 65, NT], BF16, kind="ExternalInput")
        KNa = P.dram("KNa", [16, 128, NTOK], BF16, kind="ExternalInput")
        KRa = P.dram("KRa", [65, NTOK], BF16, kind="ExternalInput")
        Va = P.dram("Va", [16, NCORE, 128, NT // 128, 128], BF16, kind="ExternalInput")
        x2T = P.dram("x2T", [D_MODEL, NT], F32, kind="ExternalInput")
        Wo = P.dram("Wo", [16, 128, 16, 128], F32, kind="ExternalInput")
        Wfi = P.dram("Wfi", [64, 128, 16, 128], F32, kind="ExternalInput")
        Wfo = P.dram("Wfo", [16, 128, 64, 128], F32, kind="ExternalInput")
        gv = P.dram("gv", [128, 3, 16], F32, kind="ExternalInput")
        yT = P.dram("yT", [D_MODEL, NT], F32, kind="ExternalOutput")
        aoT = P.dram("aoT", [D_MODEL, NT], BF16)
        Wo_b = P.dram("Wo_b", [16, 128, 16, 128], BF16)
        Wfi_b = P.dram("Wfi_b", [64, 128, 16, 128], BF16)
        Wfo_b = P.dram("Wfo_b", [16, 128, 64, 128], BF16)
        cast_w(P, Wo_b, Wo, 2)
        cast_w(P, Wfi_b, Wfi, 8)
        cast_w(P, Wfo_b, Wfo, 8)
        SEG = max(slabs)
        with ExitStack() as pes:
            P.pes = pes
            ones = P.sb("ones", [128, 128], BF16)
            P.op(POOL, lambda e: e.memset(ones[:], 1.0), writes=[ones])
            QG = 2
            qn = [P.sb("qn%d" % i, [128, QG * TQ], BF16) for i in range(2)]
            qr = [P.sb("qr%d" % i, [65, QG * TQ], BF16) for i in range(2)]
            kn = [P.sb("kn%d" % i, [128, SEG], BF16) for i in range(2)]
            kr = [P.sb("kr%d" % i, [65, SEG], BF16) for i in range(2)]
            vv = [P.sb("vv%d" % i, [128, SEG // 128, 128], BF16) for i in range(2)]
            pt = [P.sb("pt%d" % i, [128, TQ], BF16) for i in range(3)]
            rinv = P.sb("rinv", [128, TQ], F32)
            ao = [P.sb("ao%d" % i, [128, TQ], BF16) for i in range(2)]
            ps_s = [P.ps("ps_s%d" % i, [128, 512]) for i in range(3)]
            ps_o = [P.ps("ps_o%d" % i, [128, 512]) for i in range(QG)]
            ps_r = [P.ps("ps_r%d" % i, [128, 512]) for i in range(QG)]
            soff = 0
            cnt = 0
            hc = 0
            sc = 0
            ac = 0
            for slab in slabs:
                nkb = slab // 128
                for qg in range(slab // (QG * TQ)):
                    q0 = soff + qg * QG * TQ
                    for h in range(16):
                        qn_, qr_ = qn[hc % 2], qr[hc % 2]
                        hc += 1
                        P.dma(SP, qn_[:], QN.t[h, :, q0:q0 + QG * TQ], reads=[QN], writes=[qn_])
                        P.dma(SP, qr_[:], QR.t[h, :, q0:q0 + QG * TQ], reads=[QR], writes=[qr_])
                        nblk = NCORE * nkb
                        bi = 0
                        for r in range(NCORE):
                            k0 = r * NT + soff
                            kn_, kr_, v_ = kn[sc % 2], kr[sc % 2], vv[sc % 2]
                            sc += 1
                            P.dma(SP, kn_[:, 0:slab], KNa.t[h, :, k0:k0 + slab], reads=[KNa], writes=[kn_])
                            P.dma(SP, kr_[:, 0:slab], KRa.t[:, k0:k0 + slab], reads=[KRa], writes=[kr_])
                            P.dma(ACT, v_[:, 0:nkb, :], Va.t[h, r, :, soff // 128:soff // 128 + nkb, :], reads=[Va], writes=[v_])
                            for kb in range(nkb):
                                ksl = slice(kb * 128, (kb + 1) * 128)
                                f, l = (bi == 0), (bi == nblk - 1)
                                for j in range(QG):
                                    ps = ps_s[cnt % 3]
                                    pt_ = pt[cnt % 3]
                                    cnt += 1
                                    qsl = slice(j * TQ, (j + 1) * TQ)
                                    _mm_group(P, ps, ps[:, 0:TQ], [(kn_[:, ksl], qn_[:, qsl]), (kr_[:, ksl], qr_[:, qsl])],
                                              [kn_, kr_, qn_, qr_])
                                    P.op(ACT, lambda e, ps=ps, pt_=pt_: e.activation(out=pt_[:], in_=ps[:, 0:TQ], func=AF.Exp, scale=scale),
                                         reads=[ps], writes=[pt_])
                                    po, pr = ps_o[j], ps_r[j]

                                    def acc(e, v_=v_, kb=kb, pt_=pt_, po=po, pr=pr, f=f, l=l):
                                        e.matmul(po[:, 0:TQ], v_[:, kb, :], pt_[:], start=f, stop=l)
                                        return e.matmul(pr[:, 0:TQ], ones[:], pt_[:], start=f, stop=l)
                                    P.op(PE, acc, reads=[v_, pt_, ones], writes=[po, pr])
                                bi += 1
                        for j in range(QG):
                            po, pr = ps_o[j], ps_r[j]
                            ao_ = ao[ac % 2]
                            ac += 1
                            P.op(DVE, lambda e, pr=pr: e.reciprocal(out=rinv[:], in_=pr[:, 0:TQ]), reads=[pr], writes=[rinv])
                            P.op(DVE, lambda e, po=po, ao_=ao_: e.tensor_tensor(out=ao_[:], in0=po[:, 0:TQ], in1=rinv[:], op=ALU.mult),
                                 reads=[po, rinv], writes=[ao_])
                            P.dma(SP, aoT.t[h * 128:(h + 1) * 128, q0 + j * TQ:q0 + (j + 1) * TQ], ao_[:], reads=[ao_], writes=[aoT])
                soff += slab
            P.flush()
        P.pes = None
        with ExitStack() as pes:
            P.pes = pes
            Bf = PostBufs(P, TT, 16)
            g_ts = [P.sb("g_t%d" % i, [128, 16], F32) for i in range(3)]
            for i in range(3):
                P.dma(SP, g_ts[i][:], gv.t[:, i, :], writes=[g_ts[i]])
            for ti in range(NT // TT):
                t0 = ti * TT
                post_block(P, Bf, 16, aoT, Wo_b, x2T, g_ts[0], g_ts[1], g_ts[2], Wfi_b, Wfo_b, t0)
                P.dma(SP, yT.t[:, t0:t0 + TT].rearrange("(c p) t -> p c t", p=128), Bf.xs[:], reads=[Bf.xs], writes=[yT])
            P.flush()
        P.pes = None
    return nc


def run_model(seq_lens, xs, norm_mix_pre, norm_mix_post, norm_ffn_pre, norm_ffn_post,
              gdn_w_in, gdn_conv_w, gdn_a_log, gdn_dt_bias, gdn_norm_w, gdn_w_out,
              mla_w_a, mla_q_a_norm, mla_w_q_b, mla_kv_a_norm, mla_w_kv_b, mla_w_o,
              ffn_w_in, ffn_w_out):
    cores = list(range(NCORE))
    xT_all = np.ascontiguousarray(xs.T)
    og = run_A(seq_lens, xT_all, norm_mix_pre, gdn_w_in, gdn_conv_w, gdn_a_log, gdn_dt_bias, gdn_norm_w)
    slabs = [L // NCORE for L in seq_lens]
    NT = sum(slabs)
    WB = prep_B_weights(norm_mix_post, norm_ffn_pre, norm_ffn_post, norm_mix_pre, gdn_w_out, ffn_w_in, ffn_w_out,
                        mla_w_a, mla_q_a_norm, mla_w_q_b, mla_kv_a_norm, mla_w_kv_b)
    toks = [_core_tokens(seq_lens, c) for c in cores]
    in_maps = []
    for c in cores:
        idx, pos = toks[c]
        C2, S2 = _rope_tabs(pos)
        m = dict(WB)
        m.update({"xT": np.ascontiguousarray(xs[idx].T), "ogT": np.ascontiguousarray(og[idx].T), "C2": C2, "S2": S2})
        in_maps.append(m)
    ncB = build_B(NT)
    resB = _launch(ncB, in_maps, "B").results
    del in_maps
    KNa = np.ascontiguousarray(np.concatenate([np.asarray(resB[c]["KN"]) for c in cores], axis=2))
    KRa = np.ascontiguousarray(np.concatenate([np.asarray(resB[c]["KR"]) for c in cores], axis=1))
    Va = np.ascontiguousarray(np.stack([np.asarray(resB[c]["V"]).reshape(NT // 128, 128, 16, 128).transpose(2, 1, 0, 3)
                                        for c in cores], axis=1))
    WC = {
        "Wo": lay_w(mla_w_o[0]), "Wfi": lay_w(ffn_w_in[1]), "Wfo": lay_w(ffn_w_out[1]),
        "gv": np.ascontiguousarray(np.stack([lay_g(norm_mix_post[1]), lay_g(norm_ffn_pre[1]), lay_g(norm_ffn_post[1])], 1)),
        "KNa": KNa, "KRa": KRa, "Va": Va,
    }
    in_maps = []
    for c in cores:
        m = dict(WC)
        m.update({"QN": np.asarray(resB[c]["QN"]), "QR": np.asarray(resB[c]["QR"]), "x2T": np.asarray(resB[c]["x2T"])})
        in_maps.append(m)
    ncC = build_C(slabs)
    resC = _launch(ncC, in_maps, "C").results
    y = np.empty((sum(seq_lens), D_MODEL), np.float32)
    for c in cores:
        y[toks[c][0]] = np.asarray(resC[c]["yT"]).T
    return y


def kernel(x_prompt, x_sample, norm_mix_pre, norm_mix_post, norm_ffn_pre, norm_ffn_post,
           gdn_w_in, gdn_conv_w, gdn_a_log, gdn_dt_bias, gdn_norm_w, gdn_w_out,
           mla_w_a, mla_q_a_norm, mla_w_q_b, mla_kv_a_norm, mla_w_kv_b, mla_w_o,
           ffn_w_in, ffn_w_out):
    a = [np.asarray(v, dtype=np.float32) for v in (
        x_prompt, x_sample, norm_mix_pre, norm_mix_post, norm_ffn_pre, norm_ffn_post,
        gdn_w_in, gdn_conv_w, gdn_a_log, gdn_dt_bias, gdn_norm_w, gdn_w_out,
        mla_w_a, mla_q_a_norm, mla_w_q_b, mla_kv_a_norm, mla_w_kv_b, mla_w_o, ffn_w_in, ffn_w_out)]
    xp, xsm = a[0], a[1]
    Bp, Lp, _ = xp.shape
    Bs, Ls, _ = xsm.shape
    seq_lens = [Lp] * Bp + [Ls] * Bs
    xs = np.concatenate([xp.reshape(Bp * Lp, D_MODEL), xsm.reshape(Bs * Ls, D_MODEL)], axis=0)
    y = run_model(seq_lens, xs, *a[2:])
    yp = y[:Bp * Lp].reshape(Bp, Lp, D_MODEL)
    ysm = y[Bp * Lp:].reshape(Bs, Ls, D_MODEL)
    return (np.ascontiguousarray(yp), np.ascontiguousarray(ysm))
```

```python
import numpy as np
import ml_dtypes
from contextlib import ExitStack
import concourse.bass as bass
import concourse.mybir as mybir
from concourse.bass_utils import run_bass_kernel_spmd

F32 = mybir.dt.float32
BF16 = mybir.dt.bfloat16
AF = mybir.ActivationFunctionType
ALU = mybir.AluOpType
NPBF = ml_dtypes.bfloat16

PE, ACT, DVE, POOL, SP = "pe", "act", "dve", "pool", "sp"
ENGS = (PE, ACT, DVE, POOL, SP)

D_MODEL = 2048
NCORE = 8
RMS_EPS = 1e-6
L2_EPS = 1e-6


class T:
    __slots__ = ("t", "w", "r", "name", "psum")

    def __init__(self, t, name="", psum=False):
        self.t = t
        self.w = {}
        self.r = {}
        self.name = name
        self.psum = psum

    def __getitem__(self, k):
        return self.t[k]


class Prog:
    NSLOT = 6

    def __init__(self, nc, es):
        self.nc = nc
        self.es = es
        self.ops = {e: [] for e in ENGS}
        self.cnt = {e: 0 for e in ENGS}
        self.sems = []
        self.esem = {}
        for e in ENGS:
            self.esem[e] = self._newsem("p_" + e)
        self.slots = {}
        self.slot_cnt = {}
        self.slot_next = {}
        for q in (SP, ACT, POOL):
            self.slots[q] = [self._newsem("d_%s%d" % (q, i)) for i in range(self.NSLOT)]
            self.slot_cnt[q] = [0] * self.NSLOT
            self.slot_next[q] = 0
        self.waited = {e: {} for e in ENGS}
        self.ninst = 0
        self.pes = None
        self.uid = 0

    def _newsem(self, name):
        h = self.es.enter_context(self.nc.semaphore(name))
        self.sems.append(h)
        return len(self.sems) - 1

    def sb(self, name, shape, dt):
        es = self.pes if self.pes is not None else self.es
        self.uid += 1
        name = "%s_%d" % (name, self.uid)
        return T(es.enter_context(self.nc.sbuf_tensor(name, list(shape), dt)), name)

    def ps(self, name, shape, dt=F32):
        es = self.pes if self.pes is not None else self.es
        self.uid += 1
        name = "%s_%d" % (name, self.uid)
        return T(es.enter_context(self.nc.psum_tensor(name, list(shape), dt)), name, psum=True)

    def dram(self, name, shape, dt, kind="Internal"):
        return T(self.nc.dram_tensor(name, list(shape), dt, kind=kind).ap(), name)

    def _deps(self, eng, reads, writes, is_dma):
        deps = {}
        own = None if is_dma else self.esem[eng]

        def add(d, skip_same):
            for s, v in d.items():
                if skip_same and s == own:
                    continue
                if deps.get(s, 0) < v:
                    deps[s] = v
        for t in reads:
            add(t.w, eng == PE)
            if t.psum:
                add(t.r, True)
        for t in writes:
            add(t.w, True)
            add(t.r, True)
        return deps

    def _waits(self, eng, deps):
        wl = []
        wd = self.waited[eng]
        for s, v in deps.items():
            if wd.get(s, 0) < v:
                wd[s] = v
                wl.append((s, v))
        return wl

    def _mark(self, ev, reads, writes):
        for t in reads:
            if t.r.get(ev[0], 0) < ev[1]:
                t.r[ev[0]] = ev[1]
        for t in writes:
            t.w = {ev[0]: ev[1]}
            t.r = {}

    def op(self, eng, fn, reads=(), writes=()):
        wl = self._waits(eng, self._deps(eng, reads, writes, False))
        self.cnt[eng] += 1
        ev = (self.esem[eng], self.cnt[eng])
        self.ops[eng].append((wl, fn, self.esem[eng], 1))
        self._mark(ev, reads, writes)
        return ev

    def dma(self, q, out, in_, reads=(), writes=(), **kw):
        k = self.slot_next[q]
        self.slot_next[q] = (k + 1) % self.NSLOT
        sem = self.slots[q][k]
        deps = self._deps(q, reads, writes, True)
        prev = 16 * self.slot_cnt[q][k]
        if prev > 0 and deps.get(sem, 0) < prev:
            deps[sem] = prev
        wl = self._waits(q, deps)
        self.slot_cnt[q][k] += 1
        ev = (sem, 16 * self.slot_cnt[q][k])

        def fn(e, out=out, in_=in_, kw=kw):
            return e.dma_start(out=out, in_=in_, **kw)
        self.ops[q].append((wl, fn, sem, 16))
        self._mark(ev, reads, writes)
        return ev

    def _all_events(self):
        ev = {}
        for e in ENGS:
            if self.cnt[e] > 0:
                ev[self.esem[e]] = self.cnt[e]
        for q in (SP, ACT, POOL):
            for k in range(self.NSLOT):
                if self.slot_cnt[q][k] > 0:
                    ev[self.slots[q][k]] = 16 * self.slot_cnt[q][k]
        return ev

    def flush(self, final=False):
        allev = self._all_events()
        sems = self.sems
        ops = self.ops
        waited = self.waited

        def run(e, name):
            for wl, fn, sem, inc in ops[name]:
                for s, v in wl:
                    e.wait_ge(sems[s], v)
                fn(e).then_inc(sems[sem], inc)
                self.ninst += 1 + len(wl)
            own = self.esem[name]
            for s, v in allev.items():
                if s == own:
                    continue
                if waited[name].get(s, 0) < v:
                    waited[name][s] = v
                    e.wait_ge(sems[s], v)
        with self.nc.Block() as block:
            @block.sync
            def _(e):
                run(e, SP)

            @block.tensor
            def _(e):
                run(e, PE)

            @block.scalar
            def _(e):
                run(e, ACT)

            @block.vector
            def _(e):
                run(e, DVE)

            @block.gpsimd
            def _(e):
                run(e, POOL)
        self.ops = {e: [] for e in ENGS}


def _mm_group(P, out_t, out_ap, pairs, reads, first=True, last=True):
    def fn(e, pairs=pairs, out_ap=out_ap):
        n = len(pairs)
        for i, (l, r) in enumerate(pairs):
            ins = e.matmul(out_ap, l, r, start=(first and i == 0), stop=(last and i == n - 1))
        return ins
    return P.op(PE, fn, reads=reads, writes=[out_t])


CH = 128
NLVL = 7


def _gdn_consts():
    i = np.arange(CH)
    s = i[:, None]
    c = i[None, :]
    f32 = np.zeros((CH, 36, CH), np.float32)
    f32[:, 0, :] = (s <= c)
    f32[:, 1, :] = (s >= c)
    f32[:, 2, :] = -1.0 * (c > s)
    f32[:, 3, :] = -1.0 * (c < s)
    f32[:, 4, :] = (c >= s)
    f32[:, 5, :] = (c <= s)
    for l in range(NLVL):
        b = 1 << l
        r = i[:, None]
        q = i[None, :]
        same = (r // (2 * b)) == (q // (2 * b))
        low = same & ((r // b) % 2 == 1) & ((q // b) % 2 == 0)
        f32[:, 6 + l, :] = low
        f32[:, 6 + NLVL + l, :] = low.T
    f32[:, 20, :] = np.eye(CH)
    f32[:, 21, :] = 1.0
    return f32


def build_A(seq_lens, TT=512, upto=9, debug=False):
    NTOK = sum(seq_lens)
    offs = np.concatenate([[0], np.cumsum(seq_lens)]).astype(int)
    nc = bass.Bass("TRN2", target_bir_lowering=False)
    with ExitStack() as es:
        P = Prog(nc, es)
        xT = P.dram("xT", [D_MODEL, NTOK], F32, kind="ExternalInput")
        gpre = P.dram("gpre", [128, 16], F32, kind="ExternalInput")
        Wqkv = P.dram("Wqkv", [D_MODEL, 1024], F32, kind="ExternalInput")
        Wz = P.dram("Wz", [D_MODEL, 512], F32, kind="ExternalInput")
        Wg = P.dram("Wg", [D_MODEL, 16], F32, kind="ExternalInput")
        convw = P.dram("convw", [128, 8, 5], F32, kind="ExternalInput")
        alog = P.dram("alog", [128, 8], F32, kind="ExternalInput")
        dtb = P.dram("dtb", [128, 8], F32, kind="ExternalInput")
        gnw = P.dram("gnw", [128, 128], F32, kind="ExternalInput")
        cst = P.dram("cst", [128, 36, 128], F32, kind="ExternalInput")
        og = P.dram("og", [NTOK, 512], BF16, kind="ExternalOutput")
        Wqkv_b = P.dram("Wqkv_b", [D_MODEL, 1024], BF16)
        Wz_b = P.dram("Wz_b", [D_MODEL, 512], BF16)
        Wg_b = P.dram("Wg_b", [D_MODEL, 16], BF16)
        dk = "ExternalOutput" if debug else "Internal"
        rawT = P.dram("rawT", [8, 128, NTOK], F32, kind=dk)
        zsil = P.dram("zsil", [NTOK, 512], F32, kind=dk)
        gates = P.dram("gates", [NTOK, 16], F32, kind=dk)
        qkT = P.dram("qkT", [4, 128, NTOK], BF16, kind=dk)
        ktok = P.dram("ktok", [NTOK, 2, 128], BF16, kind=dk)
        vtok = P.dram("vtok", [NTOK, 4, 128], BF16, kind=dk)
        osc = P.dram("osc", [2, NTOK, 512], F32, kind=dk)

        P.dma(POOL, Wqkv_b.t, Wqkv.t, reads=[Wqkv], writes=[Wqkv_b])
        P.dma(POOL, Wz_b.t, Wz.t, reads=[Wz], writes=[Wz_b])
        P.dma(POOL, Wg_b.t, Wg.t, reads=[Wg], writes=[Wg_b])

        with ExitStack() as pes:
            P.pes = pes
            wqkv = P.sb("wqkv", [128, 16, 1024], BF16)
            wz = P.sb("wz", [128, 16, 512], BF16)
            wg = P.sb("wg", [128, 16, 16], BF16)
            gt = P.sb("gt", [128, 16], F32)
            ones = P.sb("ones", [128, 128], BF16)
            alog_s = P.sb("alog_s", [128, 8], F32)
            nega = P.sb("nega", [128, 8], F32)
            dtb_s = P.sb("dtb_s", [128, 8], F32)
            xs = [P.sb("xs%d" % i, [128, 16, TT], F32) for i in range(2)]
            sq = P.sb("sq", [128, 16, TT], BF16)
            hb = P.sb("hb", [128, 16, TT], BF16)
            rstd = P.sb("rstd", [128, TT], F32)
            rcol = P.sb("rcol", [128, 4], F32)
            rawo = [P.sb("rawo%d" % i, [128, TT], F32) for i in range(2)]
            zo = [P.sb("zo%d" % i, [128, 512], F32) for i in range(2)]
            gsb = [P.sb("gsb%d" % i, [128, 16], F32) for i in range(2)]
            gtmp = P.sb("gtmp", [128, 8], F32)
            pss = P.ps("pss", [128, 512])
            psc = P.ps("psc", [128, 512])
            pm = [P.ps("pm%d" % i, [128, 512]) for i in range(4)]
            psg = P.ps("psg", [128, 512])

            P.dma(SP, wqkv[:], Wqkv_b.t.rearrange("(c p) m -> p c m", p=128), reads=[Wqkv_b], writes=[wqkv])
            P.dma(SP, wz[:], Wz_b.t.rearrange("(c p) m -> p c m", p=128), reads=[Wz_b], writes=[wz])
            P.dma(SP, wg[:], Wg_b.t.rearrange("(c p) m -> p c m", p=128), reads=[Wg_b], writes=[wg])
            P.dma(SP, gt[:], gpre.t, writes=[gt])
            P.dma(SP, alog_s[:], alog.t, writes=[alog_s])
            P.dma(SP, dtb_s[:], dtb.t, writes=[dtb_s])
            P.op(POOL, lambda e: e.memset(ones[:], 1.0), writes=[ones])
            P.op(ACT, lambda e: e.activation(out=nega[:], in_=alog_s[:], func=AF.Exp), reads=[alog_s], writes=[nega])
            P.op(DVE, lambda e: e.tensor_scalar(out=nega[:], in0=nega[:], scalar1=-1.0, scalar2=None, op0=ALU.mult),
                 reads=[nega], writes=[nega])

            ntile = NTOK // TT
            for ti in range(ntile):
                t0 = ti * TT
                x = xs[ti % 2]
                P.dma(SP, x[:], xT.t[:, t0:t0 + TT].rearrange("(c p) t -> p c t", p=128), writes=[x])
                P.op(ACT, lambda e, x=x: e.activation(out=sq[:], in_=x[:], func=AF.Square), reads=[x], writes=[sq])
                P.op(DVE, lambda e, x=x: e.tensor_tensor(out=hb[:], in0=x[:], in1=gt[:].unsqueeze(2).to_broadcast([128, 16, TT]),
                                                         op=ALU.mult), reads=[x, gt], writes=[hb])
                _mm_group(P, pss, pss[:, 0:TT], [(ones[:], sq[:, k, :]) for k in range(16)], [ones, sq])
                P.op(ACT, lambda e: e.activation(out=rstd[:], in_=pss[:, 0:TT], func=AF.Ln, bias=RMS_EPS, scale=1.0 / D_MODEL),
                     reads=[pss], writes=[rstd])
                P.op(ACT, lambda e: e.activation(out=rstd[:], in_=rstd[:], func=AF.Exp, scale=-0.5), reads=[rstd], writes=[rstd])
                nsub = TT // 128

                def colsum(e):
                    for sub in range(nsub):
                        for k in range(16):
                            ins = e.matmul(psc[:, sub:sub + 1], sq[:, k, sub * 128:(sub + 1) * 128], ones[:, 0:1],
                                           start=(k == 0), stop=(k == 15))
                    return ins
                P.op(PE, colsum, reads=[sq, ones], writes=[psc])
                P.op(ACT, lambda e: e.activation(out=rcol[:, 0:nsub], in_=psc[:, 0:nsub], func=AF.Ln, bias=RMS_EPS,
                                                 scale=1.0 / D_MODEL), reads=[psc], writes=[rcol])
                P.op(ACT, lambda e: e.activation(out=rcol[:, 0:nsub], in_=rcol[:, 0:nsub], func=AF.Exp, scale=-0.5),
                     reads=[rcol], writes=[rcol])
                for m in range(8):
                    p = pm[m % 4]
                    ro = rawo[m % 2]
                    _mm_group(P, p, p[:, 0:TT], [(wqkv[:, k, m * 128:(m + 1) * 128], hb[:, k, :]) for k in range(16)], [wqkv, hb])
                    P.op(DVE, lambda e, p=p, ro=ro: e.tensor_tensor(out=ro[:], in0=p[:, 0:TT], in1=rstd[:], op=ALU.mult),
                         reads=[p, rstd], writes=[ro])
                    P.dma(SP, rawT.t[m, :, t0:t0 + TT], ro[:], reads=[ro], writes=[rawT])
                for sub in range(nsub):
                    p = pm[sub % 4]
                    z = zo[sub % 2]
                    _mm_group(P, p, p[:, 0:512], [(hb[:, k, sub * 128:(sub + 1) * 128], wz[:, k, :]) for k in range(16)], [wz, hb])
                    P.op(ACT, lambda e, p=p, z=z, sub=sub: e.activation(out=z[:], in_=p[:, 0:512], func=AF.Silu,
                                                                        scale=rcol[:, sub:sub + 1]), reads=[p, rcol], writes=[z])
                    P.dma(SP, zsil.t[t0 + sub * 128:t0 + (sub + 1) * 128, :], z[:], reads=[z], writes=[zsil])
                    g = gsb[sub % 2]
                    _mm_group(P, psg, psg[:, 0:16], [(hb[:, k, sub * 128:(sub + 1) * 128], wg[:, k, :]) for k in range(16)], [wg, hb])
                    P.op(ACT, lambda e, g=g, sub=sub: e.activation(out=g[:, 0:8], in_=psg[:, 0:8], func=AF.Sigmoid,
                                                                   scale=rcol[:, sub:sub + 1]), reads=[psg, rcol], writes=[g])
                    P.op(DVE, lambda e, sub=sub: e.scalar_tensor_tensor(out=gtmp[:], in0=psg[:, 8:16], scalar=rcol[:, sub:sub + 1],
                                                                        in1=dtb_s[:], op0=ALU.mult, op1=ALU.add),
                         reads=[psg, rcol, dtb_s], writes=[gtmp])
                    P.op(ACT, lambda e: e.activation(out=gtmp[:], in_=gtmp[:], func=AF.Exp), reads=[gtmp], writes=[gtmp])
                    P.op(ACT, lambda e: e.activation(out=gtmp[:], in_=gtmp[:], func=AF.Ln, bias=1.0), reads=[gtmp], writes=[gtmp])
                    P.op(DVE, lambda e, g=g: e.tensor_tensor(out=g[:, 8:16], in0=gtmp[:], in1=nega[:], op=ALU.mult),
                         reads=[gtmp, nega], writes=[g])
                    P.dma(SP, gates.t[t0 + sub * 128:t0 + (sub + 1) * 128, :], g[:], reads=[g], writes=[gates])
            P.flush()
        P.pes = None

        if upto < 2:
            return nc
        with ExitStack() as pes:
            P.pes = pes
            cw = P.sb("cw", [128, 8, 5], F32)
            ones = P.sb("ones", [128, 128], BF16)
            identb = P.sb("identb", [128, 128], BF16)
            identf = P.sb("identf", [128, 128], F32)
            raw = [P.sb("raw%d" % i, [128, TT + 4], F32) for i in range(3)]
            acc = [P.sb("acc%d" % i, [128, TT], F32) for i in range(2)]
            sqb = P.sb("sqb", [128, TT], BF16)
            rn = P.sb("rn", [128, TT], F32)
            nb = [P.sb("nb%d" % i, [128, TT], BF16) for i in range(2)]
            tk = [P.sb("tk%d" % i, [128, TT // 128, 128], BF16) for i in range(2)]
            pss = P.ps("pss", [128, 512])
            ptr = [P.ps("ptr%d" % i, [128, 8, 128], BF16) for i in range(2)]
            P.dma(SP, cw[:], convw.t, writes=[cw])
            P.dma(SP, identf[:], cst.t[:, 20, :], writes=[identf])
            P.op(POOL, lambda e: e.memset(ones[:], 1.0), writes=[ones])
            P.op(DVE, lambda e: e.tensor_copy(out=identb[:], in_=identf[:]), reads=[identf], writes=[identb])
            nsub = TT // 128
            cnt = 0
            for si, L in enumerate(seq_lens):
                for ti in range(L // TT):
                    l0 = ti * TT
                    t0 = offs[si] + l0
                    for j in range(8):
                        r = raw[cnt % 3]
                        a = acc[cnt % 2]
                        o = nb[cnt % 2]
                        tkk = tk[cnt % 2]
                        pt = ptr[cnt % 2]
                        ceng = DVE
                        cnt += 1
                        lo = 2 if l0 == 0 else 0
                        hi = TT + 2 if l0 + TT == L else TT + 4
                        if lo or hi < TT + 4:
                            P.op(POOL, lambda e, r=r: e.memset(r[:], 0.0), writes=[r])
                        P.dma(SP, r[:, lo:hi], rawT.t[j, :, t0 - 2 + lo:t0 - 2 + hi], reads=[rawT], writes=[r])
                        P.op(ceng, lambda e, r=r, a=a, j=j: e.tensor_scalar(out=a[:], in0=r[:, 0:TT], scalar1=cw[:, j, 0:1],
                                                                            scalar2=None, op0=ALU.mult), reads=[r, cw], writes=[a])
                        for tap in range(1, 5):
                            P.op(ceng, lambda e, r=r, a=a, j=j, tap=tap: e.scalar_tensor_tensor(
                                out=a[:], in0=r[:, tap:tap + TT], scalar=cw[:, j, tap:tap + 1], in1=a[:],
                                op0=ALU.mult, op1=ALU.add), reads=[r, cw, a], writes=[a])
                        P.op(ACT, lambda e, a=a: e.activation(out=a[:], in_=a[:], func=AF.Silu), reads=[a], writes=[a])
                        if j < 4:
                            P.op(ACT, lambda e, a=a: e.activation(out=sqb[:], in_=a[:], func=AF.Square), reads=[a], writes=[sqb])
                            _mm_group(P, pss, pss[:, 0:TT], [(ones[:], sqb[:])], [ones, sqb])
                            P.op(ACT, lambda e: e.activation(out=rn[:], in_=pss[:, 0:TT], func=AF.Ln, bias=L2_EPS),
                                 reads=[pss], writes=[rn])
                            sc = -0.5
                            P.op(ACT, lambda e: e.activation(out=rn[:], in_=rn[:], func=AF.Exp, scale=-0.5), reads=[rn], writes=[rn])
                            if j < 2:
                                qs = float(128 ** -0.5)
                                P.op(DVE, lambda e, a=a, o=o, qs=qs: e.scalar_tensor_tensor(
                                    out=o[:], in0=a[:], scalar=qs, in1=rn[:], op0=ALU.mult, op1=ALU.mult),
                                    reads=[a, rn], writes=[o])
                            else:
                                P.op(DVE, lambda e, a=a, o=o: e.tensor_tensor(out=o[:], in0=a[:], in1=rn[:], op=ALU.mult),
                                     reads=[a, rn], writes=[o])
                            P.dma(SP, qkT.t[j, :, t0:t0 + TT], o[:], reads=[o], writes=[qkT])
                        else:
                            P.op(DVE, lambda e, a=a, o=o: e.tensor_copy(out=o[:], in_=a[:]), reads=[a], writes=[o])
                        if j >= 2:
                            def trs(e, o=o, pt=pt):
                                for sub in range(nsub):
                                    ins = e.transpose(pt[:, sub, :], o[:, sub * 128:(sub + 1) * 128], identb[:])
                                return ins
                            P.op(PE, trs, reads=[o, identb], writes=[pt])
                            P.op(ACT, lambda e, pt=pt, tkk=tkk: e.activation(out=tkk[:], in_=pt[:, 0:nsub, :], func=AF.Copy),
                                 reads=[pt], writes=[tkk])
                            if j < 4:
                                dst = ktok.t[t0:t0 + TT, j - 2, :]
                                dt_ = ktok
                            else:
                                dst = vtok.t[t0:t0 + TT, j - 4, :]
                                dt_ = vtok
                            P.dma(SP, dst.rearrange("(s p) d -> p s d", p=128), tkk[:], reads=[tkk], writes=[dt_])
            P.flush()
        P.pes = None

        if upto < 3:
            return nc
        with ExitStack() as pes:
            P.pes = pes
            cs = P.sb("cs", [128, 36, 128], F32)
            onesf = P.sb("onesf", [128, 128], F32)
            identb = P.sb("identb", [128, 128], BF16)
            P.dma(SP, cs[:], cst.t, writes=[cs])
            P.op(POOL, lambda e: e.memset(onesf[:], 1.0), writes=[onesf])
            P.op(DVE, lambda e: e.tensor_copy(out=identb[:], in_=cs[:, 20, :]), reads=[cs], writes=[identb])
            pkq = P.ps("pkq", [128, 4, 128])
            pg = P.ps("pg", [128, 4, 128])
            pc = P.ps("pc", [128, 512])
            pT_raw = P.ps("pT", [128, 2, 512], BF16)
            pTd = [pT_raw, pT_raw]
            NST = 3 + 3 * NLVL + 3
            HLAG = (NST + 1) // 2
            S = {}
            SD = {}
            for d in range(2):
                SD[d] = dict(St=P.sb("St%d" % d, [128, 4, 128], F32), Sb=P.sb("Sb%d" % d, [128, 4, 128], BF16),
                             Stmp=P.sb("Stmp%d" % d, [128, 4, 128], F32))
            for d in range(4):
                pbank = P.ps("pb%d" % d, [128, 4, 128])
                S[d] = dict(
                    pX=pbank, pY=pbank,
                    kT=P.sb("kT%d" % d, [128, 2, 128], BF16), qT=P.sb("qT%d" % d, [128, 2, 128], BF16),
                    kt=P.sb("kt%d" % d, [128, 2, 128], BF16), vt=P.sb("vt%d" % d, [128, 4, 128], BF16),
                    gt=P.sb("gt%d" % d, [128, 16], F32),
                    Ug=P.sb("Ug%d" % d, [128, 4, 128], F32), gcc=P.sb("gcc%d" % d, [128, 4], F32),
                    Dm=P.sb("Dm%d" % d, [128, 4, 128], F32), Dsn=P.sb("Dsn%d" % d, [128, 4, 128], F32),
                    Dt=P.sb("Dt%d" % d, [128, 4, 128], F32),
                    LpT=P.sb("LpT%d" % d, [128, 4, 128], BF16), QKm=P.sb("QKm%d" % d, [128, 4, 128], BF16),
                    Pm=P.sb("Pm%d" % d, [128, 4, 128], BF16), Qm=P.sb("Qm%d" % d, [128, 4, 128], BF16),
                    X=P.sb("X%d" % d, [128, 4, 128], BF16), tmp=P.sb("tmp%d" % d, [128, 4, 128], BF16),
                    egc=P.sb("egc%d" % d, [128, 4], F32), Rg=P.sb("Rg%d" % d, [128, 4, 128], BF16),
                    nwT=P.sb("nwT%d" % d, [128, 4, 128], BF16), gl=P.sb("gl%d" % d, [128, 4], F32),
                    kd=P.sb("kd%d" % d, [128, 4], F32), egl=P.sb("egl%d" % d, [128, 4], F32),
                    vnb=P.sb("vnb%d" % d, [128, 4, 128], BF16), vns=P.sb("vns%d" % d, [128, 4, 128], BF16),
                    Asb=P.sb("Asb%d" % d, [128, 4, 128], F32), osb=P.sb("osb%d" % d, [128, 4, 128], F32),
                )

            def bc_h(ap2):
                return ap2.unsqueeze(1).to_broadcast([128, 4, 128])

            def bc_c(ap2):
                return ap2.unsqueeze(2).to_broadcast([128, 4, 128])

            def unit_stages(d, slot, t0):
                B = dict(S[slot])
                B.update(SD[d])
                last = (CH - 1) if d == 0 else 0
                st = []

                def s_load():
                    P.dma(SP, B["kT"][:], qkT.t[2:4, :, t0:t0 + CH].rearrange("h p c -> p h c"), reads=[qkT], writes=[B["kT"]])
                    P.dma(SP, B["qT"][:], qkT.t[0:2, :, t0:t0 + CH].rearrange("h p c -> p h c"), reads=[qkT], writes=[B["qT"]])
                    P.dma(SP, B["kt"][:], ktok.t[t0:t0 + CH, :, :], reads=[ktok], writes=[B["kt"]])
                    P.dma(SP, B["vt"][:], vtok.t[t0:t0 + CH, :, :], reads=[vtok], writes=[B["vt"]])
                    P.dma(SP, B["gt"][:], gates.t[t0:t0 + CH, :], reads=[gates], writes=[B["gt"]])
                st.append(s_load)

                def s_gates():
                    g = B["gt"]
                    for h in range(4):
                        P.op(POOL, lambda e, h=h: e.tensor_scalar(out=B["Ug"][:, h, :], in0=cs[:, d, :],
                                                                  scalar1=g[:, 8 + 4 * d + h:9 + 4 * d + h], scalar2=None,
                                                                  op0=ALU.mult), reads=[cs, g], writes=[B["Ug"]])
                    _mm_group(P, pg, pg[:].rearrange("p h c -> p (h c)"),
                              [(onesf[:], B["Ug"][:].rearrange("p h c -> p (h c)"))], [onesf, B["Ug"]])
                    _mm_group(P, pc, pc[:, 0:4], [(cs[:, d, :], g[:, 8 + 4 * d:12 + 4 * d])], [cs, g])
                    P.op(ACT, lambda e: e.activation(out=B["gcc"][:], in_=pc[:, 0:4], func=AF.Copy), reads=[pc], writes=[B["gcc"]])
                    P.op(DVE, lambda e: e.tensor_copy(out=B["gl"][:], in_=pg[:, :, last]), reads=[pg], writes=[B["gl"]])
                    for h in range(4):
                        P.op(DVE, lambda e, h=h: e.tensor_scalar(out=B["Dm"][:, h, :], in0=pg[:, h, :],
                                                                 scalar1=B["gcc"][:, h:h + 1], scalar2=0.0,
                                                                 op0=ALU.subtract, op1=ALU.min),
                             reads=[pg, B["gcc"]], writes=[B["Dm"]])
                    P.op(ACT, lambda e: e.activation(out=B["Dm"][:], in_=B["Dm"][:], func=AF.Exp), reads=[B["Dm"]], writes=[B["Dm"]])
                    P.op(POOL, lambda e: e.tensor_tensor(out=B["Dsn"][:], in0=B["Dm"][:], in1=bc_h(cs[:, 2 + d, :]), op=ALU.mult),
                         reads=[B["Dm"], cs], writes=[B["Dsn"]])
                    P.op(POOL, lambda e: e.tensor_tensor(out=B["Dt"][:], in0=B["Dm"][:], in1=bc_h(cs[:, 4 + d, :]), op=ALU.mult),
                         reads=[B["Dm"], cs], writes=[B["Dt"]])
                    P.op(ACT, lambda e: e.activation(out=B["egc"][:], in_=B["gcc"][:], func=AF.Exp), reads=[B["gcc"]], writes=[B["egc"]])
                    P.op(DVE, lambda e: e.tensor_tensor(out=B["kd"][:], in0=B["gl"][:], in1=B["gcc"][:], op=ALU.subtract),
                         reads=[B["gl"], B["gcc"]], writes=[B["kd"]])
                    P.op(ACT, lambda e: e.activation(out=B["kd"][:], in_=B["kd"][:], func=AF.Exp), reads=[B["kd"]], writes=[B["kd"]])
                    P.op(ACT, lambda e: e.activation(out=B["egl"][:], in_=B["gl"][:], func=AF.Exp), reads=[B["gl"]], writes=[B["egl"]])
                st.append(s_gates)

                def s_kk():
                    def fn(e):
                        for hq in range(2):
                            e.matmul(pkq[:, hq, :], B["kT"][:, hq, :], B["kT"][:, hq, :], start=True, stop=True)
                        for hq in range(2):
                            ins = e.matmul(pkq[:, 2 + hq, :], B["kT"][:, hq, :], B["qT"][:, hq, :], start=True, stop=True)
                        return ins
                    P.op(PE, fn, reads=[B["kT"], B["qT"]], writes=[pkq])
                    for hq in range(2):
                        P.op(DVE, lambda e, hq=hq: e.tensor_tensor(
                            out=B["LpT"][:, 2 * hq:2 * hq + 2, :],
                            in0=pkq[:, hq, :].unsqueeze(1).to_broadcast([128, 2, 128]),
                            in1=B["Dsn"][:, 2 * hq:2 * hq + 2, :], op=ALU.mult), reads=[pkq, B["Dsn"]], writes=[B["LpT"]])
                        P.op(DVE, lambda e, hq=hq: e.tensor_tensor(
                            out=B["QKm"][:, 2 * hq:2 * hq + 2, :],
                            in0=pkq[:, 2 + hq, :].unsqueeze(1).to_broadcast([128, 2, 128]),
                            in1=B["Dt"][:, 2 * hq:2 * hq + 2, :], op=ALU.mult), reads=[pkq, B["Dt"]], writes=[B["QKm"]])
                    bsl = B["gt"][:, 4 * d:4 * d + 4]
                    P.op(POOL, lambda e: e.tensor_tensor(out=B["Pm"][:], in0=bc_h(cs[:, 20, :]), in1=bc_c(bsl), op=ALU.mult),
                         reads=[cs, B["gt"]], writes=[B["Pm"]])
                    P.op(POOL, lambda e: e.tensor_tensor(out=B["Qm"][:], in0=bc_h(cs[:, 20, :]), in1=bc_c(bsl), op=ALU.mult),
                         reads=[cs, B["gt"]], writes=[B["Qm"]])
                st.append(s_kk)

                for l in range(NLVL):
                    def s_x(l=l):
                        def fn(e):
                            for h in range(4):
                                ins = e.matmul(B["pX"][:, h, :], B["LpT"][:, h, :], B["Pm"][:, h, :], start=True, stop=True)
                            return ins
                        P.op(PE, fn, reads=[B["LpT"], B["Pm"]], writes=[B["pX"]])
                        P.op(ACT, lambda e: e.activation(out=B["X"][:], in_=B["pX"][:], func=AF.Copy), reads=[B["pX"]], writes=[B["X"]])
                    st.append(s_x)

                    def s_y(l=l):
                        def fn(e):
                            for h in range(4):
                                ins = e.matmul(B["pY"][:, h, :], B["Qm"][:, h, :], B["X"][:, h, :], start=True, stop=True)
                            return ins
                        P.op(PE, fn, reads=[B["Qm"], B["X"]], writes=[B["pY"]])
                        mk = cs[:, 6 + d * NLVL + l, :]
                        P.op(DVE, lambda e: e.tensor_tensor(out=B["tmp"][:], in0=B["pY"][:], in1=bc_h(mk), op=ALU.mult),
                             reads=[B["pY"], cs], writes=[B["tmp"]])
                        P.op(POOL, lambda e: e.tensor_tensor(out=B["Pm"][:], in0=B["Pm"][:], in1=B["tmp"][:], op=ALU.add),
                             reads=[B["Pm"], B["tmp"]], writes=[B["Pm"]])
                    st.append(s_y)

                    def s_t(l=l):
                        def fn(e):
                            for h in range(4):
                                ins = e.transpose(pTd[d][:, d, h * 128:(h + 1) * 128], B["Pm"][:, h, :], identb[:])
                            return ins
                        P.op(PE, fn, reads=[B["Pm"], identb], writes=[pTd[d]])
                        P.op(ACT, lambda e: e.activation(out=B["Qm"][:].rearrange("p h c -> p (h c)"), in_=pTd[d][:, d, :], func=AF.Copy),
                             reads=[pTd[d]], writes=[B["Qm"]])
                    st.append(s_t)

                def s_scan1():
                    P.op(DVE, lambda e: e.tensor_tensor(out=B["Rg"][:], in0=B["Qm"][:], in1=bc_c(B["egc"][:]), op=ALU.mult),
                         reads=[B["Qm"], B["egc"]], writes=[B["Rg"]])

                    def fn(e):
                        for h in range(4):
                            ins = e.matmul(B["pX"][:, h, :], B["kt"][:, h // 2, :], B["Rg"][:, h, :], start=True, stop=True)
                        return ins
                    P.op(PE, fn, reads=[B["kt"], B["Rg"]], writes=[B["pX"]])
                    P.op(ACT, lambda e: e.activation(out=B["nwT"][:], in_=B["pX"][:], func=AF.Copy, scale=-1.0),
                         reads=[B["pX"]], writes=[B["nwT"]])
                st.append(s_scan1)

                def s_scan2():
                    def fn(e):
                        for h in range(4):
                            e.matmul(B["pY"][:, h, :], B["Qm"][:, h, :], B["vt"][:, h, :], start=True, stop=False)
                            ins = e.matmul(B["pY"][:, h, :], B["nwT"][:, h, :], B["Sb"][:, h, :], start=False, stop=True)
                        return ins
                    P.op(PE, fn, reads=[B["Qm"], B["vt"], B["nwT"], B["Sb"]], writes=[B["pY"]])
                    P.op(ACT, lambda e: e.activation(out=B["vnb"][:], in_=B["pY"][:], func=AF.Copy), reads=[B["pY"]], writes=[B["vnb"]])
                    P.op(DVE, lambda e: e.tensor_tensor(out=B["vns"][:], in0=B["pY"][:], in1=bc_c(B["kd"][:]), op=ALU.mult),
                         reads=[B["pY"], B["kd"]], writes=[B["vns"]])

                    def fa(e):
                        for h in range(4):
                            ins = e.matmul(B["pX"][:, h, :], B["qT"][:, h // 2, :], B["Sb"][:, h, :], start=True, stop=True)
                        return ins
                    P.op(PE, fa, reads=[B["qT"], B["Sb"]], writes=[B["pX"]])
                    P.op(DVE, lambda e: e.tensor_tensor(out=B["Asb"][:], in0=B["pX"][:], in1=bc_c(B["egc"][:]), op=ALU.mult),
                         reads=[B["pX"], B["egc"]], writes=[B["Asb"]])
                st.append(s_scan2)

                def s_scan3():
                    def fb(e):
                        for h in range(4):
                            ins = e.matmul(B["pY"][:, h, :], B["QKm"][:, h, :], B["vnb"][:, h, :], start=True, stop=True)
                        return ins
                    P.op(PE, fb, reads=[B["QKm"], B["vnb"]], writes=[B["pY"]])
                    P.op(DVE, lambda e: e.tensor_tensor(out=B["osb"][:], in0=B["pY"][:], in1=B["Asb"][:], op=ALU.add),
                         reads=[B["pY"], B["Asb"]], writes=[B["osb"]])
                    P.dma(SP, osc.t[d, t0:t0 + CH, :], B["osb"][:].rearrange("p h c -> p (h c)"), reads=[B["osb"]], writes=[osc])

                    def fs(e):
                        for h in range(4):
                            ins = e.matmul(B["pX"][:, h, :], B["kt"][:, h // 2, :], B["vns"][:, h, :], start=True, stop=True)
                        return ins
                    P.op(PE, fs, reads=[B["kt"], B["vns"]], writes=[B["pX"]])
                    P.op(POOL, lambda e: e.tensor_tensor(out=B["Stmp"][:], in0=B["St"][:], in1=bc_c(B["egl"][:]), op=ALU.mult),
                         reads=[B["St"], B["egl"]], writes=[B["Stmp"]])
                    P.op(DVE, lambda e: e.tensor_tensor(out=B["St"][:], in0=B["pX"][:], in1=B["Stmp"][:], op=ALU.add),
                         reads=[B["pX"], B["Stmp"]], writes=[B["St"]])
                    P.op(ACT, lambda e: e.activation(out=B["Sb"][:], in_=B["St"][:], func=AF.Copy), reads=[B["St"]], writes=[B["Sb"]])
                st.append(s_scan3)
                return st

            for si, L in enumerate(seq_lens):
                nch = L // CH
                for d in range(2):
                    P.op(POOL, lambda e, d=d: e.memset(SD[d]["St"][:], 0.0), writes=[SD[d]["St"]])
                    P.op(POOL, lambda e, d=d: e.memset(SD[d]["Sb"][:], 0.0), writes=[SD[d]["Sb"]])
                live = {}
                for step in range((nch - 1) * HLAG + NST):
                    for d in range(2):
                        n_new = step // HLAG
                        if step % HLAG == 0 and n_new < nch:
                            ci = n_new if d == 0 else nch - 1 - n_new
                            live[(d, n_new)] = unit_stages(d, d * 2 + n_new % 2, offs[si] + ci * CH)
                        for n in (n_new - 1, n_new):
                            k = step - n * HLAG
                            if n >= 0 and n < nch and 0 <= k < NST:
                                live[(d, n)][k]()
                                if k == NST - 1:
                                    del live[(d, n)]
            P.flush()
        P.pes = None

        if upto < 4:
            return nc
        with ExitStack() as pes:
            P.pes = pes
            gn = P.sb("gn", [128, 128], F32)
            P.dma(SP, gn[:], gnw.t, writes=[gn])
            of = [P.sb("of%d" % i, [128, 4, 128], F32) for i in range(2)]
            ob = [P.sb("ob%d" % i, [128, 4, 128], F32) for i in range(2)]
            zz = [P.sb("zz%d" % i, [128, 4, 128], F32) for i in range(2)]
            sqj = P.sb("sqj", [128, 128], F32)
            ssum = P.sb("ssum", [128, 4], F32)
            oo = [P.sb("oo%d" % i, [128, 4, 128], BF16) for i in range(2)]
            for bi in range(NTOK // 128):
                t0 = bi * 128
                a, b, z, o = of[bi % 2], ob[bi % 2], zz[bi % 2], oo[bi % 2]
                P.dma(SP, a[:].rearrange("p h c -> p (h c)"), osc.t[0, t0:t0 + 128, :], reads=[osc], writes=[a])
                P.dma(SP, b[:].rearrange("p h c -> p (h c)"), osc.t[1, t0:t0 + 128, :], reads=[osc], writes=[b])
                P.dma(SP, z[:].rearrange("p h c -> p (h c)"), zsil.t[t0:t0 + 128, :], reads=[zsil], writes=[z])
                P.op(DVE, lambda e, a=a, b=b: e.tensor_tensor(out=a[:], in0=a[:], in1=b[:], op=ALU.add), reads=[a, b], writes=[a])
                P.op(POOL, lambda e: e.memset(ssum[:], 0.0), writes=[ssum])
                for h in range(4):
                    P.op(ACT, lambda e, a=a, h=h: e.activation(out=sqj[:], in_=a[:, h, :], func=AF.Square,
                                                               accum_out=ssum[:, h:h + 1]), reads=[a], writes=[sqj, ssum])
                P.op(ACT, lambda e: e.activation(out=ssum[:], in_=ssum[:], func=AF.Ln, bias=RMS_EPS, scale=1.0 / 128),
                     reads=[ssum], writes=[ssum])
                P.op(ACT, lambda e: e.activation(out=ssum[:], in_=ssum[:], func=AF.Exp, scale=-0.5), reads=[ssum], writes=[ssum])
                P.op(POOL, lambda e, z=z: e.tensor_tensor(out=z[:], in0=z[:], in1=gn[:].unsqueeze(1).to_broadcast([128, 4, 128]),
                                                          op=ALU.mult), reads=[z, gn], writes=[z])
                P.op(DVE, lambda e, a=a: e.tensor_tensor(out=a[:], in0=a[:], in1=ssum[:].unsqueeze(2).to_broadcast([128, 4, 128]),
                                                         op=ALU.mult), reads=[a, ssum], writes=[a])
                P.op(DVE, lambda e, a=a, z=z, o=o: e.tensor_tensor(out=o[:], in0=a[:], in1=z[:], op=ALU.mult),
                     reads=[a, z], writes=[o])
                P.dma(SP, og.t[t0:t0 + 128, :], o[:].rearrange("p h c -> p (h c)"), reads=[o], writes=[og])
            P.flush()
        P.pes = None
    return nc


GDN_QK_HEADS = 16
GDN_V_HEADS = 32
GDN_Q_DIM = 2048
GDN_V_DIM = 4096
GDN_CONV_DIM = 8192


def prep_A(c, xT_all, norm_mix_pre, gdn_w_in, gdn_conv_w, gdn_a_log, gdn_dt_bias, gdn_norm_w, cst):
    w_in = gdn_w_in[0]
    qh = [2 * c, 2 * c + 1]
    vh = [4 * c + i for i in range(4)]
    qcols = np.concatenate([np.arange(h * 128, (h + 1) * 128) for h in qh])
    kcols = GDN_Q_DIM + qcols
    vcols = 2 * GDN_Q_DIM + np.concatenate([np.arange(h * 128, (h + 1) * 128) for h in vh])
    zcols = GDN_CONV_DIM + np.concatenate([np.arange(h * 128, (h + 1) * 128) for h in vh])
    o_b = GDN_CONV_DIM + GDN_V_DIM
    o_a = o_b + 2 * GDN_V_HEADS
    bcols = np.array([o_b + d * GDN_V_HEADS + h for d in range(2) for h in vh])
    acols = np.array([o_a + d * GDN_V_HEADS + h for d in range(2) for h in vh])
    conv_cols = np.concatenate([qcols, kcols, vcols])
    cw = gdn_conv_w[0][:, conv_cols]
    cw = np.ascontiguousarray(cw.reshape(5, 8, 128).transpose(2, 1, 0))
    al = np.array([gdn_a_log[0, d, h] for d in range(2) for h in vh], np.float32)
    db = np.array([gdn_dt_bias[0, d, h] for d in range(2) for h in vh], np.float32)
    return {
        "xT": xT_all,
        "gpre": np.ascontiguousarray(norm_mix_pre[0].reshape(16, 128).T),
        "Wqkv": np.ascontiguousarray(w_in[:, conv_cols]),
        "Wz": np.ascontiguousarray(w_in[:, zcols]),
        "Wg": np.ascontiguousarray(w_in[:, np.concatenate([bcols, acols])]),
        "convw": cw,
        "alog": np.ascontiguousarray(np.broadcast_to(al[None, :], (128, 8))),
        "dtb": np.ascontiguousarray(np.broadcast_to(db[None, :], (128, 8))),
        "gnw": np.ascontiguousarray(np.broadcast_to(gdn_norm_w[0][None, :], (128, 128))),
        "cst": cst,
    }


_PROF = []


def _launch(nc, in_maps, tag):
    res = run_bass_kernel_spmd(nc, in_maps, core_ids=list(range(NCORE)))
    t = getattr(res, "exec_time_ns", None)
    if t is not None:
        _PROF.append((tag, t))
    return res


def run_A(seq_lens, xT_all, norm_mix_pre, gdn_w_in, gdn_conv_w, gdn_a_log, gdn_dt_bias, gdn_norm_w):
    cst = _gdn_consts()
    nc = build_A(seq_lens)
    in_maps = [prep_A(c, xT_all, norm_mix_pre, gdn_w_in, gdn_conv_w, gdn_a_log, gdn_dt_bias, gdn_norm_w, cst)
               for c in range(NCORE)]
    res = _launch(nc, in_maps, "A")
    return np.concatenate([np.asarray(res.results[c]["og"]) for c in range(NCORE)], axis=1)


def lay_w(W):
    K, M = W.shape
    return np.ascontiguousarray(W.reshape(K // 128, 128, M // 128, 128).transpose(2, 1, 0, 3))


def lay_g(g):
    return np.ascontiguousarray(g.reshape(-1, 128).T)


class PostBufs:
    def __init__(self, P, TT, Kc):
        self.TT = TT
        self.xs = P.sb("xs", [128, 16, TT], F32)
        self.at = P.sb("at", [128, Kc, TT], BF16)
        self.mT = P.sb("mT", [128, 16, TT], F32)
        self.sq = P.sb("sq", [128, 16, TT], BF16)
        self.hb = P.sb("hb", [128, 16, TT], BF16)
        self.hid = P.sb("hid", [128, 64, TT], BF16)
        self.rstd = P.sb("rstd", [128, TT], F32)
        self.r2 = P.sb("r2", [128, TT], F32)
        self.tmpf = [P.sb("tmpf%d" % i, [128, TT], F32) for i in range(2)]
        self.wA = [P.sb("wA%d" % i, [128, Kc, 128], BF16) for i in range(2)]
        self.wI = [P.sb("wI%d" % i, [128, 16, 128], BF16) for i in range(3)]
        self.wO = [P.sb("wO%d" % i, [128, 32, 128], BF16) for i in range(2)]
        self.ones = P.sb("ones", [128, 128], BF16)
        self.pacc = [P.ps("pacc%d" % i, [128, 512]) for i in range(4)]
        self.pss = P.ps("pss", [128, 512])
        self.pi = 0
        P.op(POOL, lambda e: e.memset(self.ones[:], 1.0), writes=[self.ones])

    def nextp(self):
        p = self.pacc[self.pi % 4]
        self.pi += 1
        return p


class MlaBufs:
    def __init__(self, P, TT):
        self.TT = TT
        self.xs = P.sb("xs", [128, 16, TT], F32)
        self.sq = P.sb("sq", [128, 16, TT], BF16)
        self.hb = P.sb("hb", [128, 16, TT], BF16)
        self.rstd = P.sb("rstd", [128, TT], F32)
        self.ones = P.sb("ones", [128, 128], BF16)
        self.pacc = [P.ps("pacc%d" % i, [128, 512]) for i in range(4)]
        self.pss = P.ps("pss", [128, 512])
        self.pi = 0
        P.op(POOL, lambda e: e.memset(self.ones[:], 1.0), writes=[self.ones])

    def nextp(self):
        p = self.pacc[self.pi % 4]
        self.pi += 1
        return p


def _rms_rstd(P, Bf, src_t, nch, dim, out_t):
    TT = Bf.TT
    _mm_group(P, Bf.pss, Bf.pss[:, 0:TT], [(Bf.ones[:], Bf.sq[:, k, :]) for k in range(nch)], [Bf.ones, Bf.sq])
    P.op(ACT, lambda e: e.activation(out=out_t[:], in_=Bf.pss[:, 0:TT], func=AF.Ln, bias=RMS_EPS, scale=1.0 / dim),
         reads=[Bf.pss], writes=[out_t])
    P.op(ACT, lambda e: e.activation(out=out_t[:], in_=out_t[:], func=AF.Exp, scale=-0.5), reads=[out_t], writes=[out_t])


def _norm_residual(P, Bf, g_t):
    TT = Bf.TT
    P.op(POOL, lambda e: e.tensor_tensor(out=Bf.sq[:], in0=Bf.mT[:], in1=Bf.mT[:], op=ALU.mult), reads=[Bf.mT], writes=[Bf.sq])
    _rms_rstd(P, Bf, Bf.mT, 16, D_MODEL, Bf.rstd)
    for mc in range(16):
        tf = Bf.tmpf[mc % 2]
        P.op(POOL, lambda e, mc=mc, tf=tf: e.tensor_tensor(out=tf[:], in0=Bf.mT[:, mc, :], in1=Bf.rstd[:], op=ALU.mult),
             reads=[Bf.mT, Bf.rstd], writes=[tf])
        P.op(DVE, lambda e, mc=mc, tf=tf: e.scalar_tensor_tensor(out=Bf.xs[:, mc, :], in0=tf[:], scalar=g_t[:, mc:mc + 1],
                                                                 in1=Bf.xs[:, mc, :], op0=ALU.mult, op1=ALU.add),
             reads=[tf, g_t, Bf.xs], writes=[Bf.xs])


def post_block(P, Bf, Kc, a_src, Wmix_b, x_src, g_post, g_fpre, g_fpost, Wfi_b, Wfo_b, t0):
    TT = Bf.TT
    P.dma(SP, Bf.xs[:], x_src.t[:, t0:t0 + TT].rearrange("(c p) t -> p c t", p=128), reads=[x_src], writes=[Bf.xs])
    P.dma(SP, Bf.at[:], a_src.t[:, t0:t0 + TT].rearrange("(c p) t -> p c t", p=128), reads=[a_src], writes=[Bf.at])
    for mc in range(16):
        w = Bf.wA[mc % 2]
        P.dma(SP, w[:], Wmix_b.t[mc], reads=[Wmix_b], writes=[w])
        p = Bf.nextp()
        _mm_group(P, p, p[:, 0:TT], [(w[:, k, :], Bf.at[:, k, :]) for k in range(Kc)], [w, Bf.at])
        P.op(ACT, lambda e, p=p, mc=mc: e.activation(out=Bf.mT[:, mc, :], in_=p[:, 0:TT], func=AF.Copy), reads=[p], writes=[Bf.mT])
    _norm_residual(P, Bf, g_post)
    P.op(ACT, lambda e: e.activation(out=Bf.sq[:], in_=Bf.xs[:], func=AF.Square), reads=[Bf.xs], writes=[Bf.sq])
    P.op(DVE, lambda e: e.tensor_tensor(out=Bf.hb[:], in0=Bf.xs[:], in1=g_fpre[:].unsqueeze(2).to_broadcast([128, 16, TT]),
                                        op=ALU.mult), reads=[Bf.xs, g_fpre], writes=[Bf.hb])
    _rms_rstd(P, Bf, Bf.xs, 16, D_MODEL, Bf.rstd)
    P.op(DVE, lambda e: e.tensor_tensor(out=Bf.r2[:], in0=Bf.rstd[:], in1=Bf.rstd[:], op=ALU.mult), reads=[Bf.rstd], writes=[Bf.r2])
    for mc in range(64):
        w = Bf.wI[mc % 3]
        P.dma(SP, w[:], Wfi_b.t[mc], reads=[Wfi_b], writes=[w])
        p = Bf.nextp()
        _mm_group(P, p, p[:, 0:TT], [(w[:, k, :], Bf.hb[:, k, :]) for k in range(16)], [w, Bf.hb])
        tf = Bf.tmpf[mc % 2]
        P.op(ACT, lambda e, p=p, tf=tf: e.activation(out=tf[:], in_=p[:, 0:TT], func=AF.Relu), reads=[p], writes=[tf])
        P.op(POOL, lambda e, mc=mc, tf=tf: e.tensor_tensor(out=Bf.hid[:, mc, :], in0=tf[:], in1=tf[:], op=ALU.mult),
             reads=[tf], writes=[Bf.hid])
    for mc in range(16):
        p = Bf.nextp()
        for hf in range(2):
            w = Bf.wO[hf]
            P.dma(SP, w[:], Wfo_b.t[mc, :, hf * 32:(hf + 1) * 32, :], reads=[Wfo_b], writes=[w])
            _mm_group(P, p, p[:, 0:TT], [(w[:, k, :], Bf.hid[:, hf * 32 + k, :]) for k in range(32)], [w, Bf.hid],
                      first=(hf == 0), last=(hf == 1))
        P.op(DVE, lambda e, p=p, mc=mc: e.tensor_tensor(out=Bf.mT[:, mc, :], in0=p[:, 0:TT], in1=Bf.r2[:], op=ALU.mult),
             reads=[p, Bf.r2], writes=[Bf.mT])
    _norm_residual(P, Bf, g_fpost)


def cast_w(P, dst, src, nsplit):
    n = src.t.shape[0]
    step = max(1, n // nsplit)
    for i in range(0, n, step):
        P.dma(POOL, dst.t[i:i + step], src.t[i:i + step], reads=[src], writes=[dst])


def build_B(NT, TT=256, debug=False):
    nc = bass.Bass("TRN2", target_bir_lowering=False)
    with ExitStack() as es:
        P = Prog(nc, es)
        xT = P.dram("xT", [D_MODEL, NT], F32, kind="ExternalInput")
        ogT = P.dram("ogT", [4096, NT], BF16, kind="ExternalInput")
        Wout = P.dram("Wout", [16, 128, 32, 128], F32, kind="ExternalInput")
        Wfi = P.dram("Wfi", [64, 128, 16, 128], F32, kind="ExternalInput")
        Wfo = P.dram("Wfo", [16, 128, 64, 128], F32, kind="ExternalInput")
        Wa = P.dram("Wa", [12, 128, 16, 128], F32, kind="ExternalInput")
        Wq = P.dram("Wq", [48, 128, 6, 128], F32, kind="ExternalInput")
        Wk = P.dram("Wk", [16, 128, 4, 128], F32, kind="ExternalInput")
        Wv = P.dram("Wv", [128, 4, 2048], F32, kind="ExternalInput")
        gv = P.dram("gv", [128, 4, 16], F32, kind="ExternalInput")
        gq = P.dram("gq", [128, 6], F32, kind="ExternalInput")
        gkv = P.dram("gkv", [128, 4], F32, kind="ExternalInput")
        C2 = P.dram("C2", [64, NT], F32, kind="ExternalInput")
        S2 = P.dram("S2", [64, NT], F32, kind="ExternalInput")
        E64 = P.dram("E64", [128, 65], F32, kind="ExternalInput")
        x2T = P.dram("x2T", [D_MODEL, NT], F32, kind="ExternalOutput")
        QN = P.dram("QN", [16, 128, NT], BF16, kind="ExternalOutput")
        QR = P.dram("QR", [16, 65, NT], BF16, kind="ExternalOutput")
        KN = P.dram("KN", [16, 128, NT], BF16, kind="ExternalOutput")
        KR = P.dram("KR", [65, NT], BF16, kind="ExternalOutput")
        V = P.dram("V", [NT, 2048], BF16, kind="ExternalOutput")
        Wout_b = P.dram("Wout_b", [16, 128, 32, 128], BF16)
        Wfi_b = P.dram("Wfi_b", [64, 128, 16, 128], BF16)
        Wfo_b = P.dram("Wfo_b", [16, 128, 64, 128], BF16)
        Wa_b = P.dram("Wa_b", [12, 128, 16, 128], BF16)
        Wq_b = P.dram("Wq_b", [48, 128, 6, 128], BF16)
        Wk_b = P.dram("Wk_b", [16, 128, 4, 128], BF16)
        Wv_b = P.dram("Wv_b", [128, 4, 2048], BF16)
        cast_w(P, Wout_b, Wout, 4)
        cast_w(P, Wfi_b, Wfi, 8)
        cast_w(P, Wfo_b, Wfo, 8)
        cast_w(P, Wa_b, Wa, 2)
        cast_w(P, Wq_b, Wq, 2)
        cast_w(P, Wk_b, Wk, 1)
        cast_w(P, Wv_b, Wv, 1)
        with ExitStack() as pes:
            P.pes = pes
            Bf = PostBufs(P, TT, 32)
            g_ts = [P.sb("g_t%d" % i, [128, 16], F32) for i in range(3)]
            for i in range(3):
                P.dma(SP, g_ts[i][:], gv.t[:, i, :], writes=[g_ts[i]])
            for ti in range(NT // TT):
                t0 = ti * TT
                post_block(P, Bf, 32, ogT, Wout_b, xT, g_ts[0], g_ts[1], g_ts[2], Wfi_b, Wfo_b, t0)
                P.dma(SP, x2T.t[:, t0:t0 + TT].rearrange("(c p) t -> p c t", p=128), Bf.xs[:], reads=[Bf.xs], writes=[x2T])
            P.flush()
        P.pes = None
        with ExitStack() as pes:
            P.pes = pes
            Bf = MlaBufs(P, TT)
            g_t3 = P.sb("g_t3", [128, 16], F32)
            gq_t = P.sb("gq_t", [128, 6], F32)
            gkv_t = P.sb("gkv_t", [128, 4], F32)
            e64f = P.sb("e64f", [128, 65], F32)
            e64 = P.sb("e64", [128, 65], BF16)
            wv = P.sb("wv", [128, 4, 2048], BF16)
            P.dma(SP, g_t3[:], gv.t[:, 3, :], writes=[g_t3])
            P.dma(SP, gq_t[:], gq.t, writes=[gq_t])
            P.dma(SP, gkv_t[:], gkv.t, writes=[gkv_t])
            P.dma(SP, e64f[:], E64.t, writes=[e64f])
            P.op(DVE, lambda e: e.tensor_copy(out=e64[:], in_=e64f[:]), reads=[e64f], writes=[e64])
            P.dma(SP, wv[:], Wv_b.t, reads=[Wv_b], writes=[wv])
            cq = P.sb("cq", [128, 6, TT], F32)
            ckv = P.sb("ckv", [128, 4, TT], F32)
            cqb = P.sb("cqb", [128, 6, TT], BF16)
            ckvb = P.sb("ckvb", [128, 4, TT], BF16)
            rq = P.sb("rq", [128, TT], F32)
            rkv = P.sb("rkv", [128, TT], F32)
            rkc = P.sb("rkc", [128, 4], F32)
            c2 = P.sb("c2", [64, TT], F32)
            s2 = P.sb("s2", [64, TT], F32)
            c2r = P.sb("c2r", [64, TT], F32)
            s2r = P.sb("s2r", [64, TT], F32)
            Ak = P.sb("Ak", [64, TT], F32)
            Bk = P.sb("Bk", [64, TT], F32)
            krf = P.sb("krf", [64, TT], F32)
            krt = [P.sb("krt%d" % i, [65, TT], BF16) for i in range(2)]
            qrt = [P.sb("qrt%d" % i, [65, TT], BF16) for i in range(2)]
            qnt = [P.sb("qnt%d" % i, [128, TT], BF16) for i in range(2)]
            knall = P.sb("knall", [128, 16, TT], BF16)
            prn = P.sb("prn", [128, TT], BF16)
            prr = P.sb("prr", [64, TT], BF16)
            vt = [P.sb("vt%d" % i, [128, 2048], BF16) for i in range(2)]
            wa_t = [P.sb("wa_t%d" % i, [128, 16, 128], BF16) for i in range(2)]
            wq_t = [P.sb("wq_t%d" % i, [128, 6, 128], BF16) for i in range(3)]
            wk_t = [P.sb("wk_t%d" % i, [128, 4, 128], BF16) for i in range(2)]
            psc = P.ps("psc", [128, 512])
            p65 = P.ps("p65", [128, 512])
            for i in range(2):
                P.op(POOL, lambda e, i=i: e.memset(krt[i][:], 1.0), writes=[krt[i]])
            nsub = TT // 128
            for ti in range(NT // TT):
                t0 = ti * TT
                xs = Bf.xs
                P.dma(SP, xs[:], x2T.t[:, t0:t0 + TT].rearrange("(c p) t -> p c t", p=128), reads=[x2T], writes=[xs])
                P.dma(SP, c2[:], C2.t[:, t0:t0 + TT], writes=[c2])
                P.dma(SP, s2[:], S2.t[:, t0:t0 + TT], writes=[s2])
                P.op(ACT, lambda e: e.activation(out=Bf.sq[:], in_=xs[:], func=AF.Square), reads=[xs], writes=[Bf.sq])
                P.op(DVE, lambda e: e.tensor_tensor(out=Bf.hb[:], in0=xs[:], in1=g_t3[:].unsqueeze(2).to_broadcast([128, 16, TT]),
                                                    op=ALU.mult), reads=[xs, g_t3], writes=[Bf.hb])
                _rms_rstd(P, Bf, xs, 16, D_MODEL, Bf.rstd)
                for mc in range(12):
                    w = wa_t[mc % 2]
                    P.dma(SP, w[:], Wa_b.t[mc], reads=[Wa_b], writes=[w])
                    p = Bf.nextp()
                    _mm_group(P, p, p[:, 0:TT], [(w[:, k, :], Bf.hb[:, k, :]) for k in range(16)], [w, Bf.hb])
                    if mc < 6:
                        P.op(DVE, lambda e, p=p, mc=mc: e.tensor_tensor(out=cq[:, mc, :], in0=p[:, 0:TT], in1=Bf.rstd[:], op=ALU.mult),
                             reads=[p, Bf.rstd], writes=[cq])
                    elif mc < 10:
                        P.op(DVE, lambda e, p=p, mc=mc: e.tensor_tensor(out=ckv[:, mc - 6, :], in0=p[:, 0:TT], in1=Bf.rstd[:], op=ALU.mult),
                             reads=[p, Bf.rstd], writes=[ckv])
                    else:
                        dst = Ak if mc == 10 else Bk
                        P.op(DVE, lambda e, p=p, dst=dst: e.tensor_tensor(out=dst[:], in0=p[0:64, 0:TT], in1=Bf.rstd[0:64, :], op=ALU.mult),
                             reads=[p, Bf.rstd], writes=[dst])
                kr = krt[ti % 2]
                P.op(DVE, lambda e: e.tensor_tensor(out=Ak[:], in0=Ak[:], in1=c2[:], op=ALU.mult), reads=[Ak, c2], writes=[Ak])
                P.op(DVE, lambda e: e.tensor_tensor(out=Bk[:], in0=Bk[:], in1=s2[:], op=ALU.mult), reads=[Bk, s2], writes=[Bk])
                P.op(DVE, lambda e: e.tensor_tensor(out=krf[:], in0=Ak[:], in1=Bk[:], op=ALU.add), reads=[Ak, Bk], writes=[krf])
                P.op(ACT, lambda e, kr=kr: e.activation(out=kr[0:64, :], in_=krf[:], func=AF.Copy), reads=[krf], writes=[kr])
                P.dma(SP, KR.t[:, t0:t0 + TT], kr[:], reads=[kr], writes=[KR])
                P.op(POOL, lambda e: e.tensor_tensor(out=Bf.sq[:, 0:6, :], in0=cq[:], in1=cq[:], op=ALU.mult), reads=[cq], writes=[Bf.sq])
                _rms_rstd(P, Bf, cq, 6, 768, rq)
                P.op(DVE, lambda e: e.tensor_tensor(out=cqb[:], in0=cq[:], in1=gq_t[:].unsqueeze(2).to_broadcast([128, 6, TT]), op=ALU.mult),
                     reads=[cq, gq_t], writes=[cqb])
                P.op(POOL, lambda e: e.tensor_tensor(out=Bf.sq[:, 0:4, :], in0=ckv[:], in1=ckv[:], op=ALU.mult), reads=[ckv], writes=[Bf.sq])
                _rms_rstd(P, Bf, ckv, 4, 512, rkv)

                def colsum(e):
                    for sub in range(nsub):
                        for k in range(4):
                            ins = e.matmul(psc[:, sub:sub + 1], Bf.sq[:, k, sub * 128:(sub + 1) * 128], Bf.ones[:, 0:1],
                                           start=(k == 0), stop=(k == 3))
                    return ins
                P.op(PE, colsum, reads=[Bf.sq, Bf.ones], writes=[psc])
                P.op(ACT, lambda e: e.activation(out=rkc[:, 0:nsub], in_=psc[:, 0:nsub], func=AF.Ln, bias=RMS_EPS, scale=1.0 / 512),
                     reads=[psc], writes=[rkc])
                P.op(ACT, lambda e: e.activation(out=rkc[:, 0:nsub], in_=rkc[:, 0:nsub], func=AF.Exp, scale=-0.5), reads=[rkc], writes=[rkc])
                P.op(DVE, lambda e: e.tensor_tensor(out=ckvb[:], in0=ckv[:], in1=gkv_t[:].unsqueeze(2).to_broadcast([128, 4, TT]), op=ALU.mult),
                     reads=[ckv, gkv_t], writes=[ckvb])
                P.op(DVE, lambda e: e.tensor_tensor(out=c2r[:], in0=c2[:], in1=rq[0:64, :], op=ALU.mult), reads=[c2, rq], writes=[c2r])
                P.op(DVE, lambda e: e.tensor_tensor(out=s2r[:], in0=s2[:], in1=rq[0:64, :], op=ALU.mult), reads=[s2, rq], writes=[s2r])
                for h in range(16):
                    w = wk_t[h % 2]
                    P.dma(SP, w[:], Wk_b.t[h], reads=[Wk_b], writes=[w])
                    p = Bf.nextp()
                    _mm_group(P, p, p[:, 0:TT], [(w[:, k, :], ckvb[:, k, :]) for k in range(4)], [w, ckvb])
                    P.op(DVE, lambda e, p=p, h=h: e.tensor_tensor(out=knall[:, h, :], in0=p[:, 0:TT], in1=rkv[:], op=ALU.mult),
                         reads=[p, rkv], writes=[knall])
                P.dma(SP, KN.t[:, :, t0:t0 + TT].rearrange("h p t -> p h t"), knall[:], reads=[knall], writes=[KN])
                for sub in range(nsub):
                    v = vt[sub % 2]
                    for g4 in range(4):
                        p = Bf.nextp()
                        _mm_group(P, p, p[:, 0:512], [(ckvb[:, k, sub * 128:(sub + 1) * 128], wv[:, k, g4 * 512:(g4 + 1) * 512])
                                                     for k in range(4)], [ckvb, wv])
                        P.op(ACT, lambda e, p=p, v=v, g4=g4, sub=sub: e.activation(out=v[:, g4 * 512:(g4 + 1) * 512], in_=p[:, 0:512],
                                                                                    func=AF.Copy, scale=rkc[:, sub:sub + 1]),
                             reads=[p, rkc], writes=[v])
                    P.dma(SP, V.t[t0 + sub * 128:t0 + (sub + 1) * 128, :], v[:], reads=[v], writes=[V])
                for h in range(16):
                    qn = qnt[h % 2]
                    qr = qrt[h % 2]
                    w0, w1, w2 = wq_t[0], wq_t[1], wq_t[2]
                    P.dma(SP, w0[:], Wq_b.t[3 * h], reads=[Wq_b], writes=[w0])
                    P.dma(SP, w1[:], Wq_b.t[3 * h + 1], reads=[Wq_b], writes=[w1])
                    P.dma(SP, w2[:], Wq_b.t[3 * h + 2], reads=[Wq_b], writes=[w2])
                    p = Bf.nextp()
                    _mm_group(P, p, p[:, 0:TT], [(w0[:, k, :], cqb[:, k, :]) for k in range(6)], [w0, cqb])
                    P.op(DVE, lambda e, p=p, qn=qn: e.tensor_tensor(out=qn[:], in0=p[:, 0:TT], in1=rq[:], op=ALU.mult),
                         reads=[p, rq], writes=[qn])
                    P.dma(SP, QN.t[h, :, t0:t0 + TT], qn[:], reads=[qn], writes=[QN])
                    pa = Bf.nextp()
                    _mm_group(P, pa, pa[0:64, 0:TT], [(w1[:, k, 0:64], cqb[:, k, :]) for k in range(6)], [w1, cqb])
                    P.op(DVE, lambda e, pa=pa: e.tensor_tensor(out=Ak[:], in0=pa[0:64, 0:TT], in1=c2r[:], op=ALU.mult),
                         reads=[pa, c2r], writes=[Ak])
                    pb = Bf.nextp()
                    _mm_group(P, pb, pb[0:64, 0:TT], [(w2[:, k, 0:64], cqb[:, k, :]) for k in range(6)], [w2, cqb])
                    P.op(DVE, lambda e, pb=pb: e.tensor_tensor(out=Bk[:], in0=pb[0:64, 0:TT], in1=s2r[:], op=ALU.mult),
                         reads=[pb, s2r], writes=[Bk])
                    P.op(DVE, lambda e, qr=qr: e.tensor_tensor(out=qr[0:64, :], in0=Ak[:], in1=Bk[:], op=ALU.add),
                         reads=[Ak, Bk], writes=[qr])
                    P.op(POOL, lambda e, qn=qn, h=h: e.tensor_tensor(out=prn[:], in0=qn[:], in1=knall[:, h, :], op=ALU.mult),
                         reads=[qn, knall], writes=[prn])
                    P.op(POOL, lambda e, qr=qr, kr=kr: e.tensor_tensor(out=prr[:], in0=qr[0:64, :], in1=kr[0:64, :], op=ALU.mult),
                         reads=[qr, kr], writes=[prr])
                    _mm_group(P, p65, p65[0:65, 0:TT], [(e64[:, :], prn[:]), (e64[0:64, :], prr[:])], [e64, prn, prr])
                    P.op(ACT, lambda e, qr=qr: e.activation(out=qr[64:65, :], in_=p65[64:65, 0:TT], func=AF.Copy, scale=-1.0),
                         reads=[p65], writes=[qr])
                    P.dma(SP, QR.t[h, :, t0:t0 + TT], qr[:], reads=[qr], writes=[QR])
            P.flush()
        P.pes = None
    return nc


MLA_HEADS = 16
ROPE_THETA = 10000.0


def _core_tokens(seq_lens, c):
    offs = np.concatenate([[0], np.cumsum(seq_lens)]).astype(int)
    idx, pos = [], []
    for s, L in enumerate(seq_lens):
        n = L // NCORE
        p = np.arange(c * n, (c + 1) * n)
        idx.append(offs[s] + p)
        pos.append(p)
    return np.concatenate(idx), np.concatenate(pos)


def _rope_tabs(pos):
    half = 32
    inv_freq = (np.float32(ROPE_THETA) ** (-(np.arange(half, dtype=np.float32) / np.float32(half)))).astype(np.float32)
    ang = (pos.astype(np.float32)[:, None] * inv_freq[None, :]).astype(np.float32)
    cos = np.cos(ang.astype(np.float64)).astype(np.float32)
    sin = np.sin(ang.astype(np.float64)).astype(np.float32)
    C2 = np.ascontiguousarray(np.concatenate([cos, cos], 1).T)
    S2 = np.ascontiguousarray(np.concatenate([-sin, sin], 1).T)
    return C2, S2


def prep_B_weights(norm_mix_post, norm_ffn_pre, norm_ffn_post, norm_mix_pre, gdn_w_out, ffn_w_in, ffn_w_out,
                   mla_w_a, mla_q_a_norm, mla_w_q_b, mla_kv_a_norm, mla_w_kv_b):
    wa = mla_w_a[0]
    rope = wa[:, 1280:1344]
    sw = np.concatenate([rope[:, 32:], rope[:, :32]], 1)
    z64 = np.zeros((2048, 64), np.float32)
    wa_p = np.concatenate([wa[:, :1280], rope, z64, sw, z64], 1)
    wq = mla_w_q_b[0]
    z = np.zeros((768, 64), np.float32)
    cols = []
    for h in range(16):
        b = h * 192
        r = wq[:, b + 128:b + 192]
        cols += [wq[:, b:b + 128], r, z, np.concatenate([r[:, 32:], r[:, :32]], 1), z]
    wq_p = np.concatenate(cols, 1)
    wkv = mla_w_kv_b[0].reshape(512, 16, 256)
    wk = np.ascontiguousarray(wkv[:, :, :128].reshape(512, 2048))
    wv = np.ascontiguousarray(wkv[:, :, 128:].reshape(512, 2048))
    e64 = np.zeros((128, 65), np.float32)
    e64[:, 64] = 1.0
    return {
        "Wout": lay_w(gdn_w_out[0]), "Wfi": lay_w(ffn_w_in[0]), "Wfo": lay_w(ffn_w_out[0]),
        "Wa": lay_w(wa_p), "Wq": lay_w(wq_p), "Wk": lay_w(wk),
        "Wv": np.ascontiguousarray(wv.reshape(4, 128, 2048).transpose(1, 0, 2)),
        "gv": np.ascontiguousarray(np.stack([lay_g(norm_mix_post[0]), lay_g(norm_ffn_pre[0]), lay_g(norm_ffn_post[0]),
                                            lay_g(norm_mix_pre[1])], 1)),
        "gq": lay_g(mla_q_a_norm[0]), "gkv": lay_g(mla_kv_a_norm[0]), "E64": e64,
    }


def build_C(slabs, TQ=512, TT=256):
    NT = sum(slabs)
    NTOK = NCORE * NT
    scale = float(192 ** -0.5)
    nc = bass.Bass("TRN2", target_bir_lowering=False)
    with ExitStack() as es:
        P = Prog(nc, es)
        QN = P.dram("QN", [16, 128, NT], BF16, kind="ExternalInput")
        QR = P.dram("QR", [16, 65, NT], BF16, kind="ExternalInput")
        KNa = P.dram("KNa", [16, 128, NTOK], BF16, kind="ExternalInput")
        KRa = P.dram("KRa", [65, NTOK], BF16, kind="ExternalInput")
        Va = P.dram("Va", [16, NCORE, 128, NT // 128, 128], BF16, kind="ExternalInput")
        x2T = P.dram("x2T", [D_MODEL, NT], F32, kind="ExternalInput")
        Wo = P.dram("Wo", [16, 128, 16, 128], F32, kind="ExternalInput")
        Wfi = P.dram("Wfi", [64, 128, 16, 128], F32, kind="ExternalInput")
        Wfo = P.dram("Wfo", [16, 128, 64, 128], F32, kind="ExternalInput")
        gv = P.dram("gv", [128, 3, 16], F32, kind="ExternalInput")
        yT = P.dram("yT", [D_MODEL, NT], F32, kind="ExternalOutput")
        aoT = P.dram("aoT", [D_MODEL, NT], BF16)
        Wo_b = P.dram("Wo_b", [16, 128, 16, 128], BF16)
        Wfi_b = P.dram("Wfi_b", [64, 128, 16, 128], BF16)
        Wfo_b = P.dram("Wfo_b", [16, 128, 64, 128], BF16)
        cast_w(P, Wo_b, Wo, 2)
        cast_w(P, Wfi_b, Wfi, 8)
        cast_w(P, Wfo_b, Wfo, 8)
        SEG = max(slabs)
        with ExitStack() as pes:
            P.pes = pes
            ones = P.sb("ones", [128, 128], BF16)
            P.op(POOL, lambda e: e.memset(ones[:], 1.0), writes=[ones])
            QG = 2
            qn = [P.sb("qn%d" % i, [128, QG * TQ], BF16) for i in range(2)]
            qr = [P.sb("qr%d" % i, [65, QG * TQ], BF16) for i in range(2)]
            kn = [P.sb("kn%d" % i, [128, SEG], BF16) for i in range(3)]
            kr = [P.sb("kr%d" % i, [65, SEG], BF16) for i in range(3)]
            vv = [P.sb("vv%d" % i, [128, SEG // 128, 128], BF16) for i in range(3)]
            pt = [P.sb("pt%d" % i, [128, TQ], BF16) for i in range(4)]
            rinv = P.sb("rinv", [128, TQ], F32)
            ao = [P.sb("ao%d" % i, [128, TQ], BF16) for i in range(2)]
            ps_s = [P.ps("ps_s%d" % i, [128, 512]) for i in range(4)]
            ps_o = [P.ps("ps_o%d" % i, [128, 512]) for i in range(QG)]
            ps_r = [P.ps("ps_r%d" % i, [128, 512]) for i in range(QG)]
            soff = 0
            cnt = 0
            hc = 0
            sc = 0
            ac = 0
            for slab in slabs:
                nkb = slab // 128
                for qg in range(slab // (QG * TQ)):
                    q0 = soff + qg * QG * TQ
                    for h in range(16):
                        qn_, qr_ = qn[hc % 2], qr[hc % 2]
                        hc += 1
                        P.dma(SP, qn_[:], QN.t[h, :, q0:q0 + QG * TQ], reads=[QN], writes=[qn_])
                        P.dma(SP, qr_[:], QR.t[h, :, q0:q0 + QG * TQ], reads=[QR], writes=[qr_])
                        nblk = NCORE * nkb
                        bi = 0
                        pend = []
                        for r in range(NCORE):
                            k0 = r * NT + soff
                            kn_, kr_, v_ = kn[sc % 3], kr[sc % 3], vv[sc % 3]
                            sc += 1
                            P.dma(SP, kn_[:, 0:slab], KNa.t[h, :, k0:k0 + slab], reads=[KNa], writes=[kn_])
                            P.dma(SP, kr_[:, 0:slab], KRa.t[:, k0:k0 + slab], reads=[KRa], writes=[kr_])
                            P.dma(ACT, v_[:, 0:nkb, :], Va.t[h, r, :, soff // 128:soff // 128 + nkb, :], reads=[Va], writes=[v_])
                            for kb in range(nkb):
                                ksl = slice(kb * 128, (kb + 1) * 128)
                                f, l = (bi == 0), (bi == nblk - 1)
                                for j in range(QG):
                                    ps = ps_s[cnt % 4]
                                    pt_ = pt[cnt % 4]
                                    cnt += 1
                                    qsl = slice(j * TQ, (j + 1) * TQ)
                                    _mm_group(P, ps, ps[:, 0:TQ], [(kn_[:, ksl], qn_[:, qsl]), (kr_[:, ksl], qr_[:, qsl])],
                                              [kn_, kr_, qn_, qr_])
                                    P.op(ACT, lambda e, ps=ps, pt_=pt_: e.activation(out=pt_[:], in_=ps[:, 0:TQ], func=AF.Exp, scale=scale),
                                         reads=[ps], writes=[pt_])
                                    po, pr = ps_o[j], ps_r[j]

                                    def acc(e, v_=v_, kb=kb, pt_=pt_, po=po, pr=pr, f=f, l=l):
                                        e.matmul(po[:, 0:TQ], v_[:, kb, :], pt_[:], start=f, stop=l)
                                        return e.matmul(pr[:, 0:TQ], ones[:], pt_[:], start=f, stop=l)
                                    pend.append((acc, [v_, pt_, ones], [po, pr]))
                                    if len(pend) > 1:
                                        a_, r_, w_ = pend.pop(0)
                                        P.op(PE, a_, reads=r_, writes=w_)
                                bi += 1
                        while pend:
                            a_, r_, w_ = pend.pop(0)
                            P.op(PE, a_, reads=r_, writes=w_)
                        for j in range(QG):
                            po, pr = ps_o[j], ps_r[j]
                            ao_ = ao[ac % 2]
                            ac += 1
                            P.op(DVE, lambda e, pr=pr: e.reciprocal(out=rinv[:], in_=pr[:, 0:TQ]), reads=[pr], writes=[rinv])
                            P.op(DVE, lambda e, po=po, ao_=ao_: e.tensor_tensor(out=ao_[:], in0=po[:, 0:TQ], in1=rinv[:], op=ALU.mult),
                                 reads=[po, rinv], writes=[ao_])
                            P.dma(SP, aoT.t[h * 128:(h + 1) * 128, q0 + j * TQ:q0 + (j + 1) * TQ], ao_[:], reads=[ao_], writes=[aoT])
                soff += slab
            P.flush()
        P.pes = None
        with ExitStack() as pes:
            P.pes = pes
            Bf = PostBufs(P, TT, 16)
            g_ts = [P.sb("g_t%d" % i, [128, 16], F32) for i in range(3)]
            for i in range(3):
                P.dma(SP, g_ts[i][:], gv.t[:, i, :], writes=[g_ts[i]])
            for ti in range(NT // TT):
                t0 = ti * TT
                post_block(P, Bf, 16, aoT, Wo_b, x2T, g_ts[0], g_ts[1], g_ts[2], Wfi_b, Wfo_b, t0)
                P.dma(SP, yT.t[:, t0:t0 + TT].rearrange("(c p) t -> p c t", p=128), Bf.xs[:], reads=[Bf.xs], writes=[yT])
            P.flush()
        P.pes = None
    return nc


def run_model(seq_lens, xs, norm_mix_pre, norm_mix_post, norm_ffn_pre, norm_ffn_post,
              gdn_w_in, gdn_conv_w, gdn_a_log, gdn_dt_bias, gdn_norm_w, gdn_w_out,
              mla_w_a, mla_q_a_norm, mla_w_q_b, mla_kv_a_norm, mla_w_kv_b, mla_w_o,
              ffn_w_in, ffn_w_out):
    cores = list(range(NCORE))
    xT_all = np.ascontiguousarray(xs.T)
    og = run_A(seq_lens, xT_all, norm_mix_pre, gdn_w_in, gdn_conv_w, gdn_a_log, gdn_dt_bias, gdn_norm_w)
    slabs = [L // NCORE for L in seq_lens]
    NT = sum(slabs)
    WB = prep_B_weights(norm_mix_post, norm_ffn_pre, norm_ffn_post, norm_mix_pre, gdn_w_out, ffn_w_in, ffn_w_out,
                        mla_w_a, mla_q_a_norm, mla_w_q_b, mla_kv_a_norm, mla_w_kv_b)
    toks = [_core_tokens(seq_lens, c) for c in cores]
    in_maps = []
    for c in cores:
        idx, pos = toks[c]
        C2, S2 = _rope_tabs(pos)
        m = dict(WB)
        m.update({"xT": np.ascontiguousarray(xs[idx].T), "ogT": np.ascontiguousarray(og[idx].T), "C2": C2, "S2": S2})
        in_maps.append(m)
    ncB = build_B(NT)
    resB = _launch(ncB, in_maps, "B").results
    del in_maps
    KNa = np.ascontiguousarray(np.concatenate([np.asarray(resB[c]["KN"]) for c in cores], axis=2))
    KRa = np.ascontiguousarray(np.concatenate([np.asarray(resB[c]["KR"]) for c in cores], axis=1))
    Va = np.ascontiguousarray(np.stack([np.asarray(resB[c]["V"]).reshape(NT // 128, 128, 16, 128).transpose(2, 1, 0, 3)
                                        for c in cores], axis=1))
    WC = {
        "Wo": lay_w(mla_w_o[0]), "Wfi": lay_w(ffn_w_in[1]), "Wfo": lay_w(ffn_w_out[1]),
        "gv": np.ascontiguousarray(np.stack([lay_g(norm_mix_post[1]), lay_g(norm_ffn_pre[1]), lay_g(norm_ffn_post[1])], 1)),
        "KNa": KNa, "KRa": KRa, "Va": Va,
    }
    in_maps = []
    for c in cores:
        m = dict(WC)
        m.update({"QN": np.asarray(resB[c]["QN"]), "QR": np.asarray(resB[c]["QR"]), "x2T": np.asarray(resB[c]["x2T"])})
        in_maps.append(m)
    ncC = build_C(slabs)
    resC = _launch(ncC, in_maps, "C").results
    y = np.empty((sum(seq_lens), D_MODEL), np.float32)
    for c in cores:
        y[toks[c][0]] = np.asarray(resC[c]["yT"]).T
    return y


def kernel(x_prompt, x_sample, norm_mix_pre, norm_mix_post, norm_ffn_pre, norm_ffn_post,
           gdn_w_in, gdn_conv_w, gdn_a_log, gdn_dt_bias, gdn_norm_w, gdn_w_out,
           mla_w_a, mla_q_a_norm, mla_w_q_b, mla_kv_a_norm, mla_w_kv_b, mla_w_o,
           ffn_w_in, ffn_w_out):
    a = [np.asarray(v, dtype=np.float32) for v in (
        x_prompt, x_sample, norm_mix_pre, norm_mix_post, norm_ffn_pre, norm_ffn_post,
        gdn_w_in, gdn_conv_w, gdn_a_log, gdn_dt_bias, gdn_norm_w, gdn_w_out,
        mla_w_a, mla_q_a_norm, mla_w_q_b, mla_kv_a_norm, mla_w_kv_b, mla_w_o, ffn_w_in, ffn_w_out)]
    xp, xsm = a[0], a[1]
    Bp, Lp, _ = xp.shape
    Bs, Ls, _ = xsm.shape
    seq_lens = [Lp] * Bp + [Ls] * Bs
    xs = np.concatenate([xp.reshape(Bp * Lp, D_MODEL), xsm.reshape(Bs * Ls, D_MODEL)], axis=0)
    y = run_model(seq_lens, xs, *a[2:])
    yp = y[:Bp * Lp].reshape(Bp, Lp, D_MODEL)
    ysm = y[Bp * Lp:].reshape(Bs, Ls, D_MODEL)
    return (np.ascontiguousarray(yp), np.ascontiguousarray(ysm))
```

```python
import numpy as np
import ml_dtypes
from contextlib import ExitStack
import concourse.bass as bass
import concourse.mybir as mybir
from concourse.bass_utils import run_bass_kernel_spmd

F32 = mybir.dt.float32
BF16 = mybir.dt.bfloat16
AF = mybir.ActivationFunctionType
ALU = mybir.AluOpType
NPBF = ml_dtypes.bfloat16

PE, ACT, DVE, POOL, SP = "pe", "act", "dve", "pool", "sp"
ENGS = (PE, ACT, DVE, POOL, SP)

D_MODEL = 2048
NCORE = 8
RMS_EPS = 1e-6
L2_EPS = 1e-6


class T:
    __slots__ = ("t", "w", "r", "name", "psum")

    def __init__(self, t, name="", psum=False):
        self.t = t
        self.w = {}
        self.r = {}
        self.name = name
        self.psum = psum

    def __getitem__(self, k):
        return self.t[k]


class Prog:
    NSLOT = 6

    def __init__(self, nc, es):
        self.nc = nc
        self.es = es
        self.ops = {e: [] for e in ENGS}
        self.cnt = {e: 0 for e in ENGS}
        self.sems = []
        self.esem = {}
        for e in ENGS:
            self.esem[e] = self._newsem("p_" + e)
        self.slots = {}
        self.slot_cnt = {}
        self.slot_next = {}
        for q in (SP, ACT, POOL):
            self.slots[q] = [self._newsem("d_%s%d" % (q, i)) for i in range(self.NSLOT)]
            self.slot_cnt[q] = [0] * self.NSLOT
            self.slot_next[q] = 0
        self.waited = {e: {} for e in ENGS}
        self.ninst = 0
        self.pes = None
        self.uid = 0

    def _newsem(self, name):
        h = self.es.enter_context(self.nc.semaphore(name))
        self.sems.append(h)
        return len(self.sems) - 1

    def sb(self, name, shape, dt):
        es = self.pes if self.pes is not None else self.es
        self.uid += 1
        name = "%s_%d" % (name, self.uid)
        return T(es.enter_context(self.nc.sbuf_tensor(name, list(shape), dt)), name)

    def ps(self, name, shape, dt=F32):
        es = self.pes if self.pes is not None else self.es
        self.uid += 1
        name = "%s_%d" % (name, self.uid)
        return T(es.enter_context(self.nc.psum_tensor(name, list(shape), dt)), name, psum=True)

    def dram(self, name, shape, dt, kind="Internal"):
        return T(self.nc.dram_tensor(name, list(shape), dt, kind=kind).ap(), name)

    def _deps(self, eng, reads, writes, is_dma):
        deps = {}
        own = None if is_dma else self.esem[eng]

        def add(d, skip_same):
            for s, v in d.items():
                if skip_same and s == own:
                    continue
                if deps.get(s, 0) < v:
                    deps[s] = v
        for t in reads:
            add(t.w, eng == PE)
            if t.psum:
                add(t.r, True)
        for t in writes:
            add(t.w, True)
            add(t.r, True)
        return deps

    def _waits(self, eng, deps):
        wl = []
        wd = self.waited[eng]
        for s, v in deps.items():
            if wd.get(s, 0) < v:
                wd[s] = v
                wl.append((s, v))
        return wl

    def _mark(self, ev, reads, writes):
        for t in reads:
            if t.r.get(ev[0], 0) < ev[1]:
                t.r[ev[0]] = ev[1]
        for t in writes:
            t.w = {ev[0]: ev[1]}
            t.r = {}

    def op(self, eng, fn, reads=(), writes=()):
        wl = self._waits(eng, self._deps(eng, reads, writes, False))
        self.cnt[eng] += 1
        ev = (self.esem[eng], self.cnt[eng])
        self.ops[eng].append((wl, fn, self.esem[eng], 1))
        self._mark(ev, reads, writes)
        return ev

    def dma(self, q, out, in_, reads=(), writes=(), **kw):
        k = self.slot_next[q]
        self.slot_next[q] = (k + 1) % self.NSLOT
        sem = self.slots[q][k]
        deps = self._deps(q, reads, writes, True)
        prev = 16 * self.slot_cnt[q][k]
        if prev > 0 and deps.get(sem, 0) < prev:
            deps[sem] = prev
        wl = self._waits(q, deps)
        self.slot_cnt[q][k] += 1
        ev = (sem, 16 * self.slot_cnt[q][k])

        def fn(e, out=out, in_=in_, kw=kw):
            return e.dma_start(out=out, in_=in_, **kw)
        self.ops[q].append((wl, fn, sem, 16))
        self._mark(ev, reads, writes)
        return ev

    def _all_events(self):
        ev = {}
        for e in ENGS:
            if self.cnt[e] > 0:
                ev[self.esem[e]] = self.cnt[e]
        for q in (SP, ACT, POOL):
            for k in range(self.NSLOT):
                if self.slot_cnt[q][k] > 0:
                    ev[self.slots[q][k]] = 16 * self.slot_cnt[q][k]
        return ev

    def flush(self, final=False):
        allev = self._all_events()
        sems = self.sems
        ops = self.ops
        waited = self.waited

        def run(e, name):
            for wl, fn, sem, inc in ops[name]:
                for s, v in wl:
                    e.wait_ge(sems[s], v)
                fn(e).then_inc(sems[sem], inc)
                self.ninst += 1 + len(wl)
            own = self.esem[name]
            for s, v in allev.items():
                if s == own:
                    continue
                if waited[name].get(s, 0) < v:
                    waited[name][s] = v
                    e.wait_ge(sems[s], v)
        with self.nc.Block() as block:
            @block.sync
            def _(e):
                run(e, SP)

            @block.tensor
            def _(e):
                run(e, PE)

            @block.scalar
            def _(e):
                run(e, ACT)

            @block.vector
            def _(e):
                run(e, DVE)

            @block.gpsimd
            def _(e):
                run(e, POOL)
        self.ops = {e: [] for e in ENGS}


def _mm_group(P, out_t, out_ap, pairs, reads, first=True, last=True):
    def fn(e, pairs=pairs, out_ap=out_ap):
        n = len(pairs)
        for i, (l, r) in enumerate(pairs):
            ins = e.matmul(out_ap, l, r, start=(first and i == 0), stop=(last and i == n - 1))
        return ins
    return P.op(PE, fn, reads=reads, writes=[out_t])


CH = 128
NLVL = 7


def _gdn_consts():
    i = np.arange(CH)
    s = i[:, None]
    c = i[None, :]
    f32 = np.zeros((CH, 36, CH), np.float32)
    f32[:, 0, :] = (s <= c)
    f32[:, 1, :] = (s >= c)
    f32[:, 2, :] = -1.0 * (c > s)
    f32[:, 3, :] = -1.0 * (c < s)
    f32[:, 4, :] = (c >= s)
    f32[:, 5, :] = (c <= s)
    for l in range(NLVL):
        b = 1 << l
        r = i[:, None]
        q = i[None, :]
        same = (r // (2 * b)) == (q // (2 * b))
        low = same & ((r // b) % 2 == 1) & ((q // b) % 2 == 0)
        f32[:, 6 + l, :] = low
        f32[:, 6 + NLVL + l, :] = low.T
    f32[:, 20, :] = np.eye(CH)
    f32[:, 21, :] = 1.0
    return f32


def build_A(seq_lens, TT=512, upto=9, debug=False):
    NTOK = sum(seq_lens)
    offs = np.concatenate([[0], np.cumsum(seq_lens)]).astype(int)
    nc = bass.Bass("TRN2", target_bir_lowering=False)
    with ExitStack() as es:
        P = Prog(nc, es)
        xT = P.dram("xT", [D_MODEL, NTOK], F32, kind="ExternalInput")
        gpre = P.dram("gpre", [128, 16], F32, kind="ExternalInput")
        Wqkv = P.dram("Wqkv", [D_MODEL, 1024], F32, kind="ExternalInput")
        Wz = P.dram("Wz", [D_MODEL, 512], F32, kind="ExternalInput")
        Wg = P.dram("Wg", [D_MODEL, 16], F32, kind="ExternalInput")
        convw = P.dram("convw", [128, 8, 5], F32, kind="ExternalInput")
        alog = P.dram("alog", [128, 8], F32, kind="ExternalInput")
        dtb = P.dram("dtb", [128, 8], F32, kind="ExternalInput")
        gnw = P.dram("gnw", [128, 128], F32, kind="ExternalInput")
        cst = P.dram("cst", [128, 36, 128], F32, kind="ExternalInput")
        og = P.dram("og", [NTOK, 512], BF16, kind="ExternalOutput")
        Wqkv_b = P.dram("Wqkv_b", [D_MODEL, 1024], BF16)
        Wz_b = P.dram("Wz_b", [D_MODEL, 512], BF16)
        Wg_b = P.dram("Wg_b", [D_MODEL, 16], BF16)
        dk = "ExternalOutput" if debug else "Internal"
        rawT = P.dram("rawT", [8, 128, NTOK], F32, kind=dk)
        zsil = P.dram("zsil", [NTOK, 512], F32, kind=dk)
        gates = P.dram("gates", [NTOK, 16], F32, kind=dk)
        qkT = P.dram("qkT", [4, 128, NTOK], BF16, kind=dk)
        ktok = P.dram("ktok", [NTOK, 2, 128], BF16, kind=dk)
        vtok = P.dram("vtok", [NTOK, 4, 128], BF16, kind=dk)
        osc = P.dram("osc", [2, NTOK, 512], F32, kind=dk)

        P.dma(POOL, Wqkv_b.t, Wqkv.t, reads=[Wqkv], writes=[Wqkv_b])
        P.dma(POOL, Wz_b.t, Wz.t, reads=[Wz], writes=[Wz_b])
        P.dma(POOL, Wg_b.t, Wg.t, reads=[Wg], writes=[Wg_b])

        with ExitStack() as pes:
            P.pes = pes
            wqkv = P.sb("wqkv", [128, 16, 1024], BF16)
            wz = P.sb("wz", [128, 16, 512], BF16)
            wg = P.sb("wg", [128, 16, 16], BF16)
            gt = P.sb("gt", [128, 16], F32)
            ones = P.sb("ones", [128, 128], BF16)
            alog_s = P.sb("alog_s", [128, 8], F32)
            nega = P.sb("nega", [128, 8], F32)
            dtb_s = P.sb("dtb_s", [128, 8], F32)
            xs = [P.sb("xs%d" % i, [128, 16, TT], F32) for i in range(2)]
            sq = P.sb("sq", [128, 16, TT], BF16)
            hb = P.sb("hb", [128, 16, TT], BF16)
            rstd = P.sb("rstd", [128, TT], F32)
            rcol = P.sb("rcol", [128, 4], F32)
            rawo = [P.sb("rawo%d" % i, [128, TT], F32) for i in range(2)]
            zo = [P.sb("zo%d" % i, [128, 512], F32) for i in range(2)]
            gsb = [P.sb("gsb%d" % i, [128, 16], F32) for i in range(2)]
            gtmp = P.sb("gtmp", [128, 8], F32)
            pss = P.ps("pss", [128, 512])
            psc = P.ps("psc", [128, 512])
            pm = [P.ps("pm%d" % i, [128, 512]) for i in range(4)]
            psg = P.ps("psg", [128, 512])

            P.dma(SP, wqkv[:], Wqkv_b.t.rearrange("(c p) m -> p c m", p=128), reads=[Wqkv_b], writes=[wqkv])
            P.dma(SP, wz[:], Wz_b.t.rearrange("(c p) m -> p c m", p=128), reads=[Wz_b], writes=[wz])
            P.dma(SP, wg[:], Wg_b.t.rearrange("(c p) m -> p c m", p=128), reads=[Wg_b], writes=[wg])
            P.dma(SP, gt[:], gpre.t, writes=[gt])
            P.dma(SP, alog_s[:], alog.t, writes=[alog_s])
            P.dma(SP, dtb_s[:], dtb.t, writes=[dtb_s])
            P.op(POOL, lambda e: e.memset(ones[:], 1.0), writes=[ones])
            P.op(ACT, lambda e: e.activation(out=nega[:], in_=alog_s[:], func=AF.Exp), reads=[alog_s], writes=[nega])
            P.op(DVE, lambda e: e.tensor_scalar(out=nega[:], in0=nega[:], scalar1=-1.0, scalar2=None, op0=ALU.mult),
                 reads=[nega], writes=[nega])

            ntile = NTOK // TT
            for ti in range(ntile):
                t0 = ti * TT
                x = xs[ti % 2]
                P.dma(SP, x[:], xT.t[:, t0:t0 + TT].rearrange("(c p) t -> p c t", p=128), writes=[x])
                P.op(ACT, lambda e, x=x: e.activation(out=sq[:], in_=x[:], func=AF.Square), reads=[x], writes=[sq])
                P.op(DVE, lambda e, x=x: e.tensor_tensor(out=hb[:], in0=x[:], in1=gt[:].unsqueeze(2).to_broadcast([128, 16, TT]),
                                                         op=ALU.mult), reads=[x, gt], writes=[hb])
                _mm_group(P, pss, pss[:, 0:TT], [(ones[:], sq[:, k, :]) for k in range(16)], [ones, sq])
                P.op(ACT, lambda e: e.activation(out=rstd[:], in_=pss[:, 0:TT], func=AF.Ln, bias=RMS_EPS, scale=1.0 / D_MODEL),
                     reads=[pss], writes=[rstd])
                P.op(ACT, lambda e: e.activation(out=rstd[:], in_=rstd[:], func=AF.Exp, scale=-0.5), reads=[rstd], writes=[rstd])
                nsub = TT // 128

                def colsum(e):
                    for sub in range(nsub):
                        for k in range(16):
                            ins = e.matmul(psc[:, sub:sub + 1], sq[:, k, sub * 128:(sub + 1) * 128], ones[:, 0:1],
                                           start=(k == 0), stop=(k == 15))
                    return ins
                P.op(PE, colsum, reads=[sq, ones], writes=[psc])
                P.op(ACT, lambda e: e.activation(out=rcol[:, 0:nsub], in_=psc[:, 0:nsub], func=AF.Ln, bias=RMS_EPS,
                                                 scale=1.0 / D_MODEL), reads=[psc], writes=[rcol])
                P.op(ACT, lambda e: e.activation(out=rcol[:, 0:nsub], in_=rcol[:, 0:nsub], func=AF.Exp, scale=-0.5),
                     reads=[rcol], writes=[rcol])
                for m in range(8):
                    p = pm[m % 4]
                    ro = rawo[m % 2]
                    _mm_group(P, p, p[:, 0:TT], [(wqkv[:, k, m * 128:(m + 1) * 128], hb[:, k, :]) for k in range(16)], [wqkv, hb])
                    P.op(DVE, lambda e, p=p, ro=ro: e.tensor_tensor(out=ro[:], in0=p[:, 0:TT], in1=rstd[:], op=ALU.mult),
                         reads=[p, rstd], writes=[ro])
                    P.dma(POOL, rawT.t[m, :, t0:t0 + TT], ro[:], reads=[ro], writes=[rawT])
                for sub in range(nsub):
                    p = pm[sub % 4]
                    z = zo[sub % 2]
                    _mm_group(P, p, p[:, 0:512], [(hb[:, k, sub * 128:(sub + 1) * 128], wz[:, k, :]) for k in range(16)], [wz, hb])
                    P.op(ACT, lambda e, p=p, z=z, sub=sub: e.activation(out=z[:], in_=p[:, 0:512], func=AF.Silu,
                                                                        scale=rcol[:, sub:sub + 1]), reads=[p, rcol], writes=[z])
                    P.dma(POOL, zsil.t[t0 + sub * 128:t0 + (sub + 1) * 128, :], z[:], reads=[z], writes=[zsil])
                    g = gsb[sub % 2]
                    _mm_group(P, psg, psg[:, 0:16], [(hb[:, k, sub * 128:(sub + 1) * 128], wg[:, k, :]) for k in range(16)], [wg, hb])
                    P.op(ACT, lambda e, g=g, sub=sub: e.activation(out=g[:, 0:8], in_=psg[:, 0:8], func=AF.Sigmoid,
                                                                   scale=rcol[:, sub:sub + 1]), reads=[psg, rcol], writes=[g])
                    P.op(DVE, lambda e, sub=sub: e.scalar_tensor_tensor(out=gtmp[:], in0=psg[:, 8:16], scalar=rcol[:, sub:sub + 1],
                                                                        in1=dtb_s[:], op0=ALU.mult, op1=ALU.add),
                         reads=[psg, rcol, dtb_s], writes=[gtmp])
                    P.op(ACT, lambda e: e.activation(out=gtmp[:], in_=gtmp[:], func=AF.Exp), reads=[gtmp], writes=[gtmp])
                    P.op(ACT, lambda e: e.activation(out=gtmp[:], in_=gtmp[:], func=AF.Ln, bias=1.0), reads=[gtmp], writes=[gtmp])
                    P.op(DVE, lambda e, g=g: e.tensor_tensor(out=g[:, 8:16], in0=gtmp[:], in1=nega[:], op=ALU.mult),
                         reads=[gtmp, nega], writes=[g])
                    P.dma(POOL, gates.t[t0 + sub * 128:t0 + (sub + 1) * 128, :], g[:], reads=[g], writes=[gates])
            P.flush()
        P.pes = None

        if upto < 2:
            return nc
        with ExitStack() as pes:
            P.pes = pes
            cw = P.sb("cw", [128, 8, 5], F32)
            ones = P.sb("ones", [128, 128], BF16)
            identb = P.sb("identb", [128, 128], BF16)
            identf = P.sb("identf", [128, 128], F32)
            raw = [P.sb("raw%d" % i, [128, TT + 4], F32) for i in range(4)]
            acc = [P.sb("acc%d" % i, [128, TT], F32) for i in range(4)]
            sqbs = [P.sb("sqb%d" % i, [128, TT], BF16) for i in range(2)]
            rns = [P.sb("rn%d" % i, [128, TT], F32) for i in range(2)]
            nb = [P.sb("nb%d" % i, [128, TT], BF16) for i in range(4)]
            tk = [P.sb("tk%d" % i, [128, TT // 128, 128], BF16) for i in range(4)]
            psss = [P.ps("pss%d" % i, [128, 512]) for i in range(2)]
            ptr = [P.ps("ptr%d" % i, [128, 8, 128], BF16) for i in range(3)]
            P.dma(SP, cw[:], convw.t, writes=[cw])
            P.dma(SP, identf[:], cst.t[:, 20, :], writes=[identf])
            P.op(POOL, lambda e: e.memset(ones[:], 1.0), writes=[ones])
            P.op(DVE, lambda e: e.tensor_copy(out=identb[:], in_=identf[:]), reads=[identf], writes=[identb])
            nsub = TT // 128
            cnt = 0
            for si, L in enumerate(seq_lens):
                for ti in range(L // TT):
                    l0 = ti * TT
                    t0 = offs[si] + l0
                    for j in range(8):
                        r = raw[cnt % 4]
                        a = acc[cnt % 4]
                        o = nb[cnt % 4]
                        tkk = tk[cnt % 4]
                        pt = ptr[cnt % 3]
                        sqb = sqbs[cnt % 2]
                        rn = rns[cnt % 2]
                        pss = psss[cnt % 2]
                        ceng = DVE
                        cnt += 1
                        lo = 2 if l0 == 0 else 0
                        hi = TT + 2 if l0 + TT == L else TT + 4
                        if lo or hi < TT + 4:
                            P.op(POOL, lambda e, r=r: e.memset(r[:], 0.0), writes=[r])
                        P.dma(SP, r[:, lo:hi], rawT.t[j, :, t0 - 2 + lo:t0 - 2 + hi], reads=[rawT], writes=[r])
                        P.op(ceng, lambda e, r=r, a=a, j=j: e.tensor_scalar(out=a[:], in0=r[:, 0:TT], scalar1=cw[:, j, 0:1],
                                                                            scalar2=None, op0=ALU.mult), reads=[r, cw], writes=[a])
                        for tap in range(1, 5):
                            P.op(ceng, lambda e, r=r, a=a, j=j, tap=tap: e.scalar_tensor_tensor(
                                out=a[:], in0=r[:, tap:tap + TT], scalar=cw[:, j, tap:tap + 1], in1=a[:],
                                op0=ALU.mult, op1=ALU.add), reads=[r, cw, a], writes=[a])
                        P.op(ACT, lambda e, a=a: e.activation(out=a[:], in_=a[:], func=AF.Silu), reads=[a], writes=[a])
                        if j < 4:
                            P.op(ACT, lambda e, a=a, sqb=sqb: e.activation(out=sqb[:], in_=a[:], func=AF.Square), reads=[a], writes=[sqb])
                            _mm_group(P, pss, pss[:, 0:TT], [(ones[:], sqb[:])], [ones, sqb])
                            P.op(ACT, lambda e, rn=rn, pss=pss: e.activation(out=rn[:], in_=pss[:, 0:TT], func=AF.Ln, bias=L2_EPS),
                                 reads=[pss], writes=[rn])
                            P.op(ACT, lambda e, rn=rn: e.activation(out=rn[:], in_=rn[:], func=AF.Exp, scale=-0.5), reads=[rn], writes=[rn])
                            if j < 2:
                                qs = float(128 ** -0.5)
                                P.op(DVE, lambda e, a=a, o=o, qs=qs, rn=rn: e.scalar_tensor_tensor(
                                    out=o[:], in0=a[:], scalar=qs, in1=rn[:], op0=ALU.mult, op1=ALU.mult),
                                    reads=[a, rn], writes=[o])
                            else:
                                P.op(DVE, lambda e, a=a, o=o, rn=rn: e.tensor_tensor(out=o[:], in0=a[:], in1=rn[:], op=ALU.mult),
                                     reads=[a, rn], writes=[o])
                            P.dma(POOL, qkT.t[j, :, t0:t0 + TT], o[:], reads=[o], writes=[qkT])
                        else:
                            P.op(DVE, lambda e, a=a, o=o: e.tensor_copy(out=o[:], in_=a[:]), reads=[a], writes=[o])
                        if j >= 2:
                            def trs(e, o=o, pt=pt):
                                for sub in range(nsub):
                                    ins = e.transpose(pt[:, sub, :], o[:, sub * 128:(sub + 1) * 128], identb[:])
                                return ins
                            P.op(PE, trs, reads=[o, identb], writes=[pt])
                            P.op(ACT, lambda e, pt=pt, tkk=tkk: e.activation(out=tkk[:], in_=pt[:, 0:nsub, :], func=AF.Copy),
                                 reads=[pt], writes=[tkk])
                            if j < 4:
                                dst = ktok.t[t0:t0 + TT, j - 2, :]
                                dt_ = ktok
                            else:
                                dst = vtok.t[t0:t0 + TT, j - 4, :]
                                dt_ = vtok
                            P.dma(POOL, dst.rearrange("(s p) d -> p s d", p=128), tkk[:], reads=[tkk], writes=[dt_])
            P.flush()
        P.pes = None

        if upto < 3:
            return nc
        with ExitStack() as pes:
            P.pes = pes
            cs = P.sb("cs", [128, 36, 128], F32)
            onesf = P.sb("onesf", [128, 128], F32)
            identb = P.sb("identb", [128, 128], BF16)
            P.dma(SP, cs[:], cst.t, writes=[cs])
            P.op(POOL, lambda e: e.memset(onesf[:], 1.0), writes=[onesf])
            P.op(DVE, lambda e: e.tensor_copy(out=identb[:], in_=cs[:, 20, :]), reads=[cs], writes=[identb])
            pkq = P.ps("pkq", [128, 4, 128])
            pg = P.ps("pg", [128, 4, 128])
            pc = P.ps("pc", [128, 512])
            pT_raw = P.ps("pT", [128, 2, 512], BF16)
            pTd = [pT_raw, pT_raw]
            NST = 3 + 3 * NLVL + 3
            HLAG = (NST + 1) // 2
            S = {}
            SD = {}
            for d in range(2):
                SD[d] = dict(St=P.sb("St%d" % d, [128, 4, 128], F32), Sb=P.sb("Sb%d" % d, [128, 4, 128], BF16),
                             Stmp=P.sb("Stmp%d" % d, [128, 4, 128], F32))
            for d in range(4):
                pbank = P.ps("pb%d" % d, [128, 4, 128])
                S[d] = dict(
                    pX=pbank, pY=pbank,
                    kT=P.sb("kT%d" % d, [128, 2, 128], BF16), qT=P.sb("qT%d" % d, [128, 2, 128], BF16),
                    kt=P.sb("kt%d" % d, [128, 2, 128], BF16), vt=P.sb("vt%d" % d, [128, 4, 128], BF16),
                    gt=P.sb("gt%d" % d, [128, 16], F32),
                    Ug=P.sb("Ug%d" % d, [128, 4, 128], F32), gcc=P.sb("gcc%d" % d, [128, 4], F32),
                    Dm=P.sb("Dm%d" % d, [128, 4, 128], F32), Dsn=P.sb("Dsn%d" % d, [128, 4, 128], F32),
                    Dt=P.sb("Dt%d" % d, [128, 4, 128], F32),
                    LpT=P.sb("LpT%d" % d, [128, 4, 128], BF16), QKm=P.sb("QKm%d" % d, [128, 4, 128], BF16),
                    Pm=P.sb("Pm%d" % d, [128, 4, 128], BF16), Qm=P.sb("Qm%d" % d, [128, 4, 128], BF16),
                    X=P.sb("X%d" % d, [128, 4, 128], BF16), tmp=P.sb("tmp%d" % d, [128, 4, 128], BF16),
                    egc=P.sb("egc%d" % d, [128, 4], F32), Rg=P.sb("Rg%d" % d, [128, 4, 128], BF16),
                    nwT=P.sb("nwT%d" % d, [128, 4, 128], BF16), gl=P.sb("gl%d" % d, [128, 4], F32),
                    kd=P.sb("kd%d" % d, [128, 4], F32), egl=P.sb("egl%d" % d, [128, 4], F32),
                    vnb=P.sb("vnb%d" % d, [128, 4, 128], BF16), vns=P.sb("vns%d" % d, [128, 4, 128], BF16),
                    Asb=P.sb("Asb%d" % d, [128, 4, 128], F32), osb=P.sb("osb%d" % d, [128, 4, 128], F32),
                )

            def bc_h(ap2):
                return ap2.unsqueeze(1).to_broadcast([128, 4, 128])

            def bc_c(ap2):
                return ap2.unsqueeze(2).to_broadcast([128, 4, 128])

            def unit_stages(d, slot, t0):
                B = dict(S[slot])
                B.update(SD[d])
                last = (CH - 1) if d == 0 else 0
                st = []

                def s_load():
                    P.dma(SP, B["kT"][:], qkT.t[2:4, :, t0:t0 + CH].rearrange("h p c -> p h c"), reads=[qkT], writes=[B["kT"]])
                    P.dma(SP, B["qT"][:], qkT.t[0:2, :, t0:t0 + CH].rearrange("h p c -> p h c"), reads=[qkT], writes=[B["qT"]])
                    P.dma(SP, B["kt"][:], ktok.t[t0:t0 + CH, :, :], reads=[ktok], writes=[B["kt"]])
                    P.dma(SP, B["vt"][:], vtok.t[t0:t0 + CH, :, :], reads=[vtok], writes=[B["vt"]])
                    P.dma(SP, B["gt"][:], gates.t[t0:t0 + CH, :], reads=[gates], writes=[B["gt"]])
                st.append(s_load)

                def s_gates():
                    g = B["gt"]
                    for h in range(4):
                        P.op(POOL, lambda e, h=h: e.tensor_scalar(out=B["Ug"][:, h, :], in0=cs[:, d, :],
                                                                  scalar1=g[:, 8 + 4 * d + h:9 + 4 * d + h], scalar2=None,
                                                                  op0=ALU.mult), reads=[cs, g], writes=[B["Ug"]])
                    _mm_group(P, pg, pg[:].rearrange("p h c -> p (h c)"),
                              [(onesf[:], B["Ug"][:].rearrange("p h c -> p (h c)"))], [onesf, B["Ug"]])
                    _mm_group(P, pc, pc[:, 0:4], [(cs[:, d, :], g[:, 8 + 4 * d:12 + 4 * d])], [cs, g])
                    P.op(ACT, lambda e: e.activation(out=B["gcc"][:], in_=pc[:, 0:4], func=AF.Copy), reads=[pc], writes=[B["gcc"]])
                    P.op(DVE, lambda e: e.tensor_copy(out=B["gl"][:], in_=pg[:, :, last]), reads=[pg], writes=[B["gl"]])
                    for h in range(4):
                        P.op(DVE, lambda e, h=h: e.tensor_scalar(out=B["Dm"][:, h, :], in0=pg[:, h, :],
                                                                 scalar1=B["gcc"][:, h:h + 1], scalar2=0.0,
                                                                 op0=ALU.subtract, op1=ALU.min),
                             reads=[pg, B["gcc"]], writes=[B["Dm"]])
                    P.op(ACT, lambda e: e.activation(out=B["Dm"][:], in_=B["Dm"][:], func=AF.Exp), reads=[B["Dm"]], writes=[B["Dm"]])
                    P.op(DVE, lambda e: e.tensor_tensor(out=B["Dsn"][:], in0=B["Dm"][:], in1=bc_h(cs[:, 2 + d, :]), op=ALU.mult),
                         reads=[B["Dm"], cs], writes=[B["Dsn"]])
                    P.op(POOL, lambda e: e.tensor_tensor(out=B["Dt"][:], in0=B["Dm"][:], in1=bc_h(cs[:, 4 + d, :]), op=ALU.mult),
                         reads=[B["Dm"], cs], writes=[B["Dt"]])
                    P.op(ACT, lambda e: e.activation(out=B["egc"][:], in_=B["gcc"][:], func=AF.Exp), reads=[B["gcc"]], writes=[B["egc"]])
                    P.op(DVE, lambda e: e.tensor_tensor(out=B["kd"][:], in0=B["gl"][:], in1=B["gcc"][:], op=ALU.subtract),
                         reads=[B["gl"], B["gcc"]], writes=[B["kd"]])
                    P.op(ACT, lambda e: e.activation(out=B["kd"][:], in_=B["kd"][:], func=AF.Exp), reads=[B["kd"]], writes=[B["kd"]])
                    P.op(ACT, lambda e: e.activation(out=B["egl"][:], in_=B["gl"][:], func=AF.Exp), reads=[B["gl"]], writes=[B["egl"]])
                st.append(s_gates)

                def s_kk():
                    def fn(e):
                        for hq in range(2):
                            e.matmul(pkq[:, hq, :], B["kT"][:, hq, :], B["kT"][:, hq, :], start=True, stop=True)
                        for hq in range(2):
                            ins = e.matmul(pkq[:, 2 + hq, :], B["kT"][:, hq, :], B["qT"][:, hq, :], start=True, stop=True)
                        return ins
                    P.op(PE, fn, reads=[B["kT"], B["qT"]], writes=[pkq])
                    for hq in range(2):
                        P.op(DVE, lambda e, hq=hq: e.tensor_tensor(
                            out=B["LpT"][:, 2 * hq:2 * hq + 2, :],
                            in0=pkq[:, hq, :].unsqueeze(1).to_broadcast([128, 2, 128]),
                            in1=B["Dsn"][:, 2 * hq:2 * hq + 2, :], op=ALU.mult), reads=[pkq, B["Dsn"]], writes=[B["LpT"]])
                        P.op(DVE, lambda e, hq=hq: e.tensor_tensor(
                            out=B["QKm"][:, 2 * hq:2 * hq + 2, :],
                            in0=pkq[:, 2 + hq, :].unsqueeze(1).to_broadcast([128, 2, 128]),
                            in1=B["Dt"][:, 2 * hq:2 * hq + 2, :], op=ALU.mult), reads=[pkq, B["Dt"]], writes=[B["QKm"]])
                    bsl = B["gt"][:, 4 * d:4 * d + 4]
                    P.op(DVE, lambda e: e.tensor_tensor(out=B["Pm"][:], in0=bc_h(cs[:, 20, :]), in1=bc_c(bsl), op=ALU.mult),
                         reads=[cs, B["gt"]], writes=[B["Pm"]])
                    P.op(POOL, lambda e: e.tensor_tensor(out=B["Qm"][:], in0=bc_h(cs[:, 20, :]), in1=bc_c(bsl), op=ALU.mult),
                         reads=[cs, B["gt"]], writes=[B["Qm"]])
                st.append(s_kk)

                for l in range(NLVL):
                    def s_x(l=l):
                        def fn(e):
                            for h in range(4):
                                ins = e.matmul(B["pX"][:, h, :], B["LpT"][:, h, :], B["Pm"][:, h, :], start=True, stop=True)
                            return ins
                        P.op(PE, fn, reads=[B["LpT"], B["Pm"]], writes=[B["pX"]])
                        P.op(ACT, lambda e: e.activation(out=B["X"][:], in_=B["pX"][:], func=AF.Copy), reads=[B["pX"]], writes=[B["X"]])
                    st.append(s_x)

                    def s_y(l=l):
                        def fn(e):
                            for h in range(4):
                                ins = e.matmul(B["pY"][:, h, :], B["Qm"][:, h, :], B["X"][:, h, :], start=True, stop=True)
                            return ins
                        P.op(PE, fn, reads=[B["Qm"], B["X"]], writes=[B["pY"]])
                        mk = cs[:, 6 + d * NLVL + l, :]
                        P.op(DVE, lambda e: e.tensor_tensor(out=B["tmp"][:], in0=B["pY"][:], in1=bc_h(mk), op=ALU.mult),
                             reads=[B["pY"], cs], writes=[B["tmp"]])
                        P.op(DVE, lambda e: e.tensor_tensor(out=B["Pm"][:], in0=B["Pm"][:], in1=B["tmp"][:], op=ALU.add),
                             reads=[B["Pm"], B["tmp"]], writes=[B["Pm"]])
                    st.append(s_y)

                    def s_t(l=l):
                        def fn(e):
                            for h in range(4):
                                ins = e.transpose(pTd[d][:, d, h * 128:(h + 1) * 128], B["Pm"][:, h, :], identb[:])
                            return ins
                        P.op(PE, fn, reads=[B["Pm"], identb], writes=[pTd[d]])
                        P.op(ACT, lambda e: e.activation(out=B["Qm"][:].rearrange("p h c -> p (h c)"), in_=pTd[d][:, d, :], func=AF.Copy),
                             reads=[pTd[d]], writes=[B["Qm"]])
                    st.append(s_t)

                def s_scan1():
                    P.op(DVE, lambda e: e.tensor_tensor(out=B["Rg"][:], in0=B["Qm"][:], in1=bc_c(B["egc"][:]), op=ALU.mult),
                         reads=[B["Qm"], B["egc"]], writes=[B["Rg"]])

                    def fn(e):
                        for h in range(4):
                            ins = e.matmul(B["pX"][:, h, :], B["kt"][:, h // 2, :], B["Rg"][:, h, :], start=True, stop=True)
                        return ins
                    P.op(PE, fn, reads=[B["kt"], B["Rg"]], writes=[B["pX"]])
                    P.op(ACT, lambda e: e.activation(out=B["nwT"][:], in_=B["pX"][:], func=AF.Copy, scale=-1.0),
                         reads=[B["pX"]], writes=[B["nwT"]])
                st.append(s_scan1)

                def s_scan2():
                    def fn(e):
                        for h in range(4):
                            e.matmul(B["pY"][:, h, :], B["Qm"][:, h, :], B["vt"][:, h, :], start=True, stop=False)
                            ins = e.matmul(B["pY"][:, h, :], B["nwT"][:, h, :], B["Sb"][:, h, :], start=False, stop=True)
                        return ins
                    P.op(PE, fn, reads=[B["Qm"], B["vt"], B["nwT"], B["Sb"]], writes=[B["pY"]])
                    P.op(ACT, lambda e: e.activation(out=B["vnb"][:], in_=B["pY"][:], func=AF.Copy), reads=[B["pY"]], writes=[B["vnb"]])
                    P.op(DVE, lambda e: e.tensor_tensor(out=B["vns"][:], in0=B["pY"][:], in1=bc_c(B["kd"][:]), op=ALU.mult),
                         reads=[B["pY"], B["kd"]], writes=[B["vns"]])

                    def fa(e):
                        for h in range(4):
                            ins = e.matmul(B["pX"][:, h, :], B["qT"][:, h // 2, :], B["Sb"][:, h, :], start=True, stop=True)
                        return ins
                    P.op(PE, fa, reads=[B["qT"], B["Sb"]], writes=[B["pX"]])
                    P.op(DVE, lambda e: e.tensor_tensor(out=B["Asb"][:], in0=B["pX"][:], in1=bc_c(B["egc"][:]), op=ALU.mult),
                         reads=[B["pX"], B["egc"]], writes=[B["Asb"]])
                st.append(s_scan2)

                def s_scan3():
                    def fb(e):
                        for h in range(4):
                            ins = e.matmul(B["pY"][:, h, :], B["QKm"][:, h, :], B["vnb"][:, h, :], start=True, stop=True)
                        return ins
                    P.op(PE, fb, reads=[B["QKm"], B["vnb"]], writes=[B["pY"]])
                    P.op(DVE, lambda e: e.tensor_tensor(out=B["osb"][:], in0=B["pY"][:], in1=B["Asb"][:], op=ALU.add),
                         reads=[B["pY"], B["Asb"]], writes=[B["osb"]])
                    P.dma(ACT, osc.t[d, t0:t0 + CH, :], B["osb"][:].rearrange("p h c -> p (h c)"), reads=[B["osb"]], writes=[osc])

                    def fs(e):
                        for h in range(4):
                            ins = e.matmul(B["pX"][:, h, :], B["kt"][:, h // 2, :], B["vns"][:, h, :], start=True, stop=True)
                        return ins
                    P.op(PE, fs, reads=[B["kt"], B["vns"]], writes=[B["pX"]])
                    P.op(POOL, lambda e: e.tensor_tensor(out=B["Stmp"][:], in0=B["St"][:], in1=bc_c(B["egl"][:]), op=ALU.mult),
                         reads=[B["St"], B["egl"]], writes=[B["Stmp"]])
                    P.op(DVE, lambda e: e.tensor_tensor(out=B["St"][:], in0=B["pX"][:], in1=B["Stmp"][:], op=ALU.add),
                         reads=[B["pX"], B["Stmp"]], writes=[B["St"]])
                    P.op(ACT, lambda e: e.activation(out=B["Sb"][:], in_=B["St"][:], func=AF.Copy), reads=[B["St"]], writes=[B["Sb"]])
                st.append(s_scan3)
                return st

            for si, L in enumerate(seq_lens):
                nch = L // CH
                for d in range(2):
                    P.op(POOL, lambda e, d=d: e.memset(SD[d]["St"][:], 0.0), writes=[SD[d]["St"]])
                    P.op(POOL, lambda e, d=d: e.memset(SD[d]["Sb"][:], 0.0), writes=[SD[d]["Sb"]])
                live = {}
                for step in range((nch - 1) * HLAG + NST):
                    for d in range(2):
                        n_new = step // HLAG
                        if step % HLAG == 0 and n_new < nch:
                            ci = n_new if d == 0 else nch - 1 - n_new
                            live[(d, n_new)] = unit_stages(d, d * 2 + n_new % 2, offs[si] + ci * CH)
                        for n in (n_new - 1, n_new):
                            k = step - n * HLAG
                            if n >= 0 and n < nch and 0 <= k < NST:
                                live[(d, n)][k]()
                                if k == NST - 1:
                                    del live[(d, n)]
            P.flush()
        P.pes = None

        if upto < 4:
            return nc
        with ExitStack() as pes:
            P.pes = pes
            gn = P.sb("gn", [128, 128], F32)
            P.dma(SP, gn[:], gnw.t, writes=[gn])
            of = [P.sb("of%d" % i, [128, 4, 128], F32) for i in range(2)]
            ob = [P.sb("ob%d" % i, [128, 4, 128], F32) for i in range(2)]
            zz = [P.sb("zz%d" % i, [128, 4, 128], F32) for i in range(2)]
            sqj = P.sb("sqj", [128, 128], F32)
            ssum = P.sb("ssum", [128, 4], F32)
            oo = [P.sb("oo%d" % i, [128, 4, 128], BF16) for i in range(2)]
            for bi in range(NTOK // 128):
                t0 = bi * 128
                a, b, z, o = of[bi % 2], ob[bi % 2], zz[bi % 2], oo[bi % 2]
                P.dma(SP, a[:].rearrange("p h c -> p (h c)"), osc.t[0, t0:t0 + 128, :], reads=[osc], writes=[a])
                P.dma(SP, b[:].rearrange("p h c -> p (h c)"), osc.t[1, t0:t0 + 128, :], reads=[osc], writes=[b])
                P.dma(SP, z[:].rearrange("p h c -> p (h c)"), zsil.t[t0:t0 + 128, :], reads=[zsil], writes=[z])
                P.op(DVE, lambda e, a=a, b=b: e.tensor_tensor(out=a[:], in0=a[:], in1=b[:], op=ALU.add), reads=[a, b], writes=[a])
                P.op(POOL, lambda e: e.memset(ssum[:], 0.0), writes=[ssum])
                for h in range(4):
                    P.op(ACT, lambda e, a=a, h=h: e.activation(out=sqj[:], in_=a[:, h, :], func=AF.Square,
                                                               accum_out=ssum[:, h:h + 1]), reads=[a], writes=[sqj, ssum])
                P.op(ACT, lambda e: e.activation(out=ssum[:], in_=ssum[:], func=AF.Ln, bias=RMS_EPS, scale=1.0 / 128),
                     reads=[ssum], writes=[ssum])
                P.op(ACT, lambda e: e.activation(out=ssum[:], in_=ssum[:], func=AF.Exp, scale=-0.5), reads=[ssum], writes=[ssum])
                P.op(POOL, lambda e, z=z: e.tensor_tensor(out=z[:], in0=z[:], in1=gn[:].unsqueeze(1).to_broadcast([128, 4, 128]),
                                                          op=ALU.mult), reads=[z, gn], writes=[z])
                P.op(DVE, lambda e, a=a: e.tensor_tensor(out=a[:], in0=a[:], in1=ssum[:].unsqueeze(2).to_broadcast([128, 4, 128]),
                                                         op=ALU.mult), reads=[a, ssum], writes=[a])
                P.op(DVE, lambda e, a=a, z=z, o=o: e.tensor_tensor(out=o[:], in0=a[:], in1=z[:], op=ALU.mult),
                     reads=[a, z], writes=[o])
                P.dma(POOL, og.t[t0:t0 + 128, :], o[:].rearrange("p h c -> p (h c)"), reads=[o], writes=[og])
            P.flush()
        P.pes = None
    return nc


GDN_QK_HEADS = 16
GDN_V_HEADS = 32
GDN_Q_DIM = 2048
GDN_V_DIM = 4096
GDN_CONV_DIM = 8192


def prep_A(c, xT_all, norm_mix_pre, gdn_w_in, gdn_conv_w, gdn_a_log, gdn_dt_bias, gdn_norm_w, cst):
    w_in = gdn_w_in[0]
    qh = [2 * c, 2 * c + 1]
    vh = [4 * c + i for i in range(4)]
    qcols = np.concatenate([np.arange(h * 128, (h + 1) * 128) for h in qh])
    kcols = GDN_Q_DIM + qcols
    vcols = 2 * GDN_Q_DIM + np.concatenate([np.arange(h * 128, (h + 1) * 128) for h in vh])
    zcols = GDN_CONV_DIM + np.concatenate([np.arange(h * 128, (h + 1) * 128) for h in vh])
    o_b = GDN_CONV_DIM + GDN_V_DIM
    o_a = o_b + 2 * GDN_V_HEADS
    bcols = np.array([o_b + d * GDN_V_HEADS + h for d in range(2) for h in vh])
    acols = np.array([o_a + d * GDN_V_HEADS + h for d in range(2) for h in vh])
    conv_cols = np.concatenate([qcols, kcols, vcols])
    cw = gdn_conv_w[0][:, conv_cols]
    cw = np.ascontiguousarray(cw.reshape(5, 8, 128).transpose(2, 1, 0))
    al = np.array([gdn_a_log[0, d, h] for d in range(2) for h in vh], np.float32)
    db = np.array([gdn_dt_bias[0, d, h] for d in range(2) for h in vh], np.float32)
    return {
        "xT": xT_all,
        "gpre": np.ascontiguousarray(norm_mix_pre[0].reshape(16, 128).T),
        "Wqkv": np.ascontiguousarray(w_in[:, conv_cols]),
        "Wz": np.ascontiguousarray(w_in[:, zcols]),
        "Wg": np.ascontiguousarray(w_in[:, np.concatenate([bcols, acols])]),
        "convw": cw,
        "alog": np.ascontiguousarray(np.broadcast_to(al[None, :], (128, 8))),
        "dtb": np.ascontiguousarray(np.broadcast_to(db[None, :], (128, 8))),
        "gnw": np.ascontiguousarray(np.broadcast_to(gdn_norm_w[0][None, :], (128, 128))),
        "cst": cst,
    }


_PROF = []


def _launch(nc, in_maps, tag):
    res = run_bass_kernel_spmd(nc, in_maps, core_ids=list(range(NCORE)))
    t = getattr(res, "exec_time_ns", None)
    if t is not None:
        _PROF.append((tag, t))
    return res


def run_A(seq_lens, xT_all, norm_mix_pre, gdn_w_in, gdn_conv_w, gdn_a_log, gdn_dt_bias, gdn_norm_w):
    cst = _gdn_consts()
    nc = build_A(seq_lens)
    in_maps = [prep_A(c, xT_all, norm_mix_pre, gdn_w_in, gdn_conv_w, gdn_a_log, gdn_dt_bias, gdn_norm_w, cst)
               for c in range(NCORE)]
    res = _launch(nc, in_maps, "A")
    return np.concatenate([np.asarray(res.results[c]["og"]) for c in range(NCORE)], axis=1)


def lay_w(W):
    K, M = W.shape
    return np.ascontiguousarray(W.reshape(K // 128, 128, M // 128, 128).transpose(2, 1, 0, 3))


def lay_g(g):
    return np.ascontiguousarray(g.reshape(-1, 128).T)


class PostBufs:
    def __init__(self, P, TT, Kc):
        self.TT = TT
        self.xs = P.sb("xs", [128, 16, TT], F32)
        self.at = P.sb("at", [128, Kc, TT], BF16)
        self.mT = P.sb("mT", [128, 16, TT], F32)
        self.sq = P.sb("sq", [128, 16, TT], BF16)
        self.hb = P.sb("hb", [128, 16, TT], BF16)
        self.hid = P.sb("hid", [128, 64, TT], BF16)
        self.rstd = P.sb("rstd", [128, TT], F32)
        self.r2 = P.sb("r2", [128, TT], F32)
        self.tmpf = [P.sb("tmpf%d" % i, [128, TT], F32) for i in range(2)]
        self.wA = [P.sb("wA%d" % i, [128, Kc, 128], BF16) for i in range(2)]
        self.wI = [P.sb("wI%d" % i, [128, 16, 128], BF16) for i in range(3)]
        self.wO = [P.sb("wO%d" % i, [128, 32, 128], BF16) for i in range(2)]
        self.ones = P.sb("ones", [128, 128], BF16)
        self.pacc = [P.ps("pacc%d" % i, [128, 512]) for i in range(4)]
        self.pss = P.ps("pss", [128, 512])
        self.pi = 0
        P.op(POOL, lambda e: e.memset(self.ones[:], 1.0), writes=[self.ones])

    def nextp(self):
        p = self.pacc[self.pi % 4]
        self.pi += 1
        return p


class MlaBufs:
    def __init__(self, P, TT):
        self.TT = TT
        self.xs = P.sb("xs", [128, 16, TT], F32)
        self.sq = P.sb("sq", [128, 16, TT], BF16)
        self.hb = P.sb("hb", [128, 16, TT], BF16)
        self.rstd = P.sb("rstd", [128, TT], F32)
        self.ones = P.sb("ones", [128, 128], BF16)
        self.pacc = [P.ps("pacc%d" % i, [128, 512]) for i in range(4)]
        self.pss = P.ps("pss", [128, 512])
        self.pi = 0
        P.op(POOL, lambda e: e.memset(self.ones[:], 1.0), writes=[self.ones])

    def nextp(self):
        p = self.pacc[self.pi % 4]
        self.pi += 1
        return p


def _rms_rstd(P, Bf, src_t, nch, dim, out_t):
    TT = Bf.TT
    _mm_group(P, Bf.pss, Bf.pss[:, 0:TT], [(Bf.ones[:], Bf.sq[:, k, :]) for k in range(nch)], [Bf.ones, Bf.sq])
    P.op(ACT, lambda e: e.activation(out=out_t[:], in_=Bf.pss[:, 0:TT], func=AF.Ln, bias=RMS_EPS, scale=1.0 / dim),
         reads=[Bf.pss], writes=[out_t])
    P.op(ACT, lambda e: e.activation(out=out_t[:], in_=out_t[:], func=AF.Exp, scale=-0.5), reads=[out_t], writes=[out_t])


def _norm_residual(P, Bf, g_t):
    TT = Bf.TT
    P.op(POOL, lambda e: e.tensor_tensor(out=Bf.sq[:], in0=Bf.mT[:], in1=Bf.mT[:], op=ALU.mult), reads=[Bf.mT], writes=[Bf.sq])
    _rms_rstd(P, Bf, Bf.mT, 16, D_MODEL, Bf.rstd)
    for mc in range(16):
        tf = Bf.tmpf[mc % 2]
        P.op(POOL, lambda e, mc=mc, tf=tf: e.tensor_tensor(out=tf[:], in0=Bf.mT[:, mc, :], in1=Bf.rstd[:], op=ALU.mult),
             reads=[Bf.mT, Bf.rstd], writes=[tf])
        P.op(DVE, lambda e, mc=mc, tf=tf: e.scalar_tensor_tensor(out=Bf.xs[:, mc, :], in0=tf[:], scalar=g_t[:, mc:mc + 1],
                                                                 in1=Bf.xs[:, mc, :], op0=ALU.mult, op1=ALU.add),
             reads=[tf, g_t, Bf.xs], writes=[Bf.xs])


def post_block(P, Bf, Kc, a_src, Wmix_b, x_src, g_post, g_fpre, g_fpost, Wfi_b, Wfo_b, t0):
    TT = Bf.TT
    P.dma(SP, Bf.xs[:], x_src.t[:, t0:t0 + TT].rearrange("(c p) t -> p c t", p=128), reads=[x_src], writes=[Bf.xs])
    P.dma(SP, Bf.at[:], a_src.t[:, t0:t0 + TT].rearrange("(c p) t -> p c t", p=128), reads=[a_src], writes=[Bf.at])
    for mc in range(16):
        w = Bf.wA[mc % 2]
        P.dma(SP, w[:], Wmix_b.t[mc], reads=[Wmix_b], writes=[w])
        p = Bf.nextp()
        _mm_group(P, p, p[:, 0:TT], [(w[:, k, :], Bf.at[:, k, :]) for k in range(Kc)], [w, Bf.at])
        P.op(ACT, lambda e, p=p, mc=mc: e.activation(out=Bf.mT[:, mc, :], in_=p[:, 0:TT], func=AF.Copy), reads=[p], writes=[Bf.mT])
    _norm_residual(P, Bf, g_post)
    P.op(ACT, lambda e: e.activation(out=Bf.sq[:], in_=Bf.xs[:], func=AF.Square), reads=[Bf.xs], writes=[Bf.sq])
    P.op(DVE, lambda e: e.tensor_tensor(out=Bf.hb[:], in0=Bf.xs[:], in1=g_fpre[:].unsqueeze(2).to_broadcast([128, 16, TT]),
                                        op=ALU.mult), reads=[Bf.xs, g_fpre], writes=[Bf.hb])
    _rms_rstd(P, Bf, Bf.xs, 16, D_MODEL, Bf.rstd)
    P.op(DVE, lambda e: e.tensor_tensor(out=Bf.r2[:], in0=Bf.rstd[:], in1=Bf.rstd[:], op=ALU.mult), reads=[Bf.rstd], writes=[Bf.r2])
    for mc in range(64):
        w = Bf.wI[mc % 3]
        P.dma(SP, w[:], Wfi_b.t[mc], reads=[Wfi_b], writes=[w])
        p = Bf.nextp()
        _mm_group(P, p, p[:, 0:TT], [(w[:, k, :], Bf.hb[:, k, :]) for k in range(16)], [w, Bf.hb])
        tf = Bf.tmpf[mc % 2]
        P.op(ACT, lambda e, p=p, tf=tf: e.activation(out=tf[:], in_=p[:, 0:TT], func=AF.Relu), reads=[p], writes=[tf])
        P.op(POOL, lambda e, mc=mc, tf=tf: e.tensor_tensor(out=Bf.hid[:, mc, :], in0=tf[:], in1=tf[:], op=ALU.mult),
             reads=[tf], writes=[Bf.hid])
    for mc in range(16):
        p = Bf.nextp()
        for hf in range(2):
            w = Bf.wO[hf]
            P.dma(SP, w[:], Wfo_b.t[mc, :, hf * 32:(hf + 1) * 32, :], reads=[Wfo_b], writes=[w])
            _mm_group(P, p, p[:, 0:TT], [(w[:, k, :], Bf.hid[:, hf * 32 + k, :]) for k in range(32)], [w, Bf.hid],
                      first=(hf == 0), last=(hf == 1))
        P.op(DVE, lambda e, p=p, mc=mc: e.tensor_tensor(out=Bf.mT[:, mc, :], in0=p[:, 0:TT], in1=Bf.r2[:], op=ALU.mult),
             reads=[p, Bf.r2], writes=[Bf.mT])
    _norm_residual(P, Bf, g_fpost)


def cast_w(P, dst, src, nsplit):
    n = src.t.shape[0]
    step = max(1, n // nsplit)
    for i in range(0, n, step):
        P.dma(POOL, dst.t[i:i + step], src.t[i:i + step], reads=[src], writes=[dst])


def build_B(NT, TT=256, debug=False):
    nc = bass.Bass("TRN2", target_bir_lowering=False)
    with ExitStack() as es:
        P = Prog(nc, es)
        xT = P.dram("xT", [D_MODEL, NT], F32, kind="ExternalInput")
        ogT = P.dram("ogT", [4096, NT], BF16, kind="ExternalInput")
        Wout = P.dram("Wout", [16, 128, 32, 128], F32, kind="ExternalInput")
        Wfi = P.dram("Wfi", [64, 128, 16, 128], F32, kind="ExternalInput")
        Wfo = P.dram("Wfo", [16, 128, 64, 128], F32, kind="ExternalInput")
        Wa = P.dram("Wa", [12, 128, 16, 128], F32, kind="ExternalInput")
        Wq = P.dram("Wq", [48, 128, 6, 128], F32, kind="ExternalInput")
        Wk = P.dram("Wk", [16, 128, 4, 128], F32, kind="ExternalInput")
        Wv = P.dram("Wv", [128, 4, 2048], F32, kind="ExternalInput")
        gv = P.dram("gv", [128, 4, 16], F32, kind="ExternalInput")
        gq = P.dram("gq", [128, 6], F32, kind="ExternalInput")
        gkv = P.dram("gkv", [128, 4], F32, kind="ExternalInput")
        C2 = P.dram("C2", [64, NT], F32, kind="ExternalInput")
        S2 = P.dram("S2", [64, NT], F32, kind="ExternalInput")
        E64 = P.dram("E64", [128, 65], F32, kind="ExternalInput")
        x2T = P.dram("x2T", [D_MODEL, NT], F32, kind="ExternalOutput")
        QN = P.dram("QN", [16, 128, NT], BF16, kind="ExternalOutput")
        QR = P.dram("QR", [16, 65, NT], BF16, kind="ExternalOutput")
        KN = P.dram("KN", [16, 128, NT], BF16, kind="ExternalOutput")
        KR = P.dram("KR", [65, NT], BF16, kind="ExternalOutput")
        V = P.dram("V", [NT, 2048], BF16, kind="ExternalOutput")
        Wout_b = P.dram("Wout_b", [16, 128, 32, 128], BF16)
        Wfi_b = P.dram("Wfi_b", [64, 128, 16, 128], BF16)
        Wfo_b = P.dram("Wfo_b", [16, 128, 64, 128], BF16)
        Wa_b = P.dram("Wa_b", [12, 128, 16, 128], BF16)
        Wq_b = P.dram("Wq_b", [48, 128, 6, 128], BF16)
        Wk_b = P.dram("Wk_b", [16, 128, 4, 128], BF16)
        Wv_b = P.dram("Wv_b", [128, 4, 2048], BF16)
        cast_w(P, Wout_b, Wout, 4)
        cast_w(P, Wfi_b, Wfi, 8)
        cast_w(P, Wfo_b, Wfo, 8)
        cast_w(P, Wa_b, Wa, 2)
        cast_w(P, Wq_b, Wq, 2)
        cast_w(P, Wk_b, Wk, 1)
        cast_w(P, Wv_b, Wv, 1)
        with ExitStack() as pes:
            P.pes = pes
            Bf = PostBufs(P, TT, 32)
            g_ts = [P.sb("g_t%d" % i, [128, 16], F32) for i in range(3)]
            for i in range(3):
                P.dma(SP, g_ts[i][:], gv.t[:, i, :], writes=[g_ts[i]])
            for ti in range(NT // TT):
                t0 = ti * TT
                post_block(P, Bf, 32, ogT, Wout_b, xT, g_ts[0], g_ts[1], g_ts[2], Wfi_b, Wfo_b, t0)
                P.dma(POOL, x2T.t[:, t0:t0 + TT].rearrange("(c p) t -> p c t", p=128), Bf.xs[:], reads=[Bf.xs], writes=[x2T])
            P.flush()
        P.pes = None
        with ExitStack() as pes:
            P.pes = pes
            Bf = MlaBufs(P, TT)
            g_t3 = P.sb("g_t3", [128, 16], F32)
            gq_t = P.sb("gq_t", [128, 6], F32)
            gkv_t = P.sb("gkv_t", [128, 4], F32)
            e64f = P.sb("e64f", [128, 65], F32)
            e64 = P.sb("e64", [128, 65], BF16)
            wv = P.sb("wv", [128, 4, 2048], BF16)
            P.dma(SP, g_t3[:], gv.t[:, 3, :], writes=[g_t3])
            P.dma(SP, gq_t[:], gq.t, writes=[gq_t])
            P.dma(SP, gkv_t[:], gkv.t, writes=[gkv_t])
            P.dma(SP, e64f[:], E64.t, writes=[e64f])
            P.op(DVE, lambda e: e.tensor_copy(out=e64[:], in_=e64f[:]), reads=[e64f], writes=[e64])
            P.dma(SP, wv[:], Wv_b.t, reads=[Wv_b], writes=[wv])
            cq = P.sb("cq", [128, 6, TT], F32)
            ckv = P.sb("ckv", [128, 4, TT], F32)
            cqb = P.sb("cqb", [128, 6, TT], BF16)
            ckvb = P.sb("ckvb", [128, 4, TT], BF16)
            rq = P.sb("rq", [128, TT], F32)
            rkv = P.sb("rkv", [128, TT], F32)
            rkc = P.sb("rkc", [128, 4], F32)
            c2 = P.sb("c2", [64, TT], F32)
            s2 = P.sb("s2", [64, TT], F32)
            c2r = P.sb("c2r", [64, TT], F32)
            s2r = P.sb("s2r", [64, TT], F32)
            Ak = P.sb("Ak", [64, TT], F32)
            Bk = P.sb("Bk", [64, TT], F32)
            krf = P.sb("krf", [64, TT], F32)
            krt = [P.sb("krt%d" % i, [65, TT], BF16) for i in range(2)]
            qrt = [P.sb("qrt%d" % i, [65, TT], BF16) for i in range(2)]
            qnt = [P.sb("qnt%d" % i, [128, TT], BF16) for i in range(2)]
            knall = P.sb("knall", [128, 16, TT], BF16)
            prn = P.sb("prn", [128, TT], BF16)
            prr = P.sb("prr", [64, TT], BF16)
            vt = [P.sb("vt%d" % i, [128, 2048], BF16) for i in range(2)]
            wa_t = [P.sb("wa_t%d" % i, [128, 16, 128], BF16) for i in range(2)]
            wq_t = [P.sb("wq_t%d" % i, [128, 6, 128], BF16) for i in range(3)]
            wk_t = [P.sb("wk_t%d" % i, [128, 4, 128], BF16) for i in range(2)]
            psc = P.ps("psc", [128, 512])
            p65 = P.ps("p65", [128, 512])
            for i in range(2):
                P.op(POOL, lambda e, i=i: e.memset(krt[i][:], 1.0), writes=[krt[i]])
            nsub = TT // 128
            for ti in range(NT // TT):
                t0 = ti * TT
                xs = Bf.xs
                P.dma(SP, xs[:], x2T.t[:, t0:t0 + TT].rearrange("(c p) t -> p c t", p=128), reads=[x2T], writes=[xs])
                P.dma(SP, c2[:], C2.t[:, t0:t0 + TT], writes=[c2])
                P.dma(SP, s2[:], S2.t[:, t0:t0 + TT], writes=[s2])
                P.op(ACT, lambda e: e.activation(out=Bf.sq[:], in_=xs[:], func=AF.Square), reads=[xs], writes=[Bf.sq])
                P.op(DVE, lambda e: e.tensor_tensor(out=Bf.hb[:], in0=xs[:], in1=g_t3[:].unsqueeze(2).to_broadcast([128, 16, TT]),
                                                    op=ALU.mult), reads=[xs, g_t3], writes=[Bf.hb])
                _rms_rstd(P, Bf, xs, 16, D_MODEL, Bf.rstd)
                for mc in range(12):
                    w = wa_t[mc % 2]
                    P.dma(SP, w[:], Wa_b.t[mc], reads=[Wa_b], writes=[w])
                    p = Bf.nextp()
                    _mm_group(P, p, p[:, 0:TT], [(w[:, k, :], Bf.hb[:, k, :]) for k in range(16)], [w, Bf.hb])
                    if mc < 6:
                        P.op(DVE, lambda e, p=p, mc=mc: e.tensor_tensor(out=cq[:, mc, :], in0=p[:, 0:TT], in1=Bf.rstd[:], op=ALU.mult),
                             reads=[p, Bf.rstd], writes=[cq])
                    elif mc < 10:
                        P.op(DVE, lambda e, p=p, mc=mc: e.tensor_tensor(out=ckv[:, mc - 6, :], in0=p[:, 0:TT], in1=Bf.rstd[:], op=ALU.mult),
                             reads=[p, Bf.rstd], writes=[ckv])
                    else:
                        dst = Ak if mc == 10 else Bk
                        P.op(DVE, lambda e, p=p, dst=dst: e.tensor_tensor(out=dst[:], in0=p[0:64, 0:TT], in1=Bf.rstd[0:64, :], op=ALU.mult),
                             reads=[p, Bf.rstd], writes=[dst])
                kr = krt[ti % 2]
                P.op(DVE, lambda e: e.tensor_tensor(out=Ak[:], in0=Ak[:], in1=c2[:], op=ALU.mult), reads=[Ak, c2], writes=[Ak])
                P.op(DVE, lambda e: e.tensor_tensor(out=Bk[:], in0=Bk[:], in1=s2[:], op=ALU.mult), reads=[Bk, s2], writes=[Bk])
                P.op(DVE, lambda e: e.tensor_tensor(out=krf[:], in0=Ak[:], in1=Bk[:], op=ALU.add), reads=[Ak, Bk], writes=[krf])
                P.op(ACT, lambda e, kr=kr: e.activation(out=kr[0:64, :], in_=krf[:], func=AF.Copy), reads=[krf], writes=[kr])
                P.dma(POOL, KR.t[:, t0:t0 + TT], kr[:], reads=[kr], writes=[KR])
                P.op(POOL, lambda e: e.tensor_tensor(out=Bf.sq[:, 0:6, :], in0=cq[:], in1=cq[:], op=ALU.mult), reads=[cq], writes=[Bf.sq])
                _rms_rstd(P, Bf, cq, 6, 768, rq)
                P.op(DVE, lambda e: e.tensor_tensor(out=cqb[:], in0=cq[:], in1=gq_t[:].unsqueeze(2).to_broadcast([128, 6, TT]), op=ALU.mult),
                     reads=[cq, gq_t], writes=[cqb])
                P.op(POOL, lambda e: e.tensor_tensor(out=Bf.sq[:, 0:4, :], in0=ckv[:], in1=ckv[:], op=ALU.mult), reads=[ckv], writes=[Bf.sq])
                _rms_rstd(P, Bf, ckv, 4, 512, rkv)

                def colsum(e):
                    for sub in range(nsub):
                        for k in range(4):
                            ins = e.matmul(psc[:, sub:sub + 1], Bf.sq[:, k, sub * 128:(sub + 1) * 128], Bf.ones[:, 0:1],
                                           start=(k == 0), stop=(k == 3))
                    return ins
                P.op(PE, colsum, reads=[Bf.sq, Bf.ones], writes=[psc])
                P.op(ACT, lambda e: e.activation(out=rkc[:, 0:nsub], in_=psc[:, 0:nsub], func=AF.Ln, bias=RMS_EPS, scale=1.0 / 512),
                     reads=[psc], writes=[rkc])
                P.op(ACT, lambda e: e.activation(out=rkc[:, 0:nsub], in_=rkc[:, 0:nsub], func=AF.Exp, scale=-0.5), reads=[rkc], writes=[rkc])
                P.op(DVE, lambda e: e.tensor_tensor(out=ckvb[:], in0=ckv[:], in1=gkv_t[:].unsqueeze(2).to_broadcast([128, 4, TT]), op=ALU.mult),
                     reads=[ckv, gkv_t], writes=[ckvb])
                P.op(DVE, lambda e: e.tensor_tensor(out=c2r[:], in0=c2[:], in1=rq[0:64, :], op=ALU.mult), reads=[c2, rq], writes=[c2r])
                P.op(DVE, lambda e: e.tensor_tensor(out=s2r[:], in0=s2[:], in1=rq[0:64, :], op=ALU.mult), reads=[s2, rq], writes=[s2r])
                for h in range(16):
                    w = wk_t[h % 2]
                    P.dma(SP, w[:], Wk_b.t[h], reads=[Wk_b], writes=[w])
                    p = Bf.nextp()
                    _mm_group(P, p, p[:, 0:TT], [(w[:, k, :], ckvb[:, k, :]) for k in range(4)], [w, ckvb])
                    P.op(DVE, lambda e, p=p, h=h: e.tensor_tensor(out=knall[:, h, :], in0=p[:, 0:TT], in1=rkv[:], op=ALU.mult),
                         reads=[p, rkv], writes=[knall])
                P.dma(POOL, KN.t[:, :, t0:t0 + TT].rearrange("h p t -> p h t"), knall[:], reads=[knall], writes=[KN])
                for sub in range(nsub):
                    v = vt[sub % 2]
                    for g4 in range(4):
                        p = Bf.nextp()
                        _mm_group(P, p, p[:, 0:512], [(ckvb[:, k, sub * 128:(sub + 1) * 128], wv[:, k, g4 * 512:(g4 + 1) * 512])
                                                     for k in range(4)], [ckvb, wv])
                        P.op(ACT, lambda e, p=p, v=v, g4=g4, sub=sub: e.activation(out=v[:, g4 * 512:(g4 + 1) * 512], in_=p[:, 0:512],
                                                                                    func=AF.Copy, scale=rkc[:, sub:sub + 1]),
                             reads=[p, rkc], writes=[v])
                    P.dma(POOL, V.t[t0 + sub * 128:t0 + (sub + 1) * 128, :], v[:], reads=[v], writes=[V])
                for h in range(16):
                    qn = qnt[h % 2]
                    qr = qrt[h % 2]
                    w0, w1, w2 = wq_t[0], wq_t[1], wq_t[2]
                    P.dma(SP, w0[:], Wq_b.t[3 * h], reads=[Wq_b], writes=[w0])
                    P.dma(SP, w1[:], Wq_b.t[3 * h + 1], reads=[Wq_b], writes=[w1])
                    P.dma(SP, w2[:], Wq_b.t[3 * h + 2], reads=[Wq_b], writes=[w2])
                    p = Bf.nextp()
                    _mm_group(P, p, p[:, 0:TT], [(w0[:, k, :], cqb[:, k, :]) for k in range(6)], [w0, cqb])
                    P.op(DVE, lambda e, p=p, qn=qn: e.tensor_tensor(out=qn[:], in0=p[:, 0:TT], in1=rq[:], op=ALU.mult),
                         reads=[p, rq], writes=[qn])
                    P.dma(POOL, QN.t[h, :, t0:t0 + TT], qn[:], reads=[qn], writes=[QN])
                    pa = Bf.nextp()
                    _mm_group(P, pa, pa[0:64, 0:TT], [(w1[:, k, 0:64], cqb[:, k, :]) for k in range(6)], [w1, cqb])
                    P.op(DVE, lambda e, pa=pa: e.tensor_tensor(out=Ak[:], in0=pa[0:64, 0:TT], in1=c2r[:], op=ALU.mult),
                         reads=[pa, c2r], writes=[Ak])
                    pb = Bf.nextp()
                    _mm_group(P, pb, pb[0:64, 0:TT], [(w2[:, k, 0:64], cqb[:, k, :]) for k in range(6)], [w2, cqb])
                    P.op(DVE, lambda e, pb=pb: e.tensor_tensor(out=Bk[:], in0=pb[0:64, 0:TT], in1=s2r[:], op=ALU.mult),
                         reads=[pb, s2r], writes=[Bk])
                    P.op(DVE, lambda e, qr=qr: e.tensor_tensor(out=qr[0:64, :], in0=Ak[:], in1=Bk[:], op=ALU.add),
                         reads=[Ak, Bk], writes=[qr])
                    P.op(POOL, lambda e, qn=qn, h=h: e.tensor_tensor(out=prn[:], in0=qn[:], in1=knall[:, h, :], op=ALU.mult),
                         reads=[qn, knall], writes=[prn])
                    P.op(POOL, lambda e, qr=qr, kr=kr: e.tensor_tensor(out=prr[:], in0=qr[0:64, :], in1=kr[0:64, :], op=ALU.mult),
                         reads=[qr, kr], writes=[prr])
                    _mm_group(P, p65, p65[0:65, 0:TT], [(e64[:, :], prn[:]), (e64[0:64, :], prr[:])], [e64, prn, prr])
                    P.op(ACT, lambda e, qr=qr: e.activation(out=qr[64:65, :], in_=p65[64:65, 0:TT], func=AF.Copy, scale=-1.0),
                         reads=[p65], writes=[qr])
                    P.dma(POOL, QR.t[h, :, t0:t0 + TT], qr[:], reads=[qr], writes=[QR])
            P.flush()
        P.pes = None
    return nc


MLA_HEADS = 16
ROPE_THETA = 10000.0


def _core_tokens(seq_lens, c):
    offs = np.concatenate([[0], np.cumsum(seq_lens)]).astype(int)
    idx, pos = [], []
    for s, L in enumerate(seq_lens):
        n = L // NCORE
        p = np.arange(c * n, (c + 1) * n)
        idx.append(offs[s] + p)
        pos.append(p)
    return np.concatenate(idx), np.concatenate(pos)


def _rope_tabs(pos):
    half = 32
    inv_freq = (np.float32(ROPE_THETA) ** (-(np.arange(half, dtype=np.float32) / np.float32(half)))).astype(np.float32)
    ang = (pos.astype(np.float32)[:, None] * inv_freq[None, :]).astype(np.float32)
    cos = np.cos(ang.astype(np.float64)).astype(np.float32)
    sin = np.sin(ang.astype(np.float64)).astype(np.float32)
    C2 = np.ascontiguousarray(np.concatenate([cos, cos], 1).T)
    S2 = np.ascontiguousarray(np.concatenate([-sin, sin], 1).T)
    return C2, S2


def prep_B_weights(norm_mix_post, norm_ffn_pre, norm_ffn_post, norm_mix_pre, gdn_w_out, ffn_w_in, ffn_w_out,
                   mla_w_a, mla_q_a_norm, mla_w_q_b, mla_kv_a_norm, mla_w_kv_b):
    wa = mla_w_a[0]
    rope = wa[:, 1280:1344]
    sw = np.concatenate([rope[:, 32:], rope[:, :32]], 1)
    z64 = np.zeros((2048, 64), np.float32)
    wa_p = np.concatenate([wa[:, :1280], rope, z64, sw, z64], 1)
    wq = mla_w_q_b[0]
    z = np.zeros((768, 64), np.float32)
    cols = []
    for h in range(16):
        b = h * 192
        r = wq[:, b + 128:b + 192]
        cols += [wq[:, b:b + 128], r, z, np.concatenate([r[:, 32:], r[:, :32]], 1), z]
    wq_p = np.concatenate(cols, 1)
    wkv = mla_w_kv_b[0].reshape(512, 16, 256)
    wk = np.ascontiguousarray(wkv[:, :, :128].reshape(512, 2048))
    wv = np.ascontiguousarray(wkv[:, :, 128:].reshape(512, 2048))
    e64 = np.zeros((128, 65), np.float32)
    e64[:, 64] = 1.0
    return {
        "Wout": lay_w(gdn_w_out[0]), "Wfi": lay_w(ffn_w_in[0]), "Wfo": lay_w(ffn_w_out[0]),
        "Wa": lay_w(wa_p), "Wq": lay_w(wq_p), "Wk": lay_w(wk),
        "Wv": np.ascontiguousarray(wv.reshape(4, 128, 2048).transpose(1, 0, 2)),
        "gv": np.ascontiguousarray(np.stack([lay_g(norm_mix_post[0]), lay_g(norm_ffn_pre[0]), lay_g(norm_ffn_post[0]),
                                            lay_g(norm_mix_pre[1])], 1)),
        "gq": lay_g(mla_q_a_norm[0]), "gkv": lay_g(mla_kv_a_norm[0]), "E64": e64,
    }


def build_C(slabs, TQ=512, TT=256):
    NT = sum(slabs)
    NTOK = NCORE * NT
    scale = float(192 ** -0.5)
    nc = bass.Bass("TRN2", target_bir_lowering=False)
    with ExitStack() as es:
        P = Prog(nc, es)
        QN = P.dram("QN", [16, 128, NT], BF16, kind="ExternalInput")
        QR = P.dram("QR", [16, 65, NT], BF16, kind="ExternalInput")
        KNa = P.dram("KNa", [16, 128, NTOK], BF16, kind="ExternalInput")
        KRa = P.dram("KRa", [65, NTOK], BF16, kind="ExternalInput")
        Va = P.dram("Va", [16, NCORE, 128, NT // 128, 128], BF16, kind="ExternalInput")
        x2T = P.dram("x2T", [D_MODEL, NT], F32, kind="ExternalInput")
        Wo = P.dram("Wo", [16, 128, 16, 128], F32, kind="ExternalInput")
        Wfi = P.dram("Wfi", [64, 128, 16, 128], F32, kind="ExternalInput")
        Wfo = P.dram("Wfo", [16, 128, 64, 128], F32, kind="ExternalInput")
        gv = P.dram("gv", [128, 3, 16], F32, kind="ExternalInput")
        yT = P.dram("yT", [D_MODEL, NT], F32, kind="ExternalOutput")
        aoT = P.dram("aoT", [D_MODEL, NT], BF16)
        Wo_b = P.dram("Wo_b", [16, 128, 16, 128], BF16)
        Wfi_b = P.dram("Wfi_b", [64, 128, 16, 128], BF16)
        Wfo_b = P.dram("Wfo_b", [16, 128, 64, 128], BF16)
        cast_w(P, Wo_b, Wo, 2)
        cast_w(P, Wfi_b, Wfi, 8)
        cast_w(P, Wfo_b, Wfo, 8)
        SEG = max(slabs)
        with ExitStack() as pes:
            P.pes = pes
            ones = P.sb("ones", [128, 128], BF16)
            P.op(POOL, lambda e: e.memset(ones[:], 1.0), writes=[ones])
            QG = 2
            qn = [P.sb("qn%d" % i, [128, QG * TQ], BF16) for i in range(2)]
            qr = [P.sb("qr%d" % i, [65, QG * TQ], BF16) for i in range(2)]
            kn = [P.sb("kn%d" % i, [128, SEG], BF16) for i in range(3)]
            kr = [P.sb("kr%d" % i, [65, SEG], BF16) for i in range(3)]
            vv = [P.sb("vv%d" % i, [128, SEG // 128, 128], BF16) for i in range(3)]
            pt = [P.sb("pt%d" % i, [128, TQ], BF16) for i in range(4)]
            rinv = P.sb("rinv", [128, TQ], F32)
            ao = [P.sb("ao%d" % i, [128, TQ], BF16) for i in range(2)]
            ps_s = [P.ps("ps_s%d" % i, [128, 512]) for i in range(4)]
            ps_o = [P.ps("ps_o%d" % i, [128, 512]) for i in range(QG)]
            ps_r = [P.ps("ps_r%d" % i, [128, 512]) for i in range(QG)]
            soff = 0
            cnt = 0
            hc = 0
            sc = 0
            ac = 0
            for slab in slabs:
                nkb = slab // 128
                for qg in range(slab // (QG * TQ)):
                    q0 = soff + qg * QG * TQ
                    for h in range(16):
                        qn_, qr_ = qn[hc % 2], qr[hc % 2]
                        hc += 1
                        P.dma(SP, qn_[:], QN.t[h, :, q0:q0 + QG * TQ], reads=[QN], writes=[qn_])
                        P.dma(SP, qr_[:], QR.t[h, :, q0:q0 + QG * TQ], reads=[QR], writes=[qr_])
                        nblk = NCORE * nkb
                        bi = 0
                        pend = []
                        for r in range(NCORE):
                            k0 = r * NT + soff
                            kn_, kr_, v_ = kn[sc % 3], kr[sc % 3], vv[sc % 3]
                            sc += 1
                            P.dma(SP, kn_[:, 0:slab], KNa.t[h, :, k0:k0 + slab], reads=[KNa], writes=[kn_])
                            P.dma(SP, kr_[:, 0:slab], KRa.t[:, k0:k0 + slab], reads=[KRa], writes=[kr_])
                            P.dma(ACT, v_[:, 0:nkb, :], Va.t[h, r, :, soff // 128:soff // 128 + nkb, :], reads=[Va], writes=[v_])
                            for kb in range(nkb):
                                ksl = slice(kb * 128, (kb + 1) * 128)
                                f, l = (bi == 0), (bi == nblk - 1)
                                for j in range(QG):
                                    ps = ps_s[cnt % 4]
                                    pt_ = pt[cnt % 4]
                                    cnt += 1
                                    qsl = slice(j * TQ, (j + 1) * TQ)
                                    _mm_group(P, ps, ps[:, 0:TQ], [(kn_[:, ksl], qn_[:, qsl]), (kr_[:, ksl], qr_[:, qsl])],
                                              [kn_, kr_, qn_, qr_])
                                    P.op(ACT, lambda e, ps=ps, pt_=pt_: e.activation(out=pt_[:], in_=ps[:, 0:TQ], func=AF.Exp, scale=scale),
                                         reads=[ps], writes=[pt_])
                                    po, pr = ps_o[j], ps_r[j]

                                    def acc(e, v_=v_, kb=kb, pt_=pt_, po=po, pr=pr, f=f, l=l):
                                        e.matmul(po[:, 0:TQ], v_[:, kb, :], pt_[:], start=f, stop=l)
                                        return e.matmul(pr[:, 0:TQ], ones[:], pt_[:], start=f, stop=l)
                                    pend.append((acc, [v_, pt_, ones], [po, pr]))
                                    if len(pend) > 1:
                                        a_, r_, w_ = pend.pop(0)
                                        P.op(PE, a_, reads=r_, writes=w_)
                                bi += 1
                        while pend:
                            a_, r_, w_ = pend.pop(0)
                            P.op(PE, a_, reads=r_, writes=w_)
                        for j in range(QG):
                            po, pr = ps_o[j], ps_r[j]
                            ao_ = ao[ac % 2]
                            ac += 1
                            P.op(DVE, lambda e, pr=pr: e.reciprocal(out=rinv[:], in_=pr[:, 0:TQ]), reads=[pr], writes=[rinv])
                            P.op(DVE, lambda e, po=po, ao_=ao_: e.tensor_tensor(out=ao_[:], in0=po[:, 0:TQ], in1=rinv[:], op=ALU.mult),
                                 reads=[po, rinv], writes=[ao_])
                            P.dma(POOL, aoT.t[h * 128:(h + 1) * 128, q0 + j * TQ:q0 + (j + 1) * TQ], ao_[:], reads=[ao_], writes=[aoT])
                soff += slab
            P.flush()
        P.pes = None
        with ExitStack() as pes:
            P.pes = pes
            Bf = PostBufs(P, TT, 16)
            g_ts = [P.sb("g_t%d" % i, [128, 16], F32) for i in range(3)]
            for i in range(3):
                P.dma(SP, g_ts[i][:], gv.t[:, i, :], writes=[g_ts[i]])
            for ti in range(NT // TT):
                t0 = ti * TT
                post_block(P, Bf, 16, aoT, Wo_b, x2T, g_ts[0], g_ts[1], g_ts[2], Wfi_b, Wfo_b, t0)
                P.dma(POOL, yT.t[:, t0:t0 + TT].rearrange("(c p) t -> p c t", p=128), Bf.xs[:], reads=[Bf.xs], writes=[yT])
            P.flush()
        P.pes = None
    return nc


def run_model(seq_lens, xs, norm_mix_pre, norm_mix_post, norm_ffn_pre, norm_ffn_post,
              gdn_w_in, gdn_conv_w, gdn_a_log, gdn_dt_bias, gdn_norm_w, gdn_w_out,
              mla_w_a, mla_q_a_norm, mla_w_q_b, mla_kv_a_norm, mla_w_kv_b, mla_w_o,
              ffn_w_in, ffn_w_out):
    cores = list(range(NCORE))
    xT_all = np.ascontiguousarray(xs.T)
    og = run_A(seq_lens, xT_all, norm_mix_pre, gdn_w_in, gdn_conv_w, gdn_a_log, gdn_dt_bias, gdn_norm_w)
    slabs = [L // NCORE for L in seq_lens]
    NT = sum(slabs)
    WB = prep_B_weights(norm_mix_post, norm_ffn_pre, norm_ffn_post, norm_mix_pre, gdn_w_out, ffn_w_in, ffn_w_out,
                        mla_w_a, mla_q_a_norm, mla_w_q_b, mla_kv_a_norm, mla_w_kv_b)
    toks = [_core_tokens(seq_lens, c) for c in cores]
    in_maps = []
    for c in cores:
        idx, pos = toks[c]
        C2, S2 = _rope_tabs(pos)
        m = dict(WB)
        m.update({"xT": np.ascontiguousarray(xs[idx].T), "ogT": np.ascontiguousarray(og[idx].T), "C2": C2, "S2": S2})
        in_maps.append(m)
    ncB = build_B(NT)
    resB = _launch(ncB, in_maps, "B").results
    del in_maps
    KNa = np.ascontiguousarray(np.concatenate([np.asarray(resB[c]["KN"]) for c in cores], axis=2))
    KRa = np.ascontiguousarray(np.concatenate([np.asarray(resB[c]["KR"]) for c in cores], axis=1))
    Va = np.ascontiguousarray(np.stack([np.asarray(resB[c]["V"]).reshape(NT // 128, 128, 16, 128).transpose(2, 1, 0, 3)
                                        for c in cores], axis=1))
    WC = {
        "Wo": lay_w(mla_w_o[0]), "Wfi": lay_w(ffn_w_in[1]), "Wfo": lay_w(ffn_w_out[1]),
        "gv": np.ascontiguousarray(np.stack([lay_g(norm_mix_post[1]), lay_g(norm_ffn_pre[1]), lay_g(norm_ffn_post[1])], 1)),
        "KNa": KNa, "KRa": KRa, "Va": Va,
    }
    in_maps = []
    for c in cores:
        m = dict(WC)
        m.update({"QN": np.asarray(resB[c]["QN"]), "QR": np.asarray(resB[c]["QR"]), "x2T": np.asarray(resB[c]["x2T"])})
        in_maps.append(m)
    ncC = build_C(slabs)
    resC = _launch(ncC, in_maps, "C").results
    y = np.empty((sum(seq_lens), D_MODEL), np.float32)
    for c in cores:
        y[toks[c][0]] = np.asarray(resC[c]["yT"]).T
    return y


def kernel(x_prompt, x_sample, norm_mix_pre, norm_mix_post, norm_ffn_pre, norm_ffn_post,
           gdn_w_in, gdn_conv_w, gdn_a_log, gdn_dt_bias, gdn_norm_w, gdn_w_out,
           mla_w_a, mla_q_a_norm, mla_w_q_b, mla_kv_a_norm, mla_w_kv_b, mla_w_o,
           ffn_w_in, ffn_w_out):
    a = [np.asarray(v, dtype=np.float32) for v in (
        x_prompt, x_sample, norm_mix_pre, norm_mix_post, norm_ffn_pre, norm_ffn_post,
        gdn_w_in, gdn_conv_w, gdn_a_log, gdn_dt_bias, gdn_norm_w, gdn_w_out,
        mla_w_a, mla_q_a_norm, mla_w_q_b, mla_kv_a_norm, mla_w_kv_b, mla_w_o, ffn_w_in, ffn_w_out)]
    xp, xsm = a[0], a[1]
    Bp, Lp, _ = xp.shape
    Bs, Ls, _ = xsm.shape
    seq_lens = [Lp] * Bp + [Ls] * Bs
    xs = np.concatenate([xp.reshape(Bp * Lp, D_MODEL), xsm.reshape(Bs * Ls, D_MODEL)], axis=0)
    y = run_model(seq_lens, xs, *a[2:])
    yp = y[:Bp * Lp].reshape(Bp, Lp, D_MODEL)
    ysm = y[Bp * Lp:].reshape(Bs, Ls, D_MODEL)
    return (np.ascontiguousarray(yp), np.ascontiguousarray(ysm))
```
